# Optimizing a Trainium2 kernel written in Bass

```python
import jax, jax.numpy as jnp
from jax import lax
import numpy as np

D_MODEL = 2048
BATCH = 4
SEQ = 2048
DEPTH = 1
DEC_BATCH = 128
DEC_SEQ = 8
PAST_LEN = 16384
PAGE_SIZE = 128

MIX_WIDTH = D_MODEL
RW_WIDTH = MIX_WIDTH // 2
RW_HEAD_DIM = 64
RW_HEADS = RW_WIDTH // RW_HEAD_DIM
RW_DECAY_LORA = max(32, int(round(1.8 * RW_WIDTH ** 0.5 / 32)) * 32)
RW_A_LORA = max(32, int(round(1.8 * RW_WIDTH ** 0.5 / 32)) * 32)
RW_GATE_LORA = max(32, int(round(0.6 * RW_WIDTH ** 0.8 / 32)) * 32)
RW_PROJ = 3 * RW_WIDTH + RW_DECAY_LORA + RW_A_LORA + RW_GATE_LORA
RW_GN_EPS = 64e-5
GLA_WIDTH = MIX_WIDTH - RW_WIDTH
GLA_HEADS = 4
GLA_KEY_WIDTH = GLA_WIDTH // 2
GLA_DK = GLA_KEY_WIDTH // GLA_HEADS
GLA_DV = GLA_WIDTH // GLA_HEADS
GLA_GATE_RANK = 16
GLA_GATE_NORM = 16.0
GLA_CHUNK = 64
GLA_PROJ = 2 * GLA_KEY_WIDTH + GLA_WIDTH + GLA_GATE_RANK + GLA_WIDTH
IN_WIDTH = RW_PROJ + GLA_PROJ
D_FF = 4 * D_MODEL
NORM_EPS = 1e-6
HEAD_NORM_EPS = 1e-5

kernel_name = 'hybrid_rwkv7_gla_decoder_step'


def rms_norm(x, g, eps=NORM_EPS):
    xf = x.astype(jnp.float32)
    y = xf * lax.rsqrt(jnp.mean(xf * xf, axis=-1, keepdims=True) + eps)
    return (y * g.astype(jnp.float32)).astype(x.dtype)


def _split(t, sizes):
    offsets = np.cumsum(sizes)[:-1].tolist()
    return jnp.split(t, offsets, axis=-1)


def rwkv7_mixer(p, shift_prev, s0, mu, w0, w2, a0, a2, g2, k_k, k_a, r_k, ln_w, ln_b):
    b, l, _ = p.shape
    prev = jnp.concatenate([shift_prev[:, None, :].astype(p.dtype), p[:, :-1]], axis=1)
    m = p + (prev - p) * mu
    r, k, v, xw, xa, xg = _split(m, [RW_WIDTH, RW_WIDTH, RW_WIDTH, RW_DECAY_LORA, RW_A_LORA, RW_GATE_LORA])
    w_log = -jax.nn.softplus(-(w0 + jnp.tanh(xw) @ w2)) - 0.5
    decay = jnp.exp(-jnp.exp(w_log))
    a = jax.nn.sigmoid(a0 + xa @ a2)
    g = jax.nn.sigmoid(xg) @ g2
    hd = lambda t: t.reshape(b, l, RW_HEADS, RW_HEAD_DIM)
    kkf = hd(k * k_k).astype(jnp.float32)
    kk = (kkf / jnp.maximum(jnp.sqrt(jnp.sum(kkf * kkf, axis=-1, keepdims=True)), 1e-12)).astype(p.dtype)
    k = k * (1 + (a - 1) * k_a)
    r, k, v, decay, a = hd(r), hd(k), hd(v), hd(decay), hd(a)

    def step(S, inp):
        r_t, w_t, k_t, v_t, kk_t, a_t = inp
        sa = jnp.einsum('bhij,bhj->bhi', S, -kk_t)
        S = (S * w_t[:, :, None, :] + sa[..., None] * (kk_t * a_t)[:, :, None, :]
             + v_t[..., None] * k_t[:, :, None, :]).astype(s0.dtype)
        y = jnp.einsum('bhij,bhj->bhi', S, r_t)
        return S, y

    xs = tuple(jnp.swapaxes(t, 0, 1) for t in (r, decay, k, v, kk, a))
    s_fin, ys = lax.scan(step, s0, xs)
    y = jnp.swapaxes(ys, 0, 1).astype(jnp.float32)
    mean = jnp.mean(y, axis=-1, keepdims=True)
    var = jnp.mean(jnp.square(y - mean), axis=-1, keepdims=True)
    yn = ((y - mean) * lax.rsqrt(var + RW_GN_EPS)).reshape(b, l, RW_WIDTH) * ln_w + ln_b
    bonus = jnp.sum(r * k * r_k, axis=-1, keepdims=True) * v
    out = (yn.astype(p.dtype) + bonus.reshape(b, l, RW_WIDTH)) * g
    return out, p[:, -1], s_fin


def gla_mixer(p, s0, gw2, gb, norm_w):
    b, l, _ = p.shape
    f32 = jnp.float32
    q, k, v, xgate, gout = _split(p, [GLA_KEY_WIDTH, GLA_KEY_WIDTH, GLA_WIDTH, GLA_GATE_RANK, GLA_WIDTH])
    gk = jax.nn.log_sigmoid((xgate @ gw2 + gb).astype(f32)) / GLA_GATE_NORM
    q = q.astype(f32) * GLA_DK ** -0.5
    c = min(GLA_CHUNK, l)
    n = -(-l // c)
    pad = n * c - l

    def chunks(t, d):
        t = jnp.pad(t.astype(f32), ((0, 0), (0, pad), (0, 0)))
        return t.reshape(b, n, c, GLA_HEADS, d)

    q, k, gk, v = chunks(q, GLA_DK), chunks(k, GLA_DK), chunks(gk, GLA_DK), chunks(v, GLA_DV)
    G = jnp.cumsum(gk, axis=2)
    q_dec = q * jnp.exp(G)
    k_inv = k * jnp.exp(-G)
    causal = jnp.tril(jnp.ones((c, c), dtype=bool))
    A = jnp.where(causal, jnp.einsum('bnchd,bnshd->bnhcs', q_dec, k_inv), 0.0)
    o_intra = jnp.einsum('bnhcs,bnshe->bnche', A, v)
    g_last = G[:, :, -1]
    k_end = k * jnp.exp(g_last[:, :, None] - G)

    def step(S, inp):
        qd, ke, vc, gl = inp
        o = jnp.einsum('bchd,bhde->bche', qd, S)
        S = S * jnp.exp(gl)[..., None] + jnp.einsum('bchd,bche->bhde', ke, vc)
        return S, o

    sw = lambda t: jnp.swapaxes(t, 0, 1)
    s_fin, o_inter = lax.scan(step, s0.astype(f32), (sw(q_dec), sw(k_end), sw(v), sw(g_last)))
    o = (o_intra + sw(o_inter)).reshape(b, n * c, GLA_HEADS, GLA_DV)[:, :l]
    on = o * lax.rsqrt(jnp.mean(o * o, axis=-1, keepdims=True) + HEAD_NORM_EPS) * norm_w.astype(f32)
    on = on * jax.nn.silu(gout.astype(f32)).reshape(b, l, GLA_HEADS, GLA_DV)
    return on.reshape(b, l, GLA_WIDTH).astype(p.dtype), s_fin.astype(s0.dtype)


def hybrid_layer(x, shift_prev, s_rw, s_gla, norm1_g, w_in, rw_mu, rw_w0, rw_w2, rw_a0, rw_a2,
                 rw_g2, rw_k_k, rw_k_a, rw_r_k, rw_ln_w, rw_ln_b, gla_gw2, gla_gb, gla_norm_w,
                 w_out, norm2_g, w_up, w_down):
    h = rms_norm(x, norm1_g)
    proj = h @ w_in
    p_rw, p_gla = proj[..., :RW_PROJ], proj[..., RW_PROJ:]
    o_rw, shift_new, s_rw_new = rwkv7_mixer(p_rw, shift_prev, s_rw, rw_mu, rw_w0, rw_w2, rw_a0, rw_a2,
                                            rw_g2, rw_k_k, rw_k_a, rw_r_k, rw_ln_w, rw_ln_b)
    o_gla, s_gla_new = gla_mixer(p_gla, s_gla, gla_gw2, gla_gb, gla_norm_w)
    x = x + jnp.concatenate([o_rw, o_gla], axis=-1) @ w_out
    u = jax.nn.relu(rms_norm(x, norm2_g) @ w_up)
    x = x + (u * u) @ w_down
    return x, shift_new, s_rw_new, s_gla_new


def trunk(x, shift, s_rw, s_gla, layer_weights, norm_f_g):
    sh_out, rw_out, gla_out = [], [], []
    for layer in range(DEPTH):
        lw = [w[layer] for w in layer_weights]
        x, sh, srw, sg = hybrid_layer(x, shift[layer], s_rw[layer], s_gla[layer], *lw)
        sh_out.append(sh)
        rw_out.append(srw)
        gla_out.append(sg)
    y = rms_norm(x, norm_f_g)
    return y, jnp.stack(sh_out), jnp.stack(rw_out), jnp.stack(gla_out)


def setup_inputs(seed: int = 0) -> dict:
    key = jax.random.key(seed)
    ks = jax.random.split(key, 32)
    L = DEPTH
    nrm = lambda k, shape, s: jax.random.normal(k, shape, jnp.float32) * s
    return {
        'x_prompt': nrm(ks[0], (BATCH, SEQ, D_MODEL), 1.0),
        'x_sample': nrm(ks[1], (DEC_BATCH, DEC_SEQ, D_MODEL), 1.0),
        'state_rwkv_shift': nrm(ks[2], (L, DEC_BATCH, RW_PROJ), 1.0),
        'state_rwkv_wkv': nrm(ks[3], (L, DEC_BATCH, RW_HEADS, RW_HEAD_DIM, RW_HEAD_DIM), 0.3),
        'state_gla': nrm(ks[4], (L, DEC_BATCH, GLA_HEADS, GLA_DK, GLA_DV), 0.3),
        'norm1_g': 1.0 + nrm(ks[5], (L, D_MODEL), 0.02),
        'w_in': nrm(ks[6], (L, D_MODEL, IN_WIDTH), D_MODEL ** -0.5),
        'rw_mu': jax.random.uniform(ks[7], (L, RW_PROJ), jnp.float32),
        'rw_w0': jax.random.uniform(ks[8], (L, RW_WIDTH), jnp.float32, minval=-6.0, maxval=-1.0),
        'rw_w2': nrm(ks[9], (L, RW_DECAY_LORA, RW_WIDTH), 0.5 * RW_DECAY_LORA ** -0.5),
        'rw_a0': nrm(ks[10], (L, RW_WIDTH), 0.1),
        'rw_a2': nrm(ks[11], (L, RW_A_LORA, RW_WIDTH), 0.5 * RW_A_LORA ** -0.5),
        'rw_g2': nrm(ks[12], (L, RW_GATE_LORA, RW_WIDTH), RW_GATE_LORA ** -0.5),
        'rw_k_k': 0.85 + nrm(ks[13], (L, RW_WIDTH), 0.02),
        'rw_k_a': 1.0 + nrm(ks[14], (L, RW_WIDTH), 0.02),
        'rw_r_k': nrm(ks[15], (L, RW_HEADS, RW_HEAD_DIM), 0.1),
        'rw_ln_w': 1.0 + nrm(ks[16], (L, RW_WIDTH), 0.02),
        'rw_ln_b': nrm(ks[17], (L, RW_WIDTH), 0.02),
        'gla_gw2': nrm(ks[18], (L, GLA_GATE_RANK, GLA_KEY_WIDTH), GLA_GATE_RANK ** -0.5),
        'gla_gb': nrm(ks[19], (L, GLA_KEY_WIDTH), 0.1),
        'gla_norm_w': 1.0 + nrm(ks[20], (L, GLA_DV), 0.02),
        'w_out': nrm(ks[21], (L, MIX_WIDTH, D_MODEL), MIX_WIDTH ** -0.5),
        'norm2_g': 1.0 + nrm(ks[22], (L, D_MODEL), 0.02),
        'w_up': nrm(ks[23], (L, D_MODEL, D_FF), D_MODEL ** -0.5),
        'w_down': nrm(ks[24], (L, D_FF, D_MODEL), D_FF ** -0.5),
        'norm_f_g': 1.0 + nrm(ks[25], (D_MODEL,), 0.02),
    }


def reference(x_prompt, x_sample, state_rwkv_shift, state_rwkv_wkv, state_gla, norm1_g, w_in, rw_mu,
              rw_w0, rw_w2, rw_a0, rw_a2, rw_g2, rw_k_k, rw_k_a, rw_r_k, rw_ln_w, rw_ln_b, gla_gw2,
              gla_gb, gla_norm_w, w_out, norm2_g, w_up, w_down, norm_f_g):
    layer_weights = (norm1_g, w_in, rw_mu, rw_w0, rw_w2, rw_a0, rw_a2, rw_g2, rw_k_k, rw_k_a, rw_r_k,
                     rw_ln_w, rw_ln_b, gla_gw2, gla_gb, gla_norm_w, w_out, norm2_g, w_up, w_down)
    dt = x_prompt.dtype
    shift0 = jnp.zeros((DEPTH, BATCH, RW_PROJ), dt)
    wkv0 = jnp.zeros((DEPTH, BATCH, RW_HEADS, RW_HEAD_DIM, RW_HEAD_DIM), dt)
    gla0 = jnp.zeros((DEPTH, BATCH, GLA_HEADS, GLA_DK, GLA_DV), dt)
    y_prompt, sh_p, wkv_p, gla_p = trunk(x_prompt, shift0, wkv0, gla0, layer_weights, norm_f_g)
    y_sample, sh_s, wkv_s, gla_s = trunk(x_sample, state_rwkv_shift, state_rwkv_wkv, state_gla,
                                         layer_weights, norm_f_g)
    return (y_prompt, y_sample, sh_p, wkv_p, gla_p, sh_s, wkv_s, gla_s)
```

```python
from contextlib import ExitStack

import numpy as np
import concourse.bass as bass
import concourse.mybir as mybir
from concourse.bass_utils import run_bass_kernel_spmd

F32 = mybir.dt.float32
BF16 = mybir.dt.bfloat16
ALU = mybir.AluOpType
AF = mybir.ActivationFunctionType

N_CORES = 8
D = 2048
KB = D // 128
SEQ = 2048
HALF = SEQ // 2
PRE = HALF
DEC_B, DEC_T = 128, 8
SMP = (DEC_B // N_CORES) * DEC_T
RW_W, RW_HD, RW_H = 1024, 64, 16
RW_PROJ = 3 * RW_W + 64 + 64 + 160
NORM_EPS = 1e-6

TOK = PRE + HALF + SMP
TT = 512


AX = mybir.AxisListType
NS = SMP // DEC_T
NTM = HALF + SMP
QT = 256
C0 = 0.6065306597126334
GN_EPS = 64e-5
HN_EPS = 1e-5
NRW = 28
NGL = 25


class Tracker:
    CE = ("tensor", "scalar", "vector")

    def __init__(self, nc, es):
        self.nc, self.ops, self.lw, self.rs = nc, [], {}, {}
        self.sem = {e: es.enter_context(nc.semaphore("c_" + e)) for e in self.CE}
        self.pool = {"sync": [es.enter_context(nc.semaphore(f"ds{i}")) for i in range(16)],
                     "gpsimd": [es.enter_context(nc.semaphore(f"dg{i}")) for i in range(12)]}
        self.bar = {}
        self.seen_after_bar = set()

    def barrier(self):
        last = {}
        for i, o in enumerate(self.ops):
            last[o["eng"]] = i
        self.bar = dict(last)
        self.bar_dma = []
        for q in ("sync", "gpsimd"):
            ids = [i for i, o in enumerate(self.ops) if o["eng"] == q and o["make"] is not None]
            self.bar_dma += ids[-len(self.pool[q]):]
        self.seen_after_bar = set()

    def add(self, eng, make, r=(), w=()):
        i = len(self.ops)
        raw, oth = set(), set()
        for k in r:
            p = self.lw.get(k)
            if p is not None:
                raw.add(p)
        for k in w:
            p = self.lw.get(k)
            if p is not None:
                oth.add(p)
            for q in self.rs.get(k, {}).values():
                oth.add(q)
        deps = set()
        for p in raw | oth:
            pe = self.ops[p]["eng"]
            if pe == eng:
                if eng == "tensor":
                    continue
                if eng in self.CE and p not in raw:
                    continue
            deps.add(p)
        if self.bar and eng not in self.seen_after_bar:
            self.seen_after_bar.add(eng)
            deps |= set(v for e, v in self.bar.items() if e != eng or eng not in self.CE)
            deps |= set(self.bar_dma)
        self.ops.append(dict(eng=eng, make=make, deps=deps, sig=False))
        for k in r:
            d = self.rs.setdefault(k, {})
            d[eng if eng in self.CE else ("dma", i)] = i
        for k in w:
            self.lw[k] = i
            self.rs[k] = {}
        return i

    def emit(self, out_ids):
        ops = self.ops
        fence = self.add("sync", None)
        ops[fence]["deps"] = set(out_ids)
        for o in ops:
            for p in o["deps"]:
                ops[p]["sig"] = True
        cnt = {e: 0 for e in self.CE}
        ndma = {q: 0 for q in self.pool}
        pcnt = {q: [0] * len(self.pool[q]) for q in self.pool}
        for o in ops:
            e = o["eng"]
            o["sv"] = None
            if o["make"] is None:
                continue
            if e in self.CE:
                if o["sig"]:
                    cnt[e] += 1
                    o["sv"] = (self.sem[e], cnt[e], 1)
            else:
                j = ndma[e] % len(self.pool[e])
                ndma[e] += 1
                pcnt[e][j] += 16
                o["sv"] = (self.pool[e][j], pcnt[e][j], 16)
        streams = {}
        for i, o in enumerate(ops):
            streams.setdefault(o["eng"], []).append(i)
        waited = {}
        self.nwaits = 0
        with self.nc.Block() as block:
            for e, ids in streams.items():
                def sec(eng, ids=ids, e=e):
                    for i in ids:
                        o = ops[i]
                        need = {}
                        for p in o["deps"]:
                            sem, val, _ = ops[p]["sv"]
                            k = id(sem)
                            if k not in need or need[k][1] < val:
                                need[k] = (sem, val)
                        for k, (sem, val) in need.items():
                            if waited.get((e, k), 0) < val:
                                eng.wait_ge(sem, val)
                                waited[(e, k)] = val
                                self.nwaits += 1
                        if o["make"] is not None:
                            ins = o["make"](eng)
                            if o["sv"] is not None:
                                ins.then_inc(o["sv"][0], o["sv"][2])
                getattr(block, e)(sec)


def host_consts():
    r = np.arange(128)
    h, s = r // 64, r % 64
    same_h = h[:, None] == h[None, :]
    sU = same_h & (s[:, None] < s[None, :])
    iU = same_h & (s[:, None] <= s[None, :])
    sL = same_h & (s[:, None] > s[None, :])
    seg = (s[:, None] // 8) == (s[None, :] // 8)
    f = lambda m: m.astype(np.float32)
    tabs = {}
    tabs["ident"] = np.eye(128, dtype=np.float32)
    tabs["bones"] = f(same_h)
    tabs["UU_p"] = np.concatenate([f(sU), f(iU)], 1)
    tabs["UU_s"] = np.concatenate([f(sU & seg), f(iU & seg)], 1)
    tabs["sL_p"] = f(sL)
    tabs["sL_s"] = f(sL & seg)
    t = np.arange(QT)
    tabs["scan_p"] = np.tile(f(t % 64 != 0)[None, :], (128, 1))
    tabs["scan_s"] = np.tile(f(t % 8 != 0)[None, :], (128, 1))
    n = np.arange(8)
    segF = f((s[None, None, :] // 8) == n[None, :, None])
    tabs["segF"] = np.tile(segF.reshape(1, 8 * 128), (128, 1))
    tabs["segT"] = f((s[:, None] // 8) == n[None, :])
    s6 = np.arange(128) % 64
    c6 = np.arange(64)
    gi = f(s6[:, None] <= c6[None, :])
    gseg = f((s6[:, None] // 8) == (c6[None, :] // 8))
    tabs["G_p"] = gi
    tabs["G_s"] = gi * gseg
    tabs["gsegF"] = np.tile(f((c6[None, None, :] // 8) == n[None, :, None]).reshape(1, 8 * 64), (128, 1))
    offs, cols, o = {}, [], 0
    for k, v in tabs.items():
        offs[k] = (o, v.shape[1])
        cols.append(v)
        o += v.shape[1]
    return np.ascontiguousarray(np.concatenate(cols, 1)), offs


CST, CST_OFF = host_consts()
NCST = CST.shape[1]


def build_nc():
    nc = bass.Bass("TRN2", target_bir_lowering=False)
    di = lambda n, sh: nc.dram_tensor(n, sh, F32, kind="ExternalInput").ap()
    do = lambda n, sh: nc.dram_tensor(n, sh, F32, kind="ExternalOutput").ap()
    xP, xM = di("xP", [128, KB, PRE]), di("xM", [128, KB, NTM])
    cst = di("cst", [128, NCST])
    gvec = di("gvec", [128, 3, KB])
    w_rw, mu_rw, shiftT = di("w_rw", [NRW, 128, KB, 128]), di("mu_rw", [128, NRW]), di("shiftT", [128, NRW, NS])
    rwv = di("rwv", [128, 7, 8])
    lora2 = di("lora2", [128, 4, 1024])
    rwH0 = di("rwH0", [8, 128, NS, 64])
    w_gl, glv, gw2 = di("w_gl", [NGL, 128, KB, 128]), di("glv", [128, 6]), di("gw2", [128, 512])
    glS0 = di("glS0", [4, 128, NS, 256])
    w_o, w_u, w_d = di("w_o", [16, 128, KB, 128]), di("w_u", [64, 128, KB, 128]), di("w_d", [4, 16, 128, KB, 128])
    yT = do("yT", [128, KB, NTM])
    o_shift = do("o_shift", [128, NRW, 1 + NS])
    o_wkvp, o_wkvs = do("o_wkvp", [8, 128, 64]), do("o_wkvs", [8, 128, NS, 64])
    o_glap, o_glas = do("o_glap", [4, 128, 256]), do("o_glas", [4, 128, NS, 256])

    with ExitStack() as es:
        AR = es.enter_context(nc.sbuf_tensor("arena", [128, 53000], F32))
        PS = [es.enter_context(nc.psum_tensor(f"ps{i}", [128, 512], F32)) for i in range(8)]
        T = Tracker(nc, es)
        top = [0]

        def alloc(n):
            o = top[0]
            top[0] += n
            assert top[0] <= 53000, top[0]
            return o

        def V(o, n, rows=None):
            return AR[:, o:o + n] if rows is None else AR[rows[0]:rows[1], o:o + n]

        o_hT, o_oT = alloc(9216), alloc(9216)
        hT = V(o_hT, 9216).bitcast(BF16).rearrange("p (k t) -> p k t", k=KB)
        oT = V(o_oT, 9216).bitcast(BF16).rearrange("p (k t) -> p k t", k=KB)
        NW = 4
        o_w = alloc(1024 * NW)
        wst = [V(o_w + 1024 * i, 1024).bitcast(BF16).rearrange("p (k f) -> p k f", k=KB) for i in range(NW)]
        o_c = alloc(NCST)
        CT = {k: V(o_c + a, n) for k, (a, n) in CST_OFF.items()}
        gv = V(alloc(48), 48).rearrange("p (a k) -> p a k", a=3)
        mut = V(alloc(NRW), NRW)
        rv = V(alloc(56), 56).rearrange("p (a k) -> p a k", a=7)
        shin = V(alloc(NRW * NS), NRW * NS).rearrange("p (c n) -> p c n", c=NRW)
        shout = V(alloc(NRW * (1 + NS)), NRW * (1 + NS)).rearrange("p (c n) -> p c n", c=NRW)
        carry = V(alloc(NRW), NRW)
        glvt = V(alloc(6), 6)
        nw0, na0, omka, ngb = V(alloc(8), 8), V(alloc(8), 8), V(alloc(8), 8), V(alloc(4), 4)
        epsn, eps24, epsg, epsh, one1 = (V(alloc(1), 1) for _ in range(5))
        gw2b = V(alloc(256), 256).bitcast(BF16)
        o_l2b = alloc(2048)
        l2b = V(o_l2b, 2048).bitcast(BF16).rearrange("p (a f) -> p a f", a=4)
        wst += [V(o_l2b + 1024 * i, 1024).bitcast(BF16).rearrange("p (k f) -> p k f", k=KB) for i in range(2)]
        onesb = V(alloc(64), 64).bitcast(BF16)
        Hrw = V(alloc(1024), 1024).rearrange("p (h c) -> p h c", h=8)
        Sgl = V(alloc(1024), 1024).rearrange("p (g e) -> p g e", g=4)
        ov0 = top[0]

        st = {"eng": 0, "w": 0, "wr": 0}

        def cp(out, in_, r, w, scale=None):
            st["eng"] ^= 1
            if st["eng"]:
                if scale is None:
                    return T.add("scalar", lambda e: e.activation(out=out, in_=in_, func=AF.Copy), r, w)
                return T.add("scalar", lambda e: e.activation(out=out, in_=in_, func=AF.Copy, scale=scale), r, w)
            if scale is None:
                return T.add("vector", lambda e: e.tensor_copy(out, in_), r, w)
            return T.add("vector", lambda e: e.tensor_scalar(out, in_, scale, None, ALU.mult), r, w)

        def act(out, in_, func, r, w, bias=None, scale=1.0, accum=None):
            kw = dict(out=out, in_=in_, func=func, scale=scale)
            if bias is not None:
                kw["bias"] = bias
            if accum is not None:
                kw["accum_out"] = accum
            return T.add("scalar", lambda e: e.activation(**kw), r, w)

        def tt(out, a, b, op, r, w):
            return T.add("vector", lambda e: e.tensor_tensor(out, a, b, op), r, w)

        def ts(out, a, s1, s2, op0, op1, r, w):
            if s2 is None:
                return T.add("vector", lambda e: e.tensor_scalar(out, a, s1, None, op0), r, w)
            return T.add("vector", lambda e: e.tensor_scalar(out, a, s1, s2, op0, op1), r, w)

        def stt(out, a, sc, b, op0, op1, r, w):
            return T.add("vector", lambda e: e.scalar_tensor_tensor(out=out, in0=a, scalar=sc, in1=b, op0=op0, op1=op1), r, w)

        def mm(out, lhsT, rhs, start, stop, r, w):
            return T.add("tensor", lambda e: e.matmul(out, lhsT, rhs, start=start, stop=stop), r, w)

        def tr(out, in_, r, w):
            return T.add("tensor", lambda e: e.transpose(out, in_, identb), r + ["identb"], w)

        def ld(out, in_, w, r=()):
            return T.add("sync", lambda e: e.dma_start(out=out, in_=in_), r, w)

        def ldc(out, in_, w, r=()):
            return T.add("gpsimd", lambda e: e.dma_start(out=out, in_=in_), r, w)

        out_ids = []

        def store(out, in_, r):
            out_ids.append(T.add("sync", lambda e: e.dma_start(out=out, in_=in_), r, ()))

        ld(V(o_c, NCST), cst[:, :], ["cst"])
        ld(gv, gvec[:, :, :], ["gv"])
        ld(mut, mu_rw[:, :], ["mut"])
        ld(rv, rwv[:, :, :], ["rv"])
        ld(shin, shiftT[:, :, :], ["shin"])
        ld(glvt, glv[:, :], ["glv"])
        ldc(gw2b, gw2[:, :], ["gw2b"])
        ldc(l2b, lora2[:, :, :], ["l2b"])
        T.add("vector", lambda e: e.memset(onesb, 1.0), (), ["onesb"])
        for tl, val in ((epsn, NORM_EPS), (eps24, 1e-24), (epsg, GN_EPS), (epsh, HN_EPS), (one1, 1.0)):
            T.add("vector", lambda e, tl=tl, val=val: e.memset(tl, val), (), ["eps"])
        T.add("vector", lambda e: e.memset(carry, 0.0), (), ["carry"])
        T.add("vector", lambda e: e.memset(Hrw, 0.0), (), [("Hrw", h) for h in range(8)])
        T.add("vector", lambda e: e.memset(Sgl, 0.0), (), ["Sgl"])
        T.add("vector", lambda e: e.memset(shout, 0.0), (), ["shout"])
        ts(nw0, rv[:, 0, :], -1.0, None, ALU.mult, None, ["rv"], ["nw0"])
        ts(na0, rv[:, 1, :], -1.0, None, ALU.mult, None, ["rv"], ["na0"])
        ts(omka, rv[:, 3, :], -1.0, 1.0, ALU.mult, ALU.add, ["rv"], ["omka"])
        ts(ngb, glvt[:, 0:4], -1.0, None, ALU.mult, None, ["glv"], ["ngb"])

        lorT = V(alloc(2304), 2304).bitcast(BF16).rearrange("p (a t) -> p a t", a=4)
        xgT = V(alloc(576), 576).bitcast(BF16)
        identb = V(alloc(64), 64).bitcast(BF16)
        Hrwb = V(alloc(512), 512).bitcast(BF16).rearrange("p (h c) -> p h c", h=8)
        T.add("vector", lambda e: e.memset(Hrwb, 0.0), (), [("Hrwb", h) for h in range(8)])
        FTS = []
        for _fp in (0, 1):
            d = {"pb": V(alloc(260), 260)}
            for nm in ("db", "mr", "mk", "mv", "T0", "T1", "T3", "T4", "T5", "T6", "T7", "T8", "T9", "T10", "T11"):
                d[nm] = V(alloc(QT), QT)
            FTS.append(d)
        FT = FTS[0]
        MT, MTO = {}, {}
        mats0 = top[0]

        def mat(nm, n, dt=BF16):
            if dt == BF16:
                MTO[nm] = alloc(n // 2)
                MT[nm] = V(MTO[nm], n // 2).bitcast(BF16)
            else:
                MTO[nm] = alloc(n)
                MT[nm] = V(MTO[nm], n)
            return MT[nm]

        def psb(p):
            return p.bitcast(BF16)

        for nm, n in (("KR", 512), ("Bf", 256), ("Cf", 256), ("BBf", 256), ("KKf", 256), ("Vf", 256), ("YN", 256), ("YN_B", 256)):
            m = mat(nm, n)
            T.add("vector", lambda e, m=m: e.memset(m, 0.0), (), [nm])
        mat("sqt", 256)
        mat("sqt_B", 256)
        for nm, n in (("KZ", 512), ("BBt", 256), ("KKt", 256), ("Vt", 256), ("LA", 512), ("AK", 512),
                      ("Lt", 256), ("X", 256), ("PP", 512), ("QU", 512), ("Mst", 256), ("Rst", 256),
                      ("Hsb", 384), ("MnT", 1024)):
            mat(nm, n)
        for nm, n in (("Hs", 384), ("stat", 64), ("H0f", 1024), ("hf", 128), ("hf_B", 128), ("Hs_B", 384), ("stat_B", 64)):
            mat(nm, n, F32)
        for nm, n in (("KR", 512), ("Bf", 256), ("Cf", 256), ("BBf", 256), ("KKf", 256), ("Vf", 256)):
            m = mat(nm + "_B", n)
            T.add("vector", lambda e, m=m: e.memset(m, 0.0), (), [nm + "_B"])
        for nm, n in (("KZ", 512), ("BBt", 256), ("KKt", 256), ("Vt", 256), ("LA", 512), ("AK", 512),
                      ("Lt", 256), ("X", 256), ("PP", 512), ("QU", 512), ("Mst", 256), ("Rst", 256), ("Hsb", 384)):
            mat(nm + "_B", n)
        for nm, base in (("BBx", "KZ_B"), ("KKx", "LA_B"), ("Rsx", "Lt_B"), ("H0b", "QU_B")):
            MT[nm] = V(MTO[base], 512).bitcast(BF16)
        T.add("vector", lambda e: e.memset(MT["H0f"], 0.0), (), ["H0f"])
        T.add("vector", lambda e: e.tensor_copy(identb, CT["ident"]), ["cst"], ["identb"])
        mix_top = top[0]
        assert mix_top - mats0 >= 6656, (mix_top, mats0)

        def norm_tokens(src, ntok, gidx, dst, dkey, final_out=None, base=None):
            resident = isinstance(src, tuple)
            o = base
            if not resident:
                xsb = [AR[:, o - 4096 * i:o - 4096 * i + 4096].rearrange("p (k t) -> p k t", k=KB) for i in (0, 1)]
                o += 4096
            sq = AR[:, o:o + 2048].bitcast(BF16).rearrange("p (k t) -> p k t", k=KB)
            lnb = AR[:, o + 2048:o + 2304]
            rsd = AR[:, o + 2304:o + 2560]
            for t0 in range(0, ntok, 256):
                tw = min(256, ntok - t0)
                if resident:
                    xv = src[0][:, :, t0:t0 + tw]
                    xk = lambda k: [src[1][k]]
                else:
                    xi = (t0 // 256) % 2
                    xs = xsb[xi]
                    ld(xs[:, :, 0:tw], src[:, :, t0:t0 + tw], ["xs%d" % xi])
                    xv = xs[:, :, 0:tw]
                    xk = lambda k, xi=xi: ["xs%d" % xi]
                for k in range(KB):
                    act(sq[:, k, 0:tw], xv[:, k, :], AF.Square, xk(k), [("sq", k)])
                for k in range(KB):
                    mm(PS[0][:, 0:tw], onesb, sq[:, k, 0:tw], k == 0, k == KB - 1, ["onesb", ("sq", k)], ["ps0"])
                act(lnb[:, 0:tw], PS[0][:, 0:tw], AF.Ln, ["ps0", "eps"], ["lnb"], bias=epsn, scale=1.0 / D)
                act(rsd[:, 0:tw], lnb[:, 0:tw], AF.Exp, ["lnb"], ["rsd"], scale=-0.5)
                for k in range(KB):
                    if final_out is None:
                        stt(dst[:, k, t0:t0 + tw], xv[:, k, :], gv[:, gidx, k:k + 1], rsd[:, 0:tw],
                            ALU.mult, ALU.mult, xk(k) + ["gv", "rsd"], [dkey])
                    else:
                        stt(xv[:, k, :], xv[:, k, :], gv[:, gidx, k:k + 1], rsd[:, 0:tw],
                            ALU.mult, ALU.mult, xk(k) + ["gv", "rsd"], xk(k))
                if final_out is not None:
                    store(final_out[:, :, t0:t0 + tw], xv, [kk for k in range(KB) for kk in xk(k)])

        def wload(src):
            i = st["w"] % NW
            st["w"] += 1
            ldc(wst[i], src, [("w", i)])
            return i

        wst += [V(o_oT + 4608 + 1024 * i, 1024).bitcast(BF16).rearrange("p (k f) -> p k f", k=KB) for i in range(2)]
        RW_SLOTS = [0, 1, 2, 3, 6, 7]

        def wload_rw(src):
            i = RW_SLOTS[st["wr"] % len(RW_SLOTS)]
            st["wr"] += 1
            ldc(wst[i], src, [("w", i)])
            return i

        def proj(wi, tc0, ntok, ps):
            for k in range(KB):
                mm(ps[:, 0:ntok], wst[wi][:, k, :], hT[:, k, tc0:tc0 + ntok], k == 0, k == KB - 1,
                   [("w", wi), "hT"], [ps_key(ps)])

        def ps_key(ps):
            for i, p in enumerate(PS):
                if p is ps:
                    return f"ps{i}"
            raise KeyError

        pp = {"i": 0}

        def proj_ps():
            pp["i"] ^= 1
            return PS[pp["i"]]

        def shift(c, ps, grp, mdst, mkey, fp=0):
            tc0, ntok, kind, last_own = grp
            pb, db = FTS[fp]["pb"], FTS[fp]["db"]
            kpb, kdb = "pb#%d" % fp, "db#%d" % fp
            cp(pb[:, 0:1], carry[:, c:c + 1], ["carry"], [kpb])
            cp(pb[:, 1:1 + ntok], ps[:, 0:ntok], [ps_key(ps)], [kpb])
            if kind == "p":
                tt(db[:, 0:ntok], pb[:, 0:ntok], pb[:, 1:1 + ntok], ALU.subtract, [kpb], [kdb])
                cp(carry[:, c:c + 1], pb[:, ntok:ntok + 1], [kpb], ["carry"])
                if last_own:
                    cp(shout[:, c, 0:1], pb[:, ntok:ntok + 1], [kpb], ["shout"])
            else:
                cp(db[:, 0:ntok], pb[:, 0:ntok], [kpb], [kdb])
                cp(db[:, 0:ntok].rearrange("p (n t) -> p n t", t=DEC_T)[:, :, 0], shin[:, c, :], ["shin", kdb], [kdb])
                tt(db[:, 0:ntok], db[:, 0:ntok], pb[:, 1:1 + ntok], ALU.subtract, [kpb, kdb], [kdb])
                cp(shout[:, c, 1:1 + NS], pb[:, 1:1 + ntok].rearrange("p (n t) -> p n t", t=DEC_T)[:, :, DEC_T - 1],
                   [kpb], ["shout"])
            stt(mdst[:, 0:ntok], db[:, 0:ntok], mut[:, c:c + 1], pb[:, 1:1 + ntok], ALU.mult, ALU.add,
                [kdb, kpb, "mut"], [mkey])

        def groups(mode):
            if mode == "P":
                return [(t, QT, "p", False) for t in range(0, PRE, QT)]
            return [(t, QT, "p", t + QT == HALF) for t in range(0, HALF, QT)] + [(HALF, SMP, "s", False)]

        pmi = {"i": 0}
        LORK = [("lorT", c) for c in range(24, 28)]
        proj_done = set()
        sub_lock = {"busy": False}

        pinned = set()

        def pm(pin=False):
            while True:
                pmi["i"] = (pmi["i"] + 1) % 6
                if pmi["i"] not in pinned:
                    break
            if pin:
                pinned.add(pmi["i"])
            return PS[2 + pmi["i"]]

        def unpin(ps):
            for i in range(6):
                if PS[2 + i] is ps:
                    pinned.discard(i)

        def b3(ap, n, m):
            return ap.rearrange("p (n m) -> p n m", n=n)

        def bc_mid(ap2, n):
            return ap2.unsqueeze(1).to_broadcast([ap2.shape[0], n, ap2.shape[1]])

        def bc_last(ap2, m):
            return ap2.unsqueeze(2).to_broadcast([ap2.shape[0], ap2.shape[1], m])

        def lora_gen(lc, mode, fp):
            k0, k1 = "T0#%d" % fp, "T1#%d" % fp
            wi = wload(w_rw[lc])
            for grp in groups(mode):
                tc0, N, kind, _ = grp
                ps = proj_ps()
                proj(wi, tc0, N, ps)
                shift(lc, ps, grp, FTS[fp]["T0"], k0, fp)
                yield
                t0, t1 = FTS[fp]["T0"][:, 0:N], FTS[fp]["T1"][:, 0:N]
                dst = lorT[:, lc - 24, tc0:tc0 + N]
                if lc == 24:
                    act(t1, t0, AF.Exp, [k0], [k1], scale=2.0)
                    act(t1, t1, AF.Ln, [k1, "eps"], [k1], bias=one1)
                    act(t1, t1, AF.Exp, [k1], [k1], scale=-1.0)
                    ts(dst, t1, -2.0, 1.0, ALU.mult, ALU.add, [k1], [("lorT", lc)])
                elif lc == 25:
                    cp(dst, t0, [k0], [("lorT", lc)])
                else:
                    act(t1, t0, AF.Exp, [k0], [k1], scale=-1.0)
                    act(t1, t1, AF.Ln, [k1, "eps"], [k1], bias=one1)
                    act(dst, t1, AF.Exp, [k1], [("lorT", lc)], scale=-1.0)
                yield

        def lora_stage(mode):
            for pair in ((24, 25), (26, 27)):
                gens = [lora_gen(lc, mode, i) for i, lc in enumerate(pair)]
                while gens:
                    for gg in list(gens):
                        try:
                            next(gg)
                        except StopIteration:
                            gens.remove(gg)

        def rw_group(hp, grp, main, wk, wv, wr, fp, tag):
            fk = lambda n: n + "#%d" % fp
            FT = FTS[fp]
            tc0, N, kind, _ = grp
            L = 64 if kind == "p" else 8
            nseg = N // L
            F = {k: v[:, 0:N] for k, v in FT.items() if k != "pb"}
            hs = slice(hp, hp + 1)
            hcols = slice(hp * 128, (hp + 1) * 128)
            ps = proj_ps(); proj(wk, tc0, N, ps); shift(8 + hp, ps, grp, FT["mk"], fk("mk"), fp); yield
            ps = proj_ps(); proj(wv, tc0, N, ps); shift(16 + hp, ps, grp, FT["mv"], fk("mv"), fp); yield
            if main:
                ps = proj_ps(); proj(wr, tc0, N, ps); shift(hp, ps, grp, FT["mr"], fk("mr"), fp); yield
            proj_done.add(tag)
            scanm = CT["scan_p" if kind == "p" else "scan_s"][:, 0:N]
            p1 = pm(); k1 = ps_key(p1)
            mm(p1[:, 0:N], l2b[:, 0, hcols], lorT[:, 0, tc0:tc0 + N], True, True, ["l2b"] + LORK, [k1])
            act(F["T0"], p1[:, 0:N], AF.Exp, [k1, "nw0"], [fk("T0")], bias=nw0[:, hs], scale=-1.0)
            act(F["T0"], F["T0"], AF.Ln, [fk("T0"), "eps"], [fk("T0")], bias=one1)
            act(F["T0"], F["T0"], AF.Exp, [fk("T0")], [fk("T0")], scale=-1.0)
            T.add("vector", lambda e: e.tensor_tensor_scan(F["T1"], scanm, F["T0"], 0.0, ALU.mult, ALU.add),
                  ["cst", fk("T0")], [fk("T1")])
            tt(F["T0"], F["T1"], F["T0"], ALU.subtract, [fk("T0"), fk("T1")], [fk("T0")])
            act(F["T3"], F["T1"], AF.Exp, [fk("T1")], [fk("T3")], scale=-C0)
            act(F["T4"], F["T1"], AF.Exp, [fk("T1")], [fk("T4")], scale=C0)
            act(F["T0"], F["T0"], AF.Exp, [fk("T0")], [fk("T0")], scale=-C0)
            yield
            p2 = pm(); k2 = ps_key(p2)
            mm(p2[:, 0:N], l2b[:, 1, hcols], lorT[:, 1, tc0:tc0 + N], True, True, ["l2b"] + LORK, [k2])
            act(F["T1"], p2[:, 0:N], AF.Exp, [k2, "na0"], [fk("T1")], bias=na0[:, hs], scale=-1.0)
            act(F["T1"], F["T1"], AF.Ln, [fk("T1"), "eps"], [fk("T1")], bias=one1)
            act(F["T1"], F["T1"], AF.Exp, [fk("T1")], [fk("T1")], scale=-1.0)
            yield
            ts(F["T5"], F["mk"], rv[:, 2, hs], None, ALU.mult, None, [fk("mk"), "rv"], [fk("T5")])
            act(F["T6"], F["T5"], AF.Square, [fk("T5")], [fk("T6")])
            p3 = pm(); k3 = ps_key(p3)
            mm(p3[:, 0:N], CT["bones"], F["T6"], True, True, ["cst", fk("T6")], [k3])
            act(F["T6"], p3[:, 0:N], AF.Ln, [k3, "eps"], [fk("T6")], bias=eps24)
            act(F["T6"], F["T6"], AF.Exp, [fk("T6")], [fk("T6")], scale=-0.5)
            tt(F["T5"], F["T5"], F["T6"], ALU.mult, [fk("T5"), fk("T6")], [fk("T5")])
            ts(F["T6"], F["T1"], rv[:, 3, hs], omka[:, hs], ALU.mult, ALU.add, [fk("T1"), "rv", "omka"], [fk("T6")])
            tt(F["T6"], F["mk"], F["T6"], ALU.mult, [fk("mk"), fk("T6")], [fk("T6")])
            tt(F["T7"], F["T5"], F["T1"], ALU.mult, [fk("T5"), fk("T1")], [fk("T7")])
            yield
            if main:
                stt(F["T8"], F["mr"], rv[:, 4, hs], F["T6"], ALU.mult, ALU.mult, [fk("mr"), "rv", fk("T6")], [fk("T8")])
                p4 = pm(); k4 = ps_key(p4)
                mm(p4[:, 0:N], CT["bones"], F["T8"], True, True, ["cst", fk("T8")], [k4])
                tt(F["T8"], p4[:, 0:N], F["mv"], ALU.mult, [k4, fk("mv")], [fk("T8")])
                p5 = pm(); k5 = ps_key(p5)
                mm(p5[:, 0:N], l2b[:, 2, hcols], lorT[:, 2, tc0:tc0 + N], True, False, ["l2b"] + LORK, [k5])
                mm(p5[:, 0:N], l2b[:, 3, hcols], lorT[:, 3, tc0:tc0 + N], False, True, ["l2b"] + LORK, [k5])
                cp(F["T9"], p5[:, 0:N], [k5], [fk("T9")])
                tt(F["mr"], F["mr"], F["T3"], ALU.mult, [fk("mr"), fk("T3"), fk("T8")], [fk("mr")])
            yield
            tt(F["T7"], F["T7"], F["T4"], ALU.mult, [fk("T7"), fk("T4")], [fk("T7")])
            tt(F["T6"], F["T6"], F["T4"], ALU.mult, [fk("T6"), fk("T4"), fk("T8")], [fk("T6")])
            tt(F["T5"], F["T5"], F["T0"], ALU.mult, [fk("T5"), fk("T0")], [fk("T5")])
            wend = bc_last(b3(F["T3"], nseg, L)[:, :, L - 1], L)
            tt(b3(F["T10"], nseg, L), b3(F["T7"], nseg, L), wend, ALU.mult, [fk("T7"), fk("T3")], [fk("T10")])
            tt(b3(F["T11"], nseg, L), b3(F["T6"], nseg, L), wend, ALU.mult, [fk("T6"), fk("T3")], [fk("T11")])
            yield
            while sub_lock["busy"]:
                yield
            sub_lock["busy"] = True
            gens = [rw_sub(hp, grp, main, sub, F, ("", "_B")[sub], fk) for sub in range(N // 128)]
            in_tail, held = set(), True
            while gens:
                for gsub in list(gens):
                    try:
                        if next(gsub) == "tail":
                            in_tail.add(id(gsub))
                    except StopIteration:
                        gens.remove(gsub)
                if held and all(id(gsub) in in_tail for gsub in gens):
                    sub_lock["busy"] = False
                    held = False
                yield
            if main:
                stt(F["db"], F["db"], rv[:, 5, hs], F["T8"], ALU.mult, ALU.add, [fk("db"), "rv", fk("T8")], [fk("db")])
                stt(oT[:, hp, tc0:tc0 + N], F["db"], rv[:, 6, hs], F["T9"], ALU.add, ALU.mult,
                    [fk("db"), "rv", fk("T9")], [("oT", hp)])
                if grp[3]:
                    store(o_wkvp[hp, 0:64, :], Hrw[0:64, hp, 0:64], [("Hrw", hp)])
                    store(o_wkvp[hp, 64:128, :], Hrw[64:128, hp, 64:128], [("Hrw", hp)])

        def rw_sub(hp, grp, main, sub, F, sfx, fk):
            kn = lambda n: n + sfx
            tc0, N, kind, _ = grp
            c0 = sub * 128
            KR, KZ, LA, AK, QU = (b3(MT[kn(n)], 2, 256) for n in ("KR", "KZ", "LA", "AK", "QU"))
            Bf, Cf, BBf, KKf, Vf, BBt, KKt, Vt, Lt, X, Mst, Rst = (
                b3(MT[kn(n)], 2, 128) for n in ("Bf", "Cf", "BBf", "KKf", "Vf", "BBt", "KKt", "Vt", "Lt", "X", "Mst", "Rst"))
            PP = b3(MT[kn("PP")], 2, 256)
            stat = MT[kn("stat")]
            srcs = [("T5", KR, 0, kn("KR")), ("T7", Bf, 0, kn("Bf")), ("T6", Cf, 0, kn("Cf")), ("T10", BBf, 0, kn("BBf")),
                    ("T11", KKf, 0, kn("KKf")), ("mv", Vf, 0, kn("Vf"))]
            if main:
                srcs.append(("mr", KR, 128, kn("KR")))
            for nm, dst, co, dk in srcs:
                for h in (0, 1):
                    rows = slice(h * 64, h * 64 + 64)
                    cp(dst[rows, :, co + h * 64:co + h * 64 + 64],
                       F[nm][rows, c0:c0 + 128].rearrange("p (q s) -> p q s", q=2), [fk(nm)], [dk])
            yield
            for src, sk, dst, dk, co in ((KR, kn("KR"), KZ, kn("KZ"), 0), (BBf, kn("BBf"), BBt, kn("BBt"), 0),
                                         (KKf, kn("KKf"), KKt, kn("KKt"), 0), (Vf, kn("Vf"), Vt, kn("Vt"), 0)):
                p = pm(); k = ps_key(p)
                for q in (0, 1):
                    tr(psb(p)[:, q * 128:(q + 1) * 128], src[:, q, 0:128], [sk], [k])
                cp(dst[:, :, 0:128], b3(psb(p)[:, 0:256], 2, 128), [k], [dk])
            yield
            UU = CT["UU_p" if kind == "p" else "UU_s"]
            sL = CT["sL_p" if kind == "p" else "sL_s"]
            W = 256 if main else 128
            for lhs, lk, dst, dk in ((Bf, kn("Bf"), LA, kn("LA")), (Cf, kn("Cf"), AK, kn("AK"))):
                p = pm(); k = ps_key(p)
                for q in (0, 1):
                    mm(p[:, q * 256:q * 256 + W], lhs[:, q, :], KR[:, q, 0:W], True, True, [lk, kn("KR")], [k])
                tt(dst[:, :, 0:W], b3(p, 2, 256)[:, :, 0:W], bc_mid(UU[:, 0:W], 2), ALU.mult, [k, "cst"], [dk])
            p = pm(); k = ps_key(p)
            for q in (0, 1):
                mm(p[:, q * 128:(q + 1) * 128], KR[:, q, 0:128], Bf[:, q, :], True, True, [kn("KR"), kn("Bf")], [k])
            tt(Lt, b3(p[:, 0:256], 2, 128), bc_mid(sL, 2), ALU.mult, [k, "cst"], [kn("Lt")])
            yield
            p = pm(); k = ps_key(p)
            for q in (0, 1):
                mm(p[:, q * 256:q * 256 + 128], Lt[:, q, :], LA[:, q, 0:128], True, True, [kn("Lt"), kn("LA")], [k])
                mm(p[:, q * 256 + 128:q * 256 + 256], LA[:, q, 0:128], Lt[:, q, :], True, True, [kn("Lt"), kn("LA")], [k])
            act(PP, b3(p, 2, 256), AF.Copy, [k], [kn("PP")])
            tt(X, bc_mid(CT["ident"], 2), LA[:, :, 0:128], ALU.subtract, ["cst", kn("LA")], [kn("X")])
            for lvl in range(5):
                yield
                p = pm(); k = ps_key(p)
                for q in (0, 1):
                    mm(p[:, q * 128:(q + 1) * 128], PP[:, q, 128:256], X[:, q, :], True, True, [kn("PP"), kn("X")], [k])
                tt(X, X, b3(p[:, 0:256], 2, 128), ALU.add, [kn("X"), k], [kn("X")])
                if lvl < 4:
                    p = pm(); k = ps_key(p)
                    for q in (0, 1):
                        mm(p[:, q * 256:q * 256 + 128], PP[:, q, 128:256], PP[:, q, 0:128], True, True, [kn("PP")], [k])
                        mm(p[:, q * 256 + 128:q * 256 + 256], PP[:, q, 0:128], PP[:, q, 128:256], True, True, [kn("PP")], [k])
                    act(PP, b3(p, 2, 256), AF.Copy, [k], [kn("PP")])
            yield
            p = pm(); k = ps_key(p)
            for q in (0, 1):
                mm(p[:, q * 128:(q + 1) * 128], AK[:, q, 0:128], Vt[:, q, :], True, True, [kn("AK"), kn("Vt")], [k])
            cp(KZ[:, :, 128:256], b3(p[:, 0:256], 2, 128), [k], [kn("KZ")])
            p = pm(); k = ps_key(p)
            for q in (0, 1):
                mm(p[:, q * 256:(q + 1) * 256], X[:, q, :], KZ[:, q, :], True, True, [kn("X"), kn("KZ")], [k])
            cp(QU, b3(p, 2, 256), [k], [kn("QU")], scale=-1.0)
            yield
            if main:
                p = pm(); k = ps_key(p)
                for q in (0, 1):
                    mm(p[:, q * 128:(q + 1) * 128], QU[:, q, 0:128], LA[:, q, 128:256], True, True, [kn("QU"), kn("LA")], [k])
                tt(Rst, b3(p[:, 0:256], 2, 128), KR[:, :, 128:256], ALU.add, [k, kn("KR")], [kn("Rst")])
            pY = None
            yield
            if kind == "p":
                p = pm(); k = ps_key(p)
                for q in (0, 1):
                    mm(p[:, q * 128:(q + 1) * 128], QU[:, q, 0:128], BBt[:, q, :], True, True, [kn("QU"), kn("BBt")], [k])
                cp(Mst, b3(p[:, 0:256], 2, 128), [k], [kn("Mst")])
                Hs, Hsb = b3(MT[kn("Hs")], 3, 128), b3(MT[kn("Hsb")], 3, 128)
                st_f = [Hrw[:, hp, :], Hs[:, 1, :]]
                st_b = [Hrwb[:, hp, :], Hsb[:, 1, :]]
                kf = [("Hrw", hp), (kn("Hs"), 1)]
                kb = [("Hrwb", hp), (kn("Hsb"), 1)]
                if main:
                    pY = pm(pin=True); kY = ps_key(pY)
                for q in (0, 1):
                    if main:
                        yo = pY[:, q * 128:(q + 1) * 128]
                        mm(yo, LA[:, q, 128:256], QU[:, q, 128:256], True, False, [kn("LA"), kn("QU")], [kY])
                        mm(yo, AK[:, q, 128:256], Vt[:, q, :], False, False, [kn("AK"), kn("Vt")], [kY])
                        mm(yo, Rst[:, q, :], st_b[q], False, True, [kn("Rst"), kb[q]], [kY])
                    p = pm(); k = ps_key(p)
                    mm(p[:, 0:128], BBt[:, q, :], QU[:, q, 128:256], True, False, [kn("BBt"), kn("QU")], [k])
                    mm(p[:, 0:128], KKt[:, q, :], Vt[:, q, :], False, False, [kn("KKt"), kn("Vt")], [k])
                    mm(p[:, 0:128], Mst[:, q, :], st_b[q], False, True, [kn("Mst"), kb[q]], [k])
                    wc = F["T3"][:, c0 + q * 64 + 63:c0 + q * 64 + 64]
                    stt(st_b[1 - q], st_f[q], wc, p[:, 0:128], ALU.mult, ALU.add, [kf[q], fk("T3"), k], [kb[1 - q]])
                    stt(st_f[1 - q], st_f[q], wc, p[:, 0:128], ALU.mult, ALU.add, [kf[q], fk("T3"), k], [kf[1 - q]])
            else:
                BBx, KKx, Rsx, H0b, H0f = (b3(MT[n], 8, 128) for n in ("BBx", "KKx", "Rsx", "H0b", "H0f"))
                kBBx, kKKx, kRsx, kH0b = ["KZ_B", "BBt_B", "KKt_B"], ["LA_B", "AK_B"], ["Lt_B", "X_B", "PP_B"], ["QU_B", "Mst_B", "Rst_B"]
                segT, segF = CT["segT"], b3(CT["segF"], 8, 128)
                pY = pm(pin=True); kY = ps_key(pY)
                for q in (0, 1):
                    n0 = (sub * 2 + q) * 8
                    ld(H0f[0:64, :, 0:64], rwH0[hp, 0:64, n0:n0 + 8, :], ["H0f"])
                    ld(H0f[64:128, :, 64:128], rwH0[hp, 64:128, n0:n0 + 8, :], ["H0f"])
                    cp(H0b, H0f, ["H0f"], kH0b)
                    tt(BBx, bc_mid(BBt[:, q, :], 8), bc_last(segT, 128), ALU.mult, [kn("BBt"), "cst"], kBBx)
                    tt(KKx, bc_mid(KKt[:, q, :], 8), bc_last(segT, 128), ALU.mult, [kn("KKt"), "cst"], kKKx)
                    tt(Rsx, bc_mid(Rst[:, q, :], 8), segF, ALU.mult, [kn("Rst"), "cst"], kRsx)
                    yo = pY[:, q * 128:(q + 1) * 128]
                    mm(yo, LA[:, q, 128:256], QU[:, q, 128:256], True, False, [kn("LA"), kn("QU")], [kY])
                    mm(yo, AK[:, q, 128:256], Vt[:, q, :], False, False, [kn("AK"), kn("Vt")], [kY])
                    for n in range(8):
                        mm(yo, Rsx[:, n, :], H0b[:, n, :], False, n == 7, kRsx + kH0b, [kY])
                    MnT8 = b3(MT["MnT"], 8, 128)
                    pM = [pm(pin=True), pm(pin=True)]
                    for n in range(8):
                        mm(pM[n // 4][:, (n % 4) * 128:(n % 4 + 1) * 128], QU[:, q, 0:128], BBx[:, n, :], True, True,
                           [kn("QU")] + kBBx, [ps_key(pM[n // 4])])
                    for hf in (0, 1):
                        cp(MnT8[:, hf * 4:hf * 4 + 4, :], b3(pM[hf], 4, 128), [ps_key(pM[hf])], [("MnT", hf)])
                        unpin(pM[hf])
                    pS = [pm(pin=True), pm(pin=True)]
                    for n in range(8):
                        po = pS[n // 4]; ko = ps_key(po)
                        oo = po[:, (n % 4) * 128:(n % 4 + 1) * 128]
                        mm(oo, BBx[:, n, :], QU[:, q, 128:256], True, False, kBBx + [kn("QU")], [ko])
                        mm(oo, KKx[:, n, :], Vt[:, q, :], False, False, kKKx + [kn("Vt")], [ko])
                        mm(oo, MnT8[:, n, :], H0b[:, n, :], False, True, [("MnT", n // 4)] + kH0b, [ko])
                    wseg = b3(F["T3"][:, c0 + q * 64:c0 + q * 64 + 64], 8, 8)[:, :, 7]
                    for hf in (0, 1):
                        hv = H0f[:, hf * 4:hf * 4 + 4, :]
                        tt(hv, hv, bc_last(wseg[:, hf * 4:hf * 4 + 4], 128), ALU.mult, ["H0f", fk("T3")] + kH0b, ["H0f"])
                        tt(hv, hv, b3(pS[hf], 4, 128), ALU.add, ["H0f", ps_key(pS[hf])], ["H0f"])
                    unpin(pS[0]); unpin(pS[1])
                    store(o_wkvs[hp, 0:64, n0:n0 + 8, :], H0f[0:64, :, 0:64], ["H0f"])
                    store(o_wkvs[hp, 64:128, n0:n0 + 8, :], H0f[64:128, :, 64:128], ["H0f"])
            yield "tail"
            if main:
                YN = b3(MT[kn("YN")], 2, 128)
                yv = b3(pY[:, 0:256], 2, 128)
                T.add("vector", lambda e: e.tensor_reduce(out=stat[:, 0:2], in_=yv, axis=AX.X, op=ALU.add), [kY], [kn("stat")])
                sqt = b3(MT[kn("sqt")], 2, 128)
                act(MT[kn("sqt")], pY[:, 0:256], AF.Square, [kY], [kn("sqt")])
                T.add("vector", lambda e: e.tensor_reduce(out=stat[:, 2:4], in_=sqt, axis=AX.X, op=ALU.add), [kn("sqt")], [kn("stat")])
                ts(stat[:, 4:6], stat[:, 0:2], 1.0 / 64, None, ALU.mult, None, [kn("stat")], [kn("stat")])
                tt(stat[:, 6:8], stat[:, 4:6], stat[:, 4:6], ALU.mult, [kn("stat")], [kn("stat")])
                stt(stat[:, 8:10], stat[:, 2:4], 1.0 / 64, stat[:, 6:8], ALU.mult, ALU.subtract, [kn("stat")], [kn("stat")])
                act(stat[:, 10:12], stat[:, 8:10], AF.Ln, [kn("stat"), "eps"], [kn("stat")], bias=epsg)
                act(stat[:, 12:14], stat[:, 10:12], AF.Exp, [kn("stat")], [kn("stat")], scale=-0.5)
                stt(stat[:, 14:16], stat[:, 4:6], -1.0, stat[:, 12:14], ALU.mult, ALU.mult, [kn("stat")], [kn("stat")])
                for h in (0, 1):
                    rows = slice(h * 64, h * 64 + 64)
                    cs = slice(h * 64, h * 64 + 64)
                    tt(YN[rows, :, cs], yv[rows, :, cs], bc_last(stat[rows, 12:14], 64), ALU.mult, [kY, kn("stat")], [kn("YN")])
                    tt(YN[rows, :, cs], YN[rows, :, cs], bc_last(stat[rows, 14:16], 64), ALU.add, [kn("YN"), kn("stat")], [kn("YN")])
                p = pm(); k = ps_key(p)
                for q in (0, 1):
                    tr(psb(p)[:, q * 128:(q + 1) * 128], YN[:, q, :], [kn("YN")], [k])
                pv = b3(psb(p)[:, 0:256], 2, 128)
                half = MT[kn("hf")].rearrange("p (q s) -> p q s", q=2)
                cp(half, pv[:, :, 0:64], [k], [kn("hf")])
                tt(F["db"][:, c0:c0 + 128].rearrange("p (q s) -> p q s", q=2), half, pv[:, :, 64:128], ALU.add,
                   [kn("hf"), k], [fk("db")])
                unpin(pY)


        def trn(out, in_, np_, r, w):
            return T.add("tensor", lambda e: e.transpose(out, in_, CT["ident"][0:np_, 0:np_]), r + ["cst"], w)

        gbase = [mats0]

        def galloc(n):
            o = gbase[0]
            gbase[0] += n
            assert gbase[0] <= mix_top
            return V(o, n)

        GK, GV, GA, GON, GSs, GS0, GKx, GQx, GST = (galloc(n) for n in (128, 256, 64, 512, 768, 2048, 512, 256, 64))
        GSET = {"": (GK, GV, GA, GON, GSs, GST, galloc(384)), "_B": tuple(galloc(n) for n in (128, 256, 64, 512, 768, 64, 384))}
        GS0b = galloc(1024)

        def gla_xgate(mode):
            wi = wload(w_gl[16])
            for grp in groups(mode):
                tc0, N, kind, _ = grp
                ps = proj_ps()
                proj(wi, tc0, N, ps)
                cp(xgT[:, tc0:tc0 + N], ps[:, 0:N], [ps_key(ps)], ["xgT"])

        gl_lock = {"busy": False}

        def gl_group(g, grp, main, W, fp, tag):
            fk = lambda n: n + "#%d" % fp
            FT = FTS[fp]
            tc0, N, kind, _ = grp
            L = 64 if kind == "p" else 8
            nseg = N // L
            F = {k: v[:, 0:N] for k, v in FT.items() if k != "pb"}
            F.update({fk(k): v for k, v in list(F.items())})
            gs = slice(g, g + 1)
            scanm = CT["scan_p" if kind == "p" else "scan_s"][:, 0:N]
            p = pm(); k = ps_key(p)
            mm(p[:, 0:N], gw2b[:, g * 128:(g + 1) * 128], xgT[:, tc0:tc0 + N], True, True, ["gw2b", "xgT"], [k])
            act(F["T0"], p[:, 0:N], AF.Exp, [k, "ngb"], [fk("T0")], bias=ngb[:, gs], scale=-1.0)
            act(F["T0"], F["T0"], AF.Ln, [fk("T0"), "eps"], [fk("T0")], bias=one1)
            T.add("vector", lambda e: e.tensor_tensor_scan(F["T1"], scanm, F["T0"], 0.0, ALU.mult, ALU.add),
                  ["cst", fk("T0")], [fk("T1")])
            act(F["T3"], F["T1"], AF.Exp, [fk("T1")], [fk("T3")], scale=-1.0 / 16)
            act(F["T4"], F["T1"], AF.Exp, [fk("T1")], [fk("T4")], scale=1.0 / 16)
            yield
            wi = W["k"]; ps = proj_ps(); proj(wi, tc0, N, ps)
            tt(F["T5"], ps[:, 0:N], F["T4"], ALU.mult, [ps_key(ps), fk("T4")], [fk("T5")])
            wend = bc_last(b3(F["T3"], nseg, L)[:, :, L - 1], L)
            tt(b3(F["T6"], nseg, L), b3(F["T5"], nseg, L), wend, ALU.mult, [fk("T5"), fk("T3")], [fk("T6")])
            for hf, nm in ((0, fk("mk")), (1, fk("mv"))):
                yield
                wi = W["v%d" % hf]; ps = proj_ps(); proj(wi, tc0, N, ps)
                cp(F[nm], ps[:, 0:N], [ps_key(ps)], [nm])
            yield
            if main:
                wi = W["q"]; ps = proj_ps(); proj(wi, tc0, N, ps)
                stt(F["T7"], ps[:, 0:N], 128.0 ** -0.5, F["T3"], ALU.mult, ALU.mult, [ps_key(ps), fk("T3")], [fk("T7")])
                cp(F["T0"].bitcast(BF16)[:, 0:N], F["T5"], [fk("T5")], [fk("T0")])
                cp(F["T1"].bitcast(BF16)[:, 0:N], F["T7"], [fk("T7")], [fk("T1")])
                for hf, nm, tn in ((0, fk("T8"), fk("T10")), (1, fk("T9"), fk("T11"))):
                    yield
                    wi = W["go%d" % hf]; ps = proj_ps(); proj(wi, tc0, N, ps)
                    cp(F[nm], ps[:, 0:N], [ps_key(ps)], [nm])
                    act(F[tn], F[nm], AF.Exp, [nm], [tn], scale=-1.0)
                    act(F[tn], F[tn], AF.Ln, [tn, "eps"], [tn], bias=one1)
                    act(F[tn], F[tn], AF.Exp, [tn], [tn], scale=-1.0)
                    tt(F[nm], F[nm], F[tn], ALU.mult, [nm, tn], [nm])
            proj_done.add(tag)
            yield
            while gl_lock["busy"]:
                yield
            gl_lock["busy"] = True
            def gl_sub(sub, sfx):
                kn = lambda n: n + sfx
                GK, GV, GA, GON, GSs, GST, GSsb = GSET[sfx]
                GKb, GVb, GAb = GK.bitcast(BF16), GV.bitcast(BF16), GA.bitcast(BF16)
                T5b, T7b = F["T0"].bitcast(BF16), F["T1"].bitcast(BF16)
                c0 = sub * 128
                GKv, GVv, GAv, ONv = b3(GKb, 2, 128), b3(GVb, 2, 256), b3(GAb, 2, 64), b3(GON, 2, 256)
                Ss, Ssb = b3(GSs, 3, 256), b3(GSsb.bitcast(BF16), 3, 256)
                p = pm(); k = ps_key(p)
                for q in (0, 1):
                    trn(p[0:64, q * 128:(q + 1) * 128], F["T6"][:, c0 + q * 64:c0 + q * 64 + 64], 128, [fk("T6")], [k])
                cp(GKb[0:64, :], p[0:64, 0:256], [k], [kn("GK")])
                p = pm(); k = ps_key(p)
                for q in (0, 1):
                    for hf, nm in ((0, fk("mk")), (1, fk("mv"))):
                        trn(p[0:64, q * 256 + hf * 128:q * 256 + hf * 128 + 128],
                            F[nm][:, c0 + q * 64:c0 + q * 64 + 64], 128, [nm], [k])
                cp(GVb[0:64, :], p[0:64, 0:512], [k], [kn("GV")])
                yield
                if main:
                    p = pm(); k = ps_key(p)
                    for q in (0, 1):
                        cs = slice(c0 + q * 64, c0 + q * 64 + 64)
                        mm(p[0:64, q * 64:(q + 1) * 64], T5b[:, cs], T7b[:, cs], True, True, [fk("T0"), fk("T1")], [k])
                    gm = CT["G_p" if kind == "p" else "G_s"]
                    tt(GAv[0:64], b3(p[0:64, 0:128], 2, 64), bc_mid(gm[0:64, :], 2), ALU.mult, [k, "cst"], [kn("GA")])
                    pO = pm(pin=True); kO = ps_key(pO)
                yield
                if kind == "p":
                    cp(Ss[:, 0, :], Sgl[:, g, :], ["Sgl"], [(kn("Ss"), 0)])
                    cp(Ssb[:, 0, :], Sgl[:, g, :], ["Sgl"], [(kn("Ssb"), 0)])
                    for q in (0, 1):
                        cs = slice(c0 + q * 64, c0 + q * 64 + 64)
                        if main:
                            oo = pO[0:64, q * 256:(q + 1) * 256]
                            mm(oo, GAv[0:64, q, :], GVv[0:64, q, :], True, False, [kn("GA"), kn("GV")], [kO])
                            mm(oo, T7b[:, cs], Ssb[:, q, :], False, True, [fk("T1"), (kn("Ssb"), q)], [kO])
                        p = pm(); k = ps_key(p)
                        mm(p[:, 0:256], GKv[0:64, q, :], GVv[0:64, q, :], True, True, [kn("GK"), kn("GV")], [k])
                        wc = F["T3"][:, c0 + q * 64 + 63:c0 + q * 64 + 64]
                        stt(Ssb[:, q + 1, :], Ss[:, q, :], wc, p[:, 0:256], ALU.mult, ALU.add,
                            [(kn("Ss"), q), fk("T3"), k], [(kn("Ssb"), q + 1)])
                        stt(Ss[:, q + 1, :], Ss[:, q, :], wc, p[:, 0:256], ALU.mult, ALU.add,
                            [(kn("Ss"), q), fk("T3"), k], [(kn("Ss"), q + 1)])
                    cp(Sgl[:, g, :], Ss[:, 2, :], [(kn("Ss"), 2)], ["Sgl"])
                else:
                    S0v, Kxv, Qxv = b3(GS0, 8, 256), b3(GKx.bitcast(BF16), 8, 128), b3(GQx.bitcast(BF16), 8, 64)
                    S0bv = b3(GS0b.bitcast(BF16), 8, 256)
                    for q in (0, 1):
                        cs = slice(c0 + q * 64, c0 + q * 64 + 64)
                        n0 = (sub * 2 + q) * 8
                        ld(S0v, glS0[g, :, n0:n0 + 8, :], ["GS0"])
                        cp(S0bv, S0v, ["GS0"], ["GS0b"])
                        tt(Qxv, bc_mid(F["T7"][:, cs], 8), b3(CT["gsegF"], 8, 64), ALU.mult, [fk("T7"), "cst"], ["GQx"])
                        tt(Kxv[0:64], bc_mid(GKv[0:64, q, :], 8), bc_last(CT["segT"][0:64, :], 128), ALU.mult,
                           [kn("GK"), "cst"], ["GKx"])
                        oo = pO[0:64, q * 256:(q + 1) * 256]
                        mm(oo, GAv[0:64, q, :], GVv[0:64, q, :], True, False, [kn("GA"), kn("GV")], [kO])
                        for n in range(8):
                            mm(oo, Qxv[:, n, :], S0bv[:, n, :], False, n == 7, ["GQx", "GS0b"], [kO])
                        wseg = b3(F["T3"][:, cs], 8, 8)[:, :, 7]
                        pss = [pm(pin=True) for _ in range(4)]
                        for n in range(8):
                            po = pss[n // 2]
                            mm(po[:, (n % 2) * 256:(n % 2 + 1) * 256], Kxv[0:64, n, :], GVv[0:64, q, :], True, True,
                               ["GKx", kn("GV")], [ps_key(po)])
                        for j in range(4):
                            sv = S0v[:, 2 * j:2 * j + 2, :]
                            tt(sv, sv, bc_last(wseg[:, 2 * j:2 * j + 2], 256), ALU.mult, ["GS0", fk("T3"), kO], ["GS0"])
                            tt(sv, sv, b3(pss[j], 2, 256), ALU.add, ["GS0", ps_key(pss[j])], ["GS0"])
                        for po in pss:
                            unpin(po)
                        store(o_glas[g, :, n0:n0 + 8, :], S0v, ["GS0"])
                yield
                if main:
                    stat = GST
                    act(GON[0:64, :], pO[0:64, 0:512], AF.Square, [kO], [kn("GON")])
                    T.add("vector", lambda e: e.tensor_reduce(out=stat[0:64, 0:2], in_=ONv[0:64], axis=AX.X, op=ALU.add),
                          [kn("GON")], [kn("stat")])
                    act(stat[0:64, 2:4], stat[0:64, 0:2], AF.Ln, [kn("stat"), "eps"], [kn("stat")], bias=epsh[0:64], scale=1.0 / 256)
                    act(stat[0:64, 4:6], stat[0:64, 2:4], AF.Exp, [kn("stat")], [kn("stat")], scale=-0.5)
                    tt(ONv[0:64], b3(pO[0:64, 0:512], 2, 256), bc_last(stat[0:64, 4:6], 256), ALU.mult, [kO, kn("stat"), kn("GON")], [kn("GON")])
                    p = pm(); k = ps_key(p)
                    for q in (0, 1):
                        for hf in (0, 1):
                            trn(p[:, hf * 128 + q * 64:hf * 128 + q * 64 + 64], ONv[0:64, q, hf * 128:(hf + 1) * 128], 64,
                                [kn("GON")], [k])
                    for hf, nm in ((0, fk("T8")), (1, fk("T9"))):
                        stt(oT[:, 8 + 2 * g + hf, tc0 + c0:tc0 + c0 + 128], p[:, hf * 128:(hf + 1) * 128],
                            glvt[:, 4 + hf:5 + hf], F[nm][:, c0:c0 + 128], ALU.mult, ALU.mult,
                            [k, "glv", nm], [("oT", 8 + 2 * g + hf)])
                    unpin(pO)
                yield
            gens = [gl_sub(sub, ("", "_B")[sub]) for sub in range(N // 128)]
            while gens:
                for gsub in list(gens):
                    try:
                        next(gsub)
                    except StopIteration:
                        gens.remove(gsub)
                yield
            gl_lock["busy"] = False
            if main and grp[3]:
                store(o_glap[g, :, :], Sgl[:, g, :], ["Sgl"])

        def zero_blk():
            for nm in ("KR", "Bf", "Cf", "BBf", "KKf", "Vf", "H0f", "KR_B", "Bf_B", "Cf_B", "BBf_B", "KKf_B", "Vf_B", "YN", "YN_B"):
                T.add("vector", lambda e, m=MT[nm]: e.memset(m, 0.0), (), [nm])

        for mode in ("P", "M"):
            main = mode == "M"
            norm_tokens(xM if main else xP, NTM if main else PRE, 0, hT, "hT", base=top[0] - 6656)
            T.barrier()
            zero_blk()
            if main:
                ldc(l2b, lora2[:, :, :], ["l2b"])
            lora_stage(mode)
            seq = [(hp, gi, grp) for hp in range(8) for gi, grp in enumerate(groups(mode))]
            wts, active, nxt = {}, [], 0

            def start(j, main=main, seq=seq, wts=wts):
                hp, gi, grp = seq[j]
                if gi == 0:
                    wk, wv = wload_rw(w_rw[8 + hp]), wload_rw(w_rw[16 + hp])
                    if main:
                        wr = wload_rw(w_rw[hp])
                    else:
                        wr = None
                        wr0 = wload_rw(w_rw[hp])
                        ps = proj_ps()
                        proj(wr0, PRE - 64, 64, ps)
                        cp(carry[:, hp:hp + 1], ps[:, 63:64], [ps_key(ps)], ["carry"])
                    wts[hp] = (wk, wv, wr)
                return rw_group(hp, grp, main, *wts[hp], j % 2, ("rw", mode, j))

            while nxt < len(seq) or active:
                while len(active) < 2 and nxt < len(seq):
                    if seq[nxt][1] == 0 and any(t not in proj_done for _, t in active):
                        break
                    active.append((start(nxt), ("rw", mode, nxt)))
                    nxt += 1
                for gg in list(active):
                    try:
                        next(gg[0])
                    except StopIteration:
                        active.remove(gg)
            T.barrier()
            gla_xgate(mode)
            gseq = [(g, gi, grp) for g in range(4) for gi, grp in enumerate(groups(mode))]
            gw, gact, gnx = {}, [], 0

            def gstart(j, main=main, gseq=gseq, gw=gw):
                g, gi, grp = gseq[j]
                if gi == 0:
                    W = {}
                    srcs = [("k", 4 + g), ("v0", 8 + 2 * g), ("v1", 9 + 2 * g)]
                    if main:
                        srcs += [("q", g), ("go0", 17 + 2 * g), ("go1", 18 + 2 * g)]
                    for slot, (nm, ci) in enumerate(srcs):
                        ldc(wst[slot], w_gl[ci], [("w", slot)])
                        W[nm] = slot
                    gw[g] = W
                return gl_group(g, grp, main, gw[g], j % 2, ("gl", mode, j))

            while gnx < len(gseq) or gact:
                while len(gact) < 2 and gnx < len(gseq):
                    if gseq[gnx][1] == 0 and any(t not in proj_done for _, t in gact):
                        break
                    gact.append((gstart(gnx), ("gl", mode, gnx)))
                    gnx += 1
                for gg in list(gact):
                    try:
                        next(gg[0])
                    except StopIteration:
                        gact.remove(gg)
            T.barrier()
        store(o_shift[:, :, :], shout, ["shout"])

        X1 = V(ov0, 18432).rearrange("p (k t) -> p k t", k=KB)
        rtmp = V(ov0 + 18432 + 2560, 512)
        ttiles = [(0, 512), (512, 512), (1024, 128)]
        for fc in range(KB):
            wi = wload(w_o[fc])
            ld(X1[:, fc, :], xM[:, fc, :], [("X1", fc)])
            for t0, tw in ttiles:
                ps = proj_ps(); k = ps_key(ps)
                for kk in range(KB):
                    mm(ps[:, 0:tw], wst[wi][:, kk, :], oT[:, kk, t0:t0 + tw], kk == 0, kk == KB - 1,
                       [("w", wi), ("oT", kk)], [k])
                tt(X1[:, fc, t0:t0 + tw], X1[:, fc, t0:t0 + tw], ps[:, 0:tw], ALU.add, [("X1", fc), k], [("X1", fc)])
        x1k = [("X1", fc) for fc in range(KB)]
        norm_tokens((X1, x1k), NTM, 1, hT, "hT", base=ov0 + 18432)
        uT = oT
        for g4 in range(4):
            for fc in range(KB):
                wi = wload(w_u[g4 * KB + fc])
                for t0, tw in ttiles:
                    ps = proj_ps(); k = ps_key(ps)
                    for kk in range(KB):
                        mm(ps[:, 0:tw], wst[wi][:, kk, :], hT[:, kk, t0:t0 + tw], kk == 0, kk == KB - 1,
                           [("w", wi), "hT"], [k])
                    act(rtmp[:, 0:tw], ps[:, 0:tw], AF.Relu, [k], ["rtmp"])
                    tt(uT[:, fc, t0:t0 + tw], rtmp[:, 0:tw], rtmp[:, 0:tw], ALU.mult, ["rtmp"], [("oT", fc)])
            for fc in range(KB):
                wi = wload(w_d[g4, fc])
                for t0, tw in ttiles:
                    ps = proj_ps(); k = ps_key(ps)
                    for kk in range(KB):
                        mm(ps[:, 0:tw], wst[wi][:, kk, :], uT[:, kk, t0:t0 + tw], kk == 0, kk == KB - 1,
                           [("w", wi), ("oT", kk)], [k])
                    tt(X1[:, fc, t0:t0 + tw], X1[:, fc, t0:t0 + tw], ps[:, 0:tw], ALU.add, [("X1", fc), k], [("X1", fc)])
        fb = ov0 + 18432
        lnF, rsF = V(fb, NTM), V(fb + NTM, NTM)
        for k in range(KB):
            act(hT[:, k, :], X1[:, k, :], AF.Square, [("X1", k)], ["hT"])
        for ti, (t0, tw) in enumerate(ttiles):
            pb_ = PS[2 + ti]
            for k in range(KB):
                mm(pb_[:, 0:tw], onesb, hT[:, k, t0:t0 + tw], k == 0, k == KB - 1, ["onesb", "hT"], [ps_key(pb_)])
            act(lnF[:, t0:t0 + tw], pb_[:, 0:tw], AF.Ln, [ps_key(pb_), "eps"], ["lnF"], bias=epsn, scale=1.0 / D)
            act(rsF[:, t0:t0 + tw], lnF[:, t0:t0 + tw], AF.Exp, ["lnF"], ["rsF"], scale=-0.5)
        for k in range(KB):
            stt(X1[:, k, :], X1[:, k, :], gv[:, 2, k:k + 1], rsF, ALU.mult, ALU.mult, [("X1", k), "gv", "rsF"], [("X1", k)])
            store(yT[:, k, :], X1[:, k, :], [("X1", k)])
        T.emit(out_ids)
    return nc


def _fm(a2d):
    t = a2d.shape[0]
    return np.ascontiguousarray(a2d.T.reshape(KB, 128, t).transpose(1, 0, 2))


def _wt(w, nch):
    return np.ascontiguousarray(w.reshape(KB, 128, nch, 128).transpose(2, 1, 0, 3))


def _pad_cols(a, segs, axis=-1):
    out = []
    for s0, n in segs:
        piece = np.take(a, np.arange(s0, s0 + n), axis=axis)
        if n < 128:
            padw = [(0, 0)] * a.ndim
            padw[axis] = (0, 128 - n)
            piece = np.pad(piece, padw)
        out.append(piece)
    return np.concatenate(out, axis=axis)


RW_SEGS = [(i * 128, 128) for i in range(24)] + [(3072, 64), (3136, 64), (3200, 128), (3328, 32)]
RW_UNPAD = np.concatenate([np.arange(c * 128, c * 128 + n) for c, (s0, n) in enumerate(RW_SEGS)])
GL0 = RW_PROJ
GL_SEGS = [(GL0 + i * 128, 128) for i in range(16)] + [(GL0 + 2048, 16)] + [(GL0 + 2064 + i * 128, 128) for i in range(8)]


def kernel(x_prompt, x_sample, state_rwkv_shift, state_rwkv_wkv, state_gla, norm1_g, w_in, rw_mu,
           rw_w0, rw_w2, rw_a0, rw_a2, rw_g2, rw_k_k, rw_k_a, rw_r_k, rw_ln_w, rw_ln_b, gla_gw2,
           gla_gb, gla_norm_w, w_out, norm2_g, w_up, w_down, norm_f_g):
    f32 = np.float32
    A = lambda z: np.asarray(z, f32)
    x_prompt, x_sample = A(x_prompt), A(x_sample)
    w_in0 = A(w_in)[0]
    w_rw = _wt(_pad_cols(w_in0, RW_SEGS), NRW)
    w_gl = _wt(_pad_cols(w_in0, GL_SEGS), NGL)
    mu = np.ascontiguousarray(_pad_cols(A(rw_mu)[0], RW_SEGS).reshape(NRW, 128).T)
    vecT = lambda v: np.ascontiguousarray(A(v).reshape(-1, 128).T)
    gvec = np.ascontiguousarray(np.stack([vecT(A(norm1_g)[0]), vecT(A(norm2_g)[0]), vecT(A(norm_f_g))], 1))
    rwv = np.ascontiguousarray(np.stack([vecT(A(v)[0]) for v in (rw_w0, rw_a0, rw_k_k, rw_k_a, rw_r_k, rw_ln_w, rw_ln_b)], 1))
    lora2 = np.zeros((128, 4, 1024), f32)
    lora2[0:64, 0], lora2[0:64, 1] = A(rw_w2)[0], A(rw_a2)[0]
    lora2[0:128, 2], lora2[0:32, 3] = A(rw_g2)[0][0:128], A(rw_g2)[0][128:160]
    glv = np.ascontiguousarray(np.concatenate([vecT(A(gla_gb)[0]), vecT(A(gla_norm_w)[0])], 1))
    gw2 = np.zeros((128, 512), f32)
    gw2[0:16] = A(gla_gw2)[0]
    w_o = _wt(A(w_out)[0], 16)
    w_u = _wt(A(w_up)[0], 64)
    w_d = np.ascontiguousarray(A(w_down)[0].reshape(4, KB, 128, 16, 128).transpose(0, 3, 2, 1, 4))
    sh = A(state_rwkv_shift)[0]
    wkv = A(state_rwkv_wkv)[0]
    gls = A(state_gla)[0]
    shared = {"cst": CST, "gvec": gvec, "w_rw": w_rw, "mu_rw": mu, "rwv": rwv, "lora2": lora2, "w_gl": w_gl,
              "glv": glv, "gw2": gw2, "w_o": w_o, "w_u": w_u, "w_d": w_d}
    in_maps = []
    for c in range(N_CORES):
        b, par = divmod(c, 2)
        own = x_prompt[b, par * HALF:(par + 1) * HALF]
        pre = x_prompt[b, 0:PRE] if par == 1 else np.zeros((PRE, D), f32)
        smp = x_sample[NS * c:NS * (c + 1)].reshape(SMP, D)
        ns = slice(NS * c, NS * (c + 1))
        shT = np.ascontiguousarray(_pad_cols(sh[ns], RW_SEGS).reshape(NS, NRW, 128).transpose(2, 1, 0))
        h0 = np.ascontiguousarray(wkv[ns].transpose(1, 3, 0, 2).reshape(8, 128, NS, 64))
        s0 = np.ascontiguousarray(gls[ns].transpose(1, 2, 0, 3))
        m = dict(shared)
        m.update({"xP": _fm(pre), "xM": _fm(np.concatenate([own, smp], 0)), "shiftT": shT, "rwH0": h0, "glS0": s0})
        in_maps.append(m)
    nc = build_nc()
    res = run_bass_kernel_spmd(nc, in_maps, core_ids=list(range(N_CORES)))

    B = x_prompt.shape[0]
    y_p = np.zeros((B, SEQ, D), f32)
    y_s = np.zeros((DEC_B, DEC_T, D), f32)
    sh_p = np.zeros((1, B, RW_PROJ), f32)
    sh_s = np.zeros((1, DEC_B, RW_PROJ), f32)
    wkv_p = np.zeros((1, B, RW_H, RW_HD, RW_HD), f32)
    wkv_s = np.zeros((1, DEC_B, RW_H, RW_HD, RW_HD), f32)
    gla_p = np.zeros((1, B, 4, 128, 256), f32)
    gla_s = np.zeros((1, DEC_B, 4, 128, 256), f32)
    for c in range(N_CORES):
        b, par = divmod(c, 2)
        r = res.results[c]
        ns = slice(NS * c, NS * (c + 1))
        y = r["yT"].transpose(2, 1, 0).reshape(NTM, D)
        y_p[b, par * HALF:(par + 1) * HALF] = y[:HALF]
        y_s[ns] = y[HALF:].reshape(NS, DEC_T, D)
        rows = r["o_shift"].transpose(2, 1, 0).reshape(1 + NS, NRW * 128)[:, RW_UNPAD]
        sh_s[0, ns] = rows[1:]
        wkv_s[0, ns] = r["o_wkvs"].reshape(8, 2, 64, NS, 64).transpose(3, 0, 1, 4, 2).reshape(NS, 16, 64, 64)
        gla_s[0, ns] = r["o_glas"].transpose(2, 0, 1, 3)
        if par == 1:
            sh_p[0, b] = rows[0]
            wkv_p[0, b] = r["o_wkvp"].reshape(16, 64, 64).transpose(0, 2, 1)
            gla_p[0, b] = r["o_glap"]
    return (y_p, y_s, sh_p, wkv_p, gla_p, sh_s, wkv_s, gla_s)
```

```python
from contextlib import ExitStack

import numpy as np
import concourse.bass as bass
import concourse.mybir as mybir
from concourse.bass_utils import run_bass_kernel_spmd

F32 = mybir.dt.float32
BF16 = mybir.dt.bfloat16
ALU = mybir.AluOpType
AF = mybir.ActivationFunctionType

N_CORES = 8
D = 2048
KB = D // 128
SEQ = 2048
HALF = SEQ // 2
PRE = HALF
DEC_B, DEC_T = 128, 8
SMP = (DEC_B // N_CORES) * DEC_T
RW_W, RW_HD, RW_H = 1024, 64, 16
RW_PROJ = 3 * RW_W + 64 + 64 + 160
NORM_EPS = 1e-6

TOK = PRE + HALF + SMP
TT = 512


AX = mybir.AxisListType
NS = SMP // DEC_T
NTM = HALF + SMP
QT = 256
C0 = 0.6065306597126334
GN_EPS = 64e-5
HN_EPS = 1e-5
NRW = 28
NGL = 25


class Tracker:
    CE = ("tensor", "scalar", "vector")

    def __init__(self, nc, es):
        self.nc, self.ops, self.lw, self.rs = nc, [], {}, {}
        self.sem = {e: es.enter_context(nc.semaphore("c_" + e)) for e in self.CE}
        self.pool = {"sync": [es.enter_context(nc.semaphore(f"ds{i}")) for i in range(16)],
                     "gpsimd": [es.enter_context(nc.semaphore(f"dg{i}")) for i in range(12)]}
        self.bar = {}
        self.seen_after_bar = set()

    def barrier(self):
        last = {}
        for i, o in enumerate(self.ops):
            last[o["eng"]] = i
        self.bar = dict(last)
        self.bar_dma = []
        for q in ("sync", "gpsimd"):
            ids = [i for i, o in enumerate(self.ops) if o["eng"] == q and o["make"] is not None]
            self.bar_dma += ids[-len(self.pool[q]):]
        self.seen_after_bar = set()

    def add(self, eng, make, r=(), w=()):
        i = len(self.ops)
        raw, oth = set(), set()
        for k in r:
            p = self.lw.get(k)
            if p is not None:
                raw.add(p)
        for k in w:
            p = self.lw.get(k)
            if p is not None:
                oth.add(p)
            for q in self.rs.get(k, {}).values():
                oth.add(q)
        deps = set()
        for p in raw | oth:
            pe = self.ops[p]["eng"]
            if pe == eng:
                if eng == "tensor":
                    continue
                if eng in self.CE and p not in raw:
                    continue
            deps.add(p)
        if self.bar and eng not in self.seen_after_bar:
            self.seen_after_bar.add(eng)
            deps |= set(v for e, v in self.bar.items() if e != eng or eng not in self.CE)
            deps |= set(self.bar_dma)
        self.ops.append(dict(eng=eng, make=make, deps=deps, sig=False))
        for k in r:
            d = self.rs.setdefault(k, {})
            d[eng if eng in self.CE else ("dma", i)] = i
        for k in w:
            self.lw[k] = i
            self.rs[k] = {}
        return i

    def emit(self, out_ids):
        ops = self.ops
        fence = self.add("sync", None)
        ops[fence]["deps"] = set(out_ids)
        for o in ops:
            for p in o["deps"]:
                ops[p]["sig"] = True
        cnt = {e: 0 for e in self.CE}
        ndma = {q: 0 for q in self.pool}
        pcnt = {q: [0] * len(self.pool[q]) for q in self.pool}
        for o in ops:
            e = o["eng"]
            o["sv"] = None
            if o["make"] is None:
                continue
            if e in self.CE:
                if o["sig"]:
                    cnt[e] += 1
                    o["sv"] = (self.sem[e], cnt[e], 1)
            else:
                j = ndma[e] % len(self.pool[e])
                ndma[e] += 1
                pcnt[e][j] += 16
                o["sv"] = (self.pool[e][j], pcnt[e][j], 16)
        streams = {}
        for i, o in enumerate(ops):
            streams.setdefault(o["eng"], []).append(i)
        waited = {}
        self.nwaits = 0
        with self.nc.Block() as block:
            for e, ids in streams.items():
                def sec(eng, ids=ids, e=e):
                    for i in ids:
                        o = ops[i]
                        need = {}
                        for p in o["deps"]:
                            sem, val, _ = ops[p]["sv"]
                            k = id(sem)
                            if k not in need or need[k][1] < val:
                                need[k] = (sem, val)
                        for k, (sem, val) in need.items():
                            if waited.get((e, k), 0) < val:
                                eng.wait_ge(sem, val)
                                waited[(e, k)] = val
                                self.nwaits += 1
                        if o["make"] is not None:
                            ins = o["make"](eng)
                            if o["sv"] is not None:
                                ins.then_inc(o["sv"][0], o["sv"][2])
                getattr(block, e)(sec)


def host_consts():
    r = np.arange(128)
    h, s = r // 64, r % 64
    same_h = h[:, None] == h[None, :]
    sU = same_h & (s[:, None] < s[None, :])
    iU = same_h & (s[:, None] <= s[None, :])
    sL = same_h & (s[:, None] > s[None, :])
    seg = (s[:, None] // 8) == (s[None, :] // 8)
    f = lambda m: m.astype(np.float32)
    tabs = {}
    tabs["ident"] = np.eye(128, dtype=np.float32)
    tabs["bones"] = f(same_h)
    tabs["UU_p"] = np.concatenate([f(sU), f(iU)], 1)
    tabs["UU_s"] = np.concatenate([f(sU & seg), f(iU & seg)], 1)
    tabs["sL_p"] = f(sL)
    tabs["sL_s"] = f(sL & seg)
    t = np.arange(QT)
    tabs["scan_p"] = np.tile(f(t % 64 != 0)[None, :], (128, 1))
    tabs["scan_s"] = np.tile(f(t % 8 != 0)[None, :], (128, 1))
    n = np.arange(8)
    segF = f((s[None, None, :] // 8) == n[None, :, None])
    tabs["segF"] = np.tile(segF.reshape(1, 8 * 128), (128, 1))
    tabs["segT"] = f((s[:, None] // 8) == n[None, :])
    s6 = np.arange(128) % 64
    c6 = np.arange(64)
    gi = f(s6[:, None] <= c6[None, :])
    gseg = f((s6[:, None] // 8) == (c6[None, :] // 8))
    tabs["G_p"] = gi
    tabs["G_s"] = gi * gseg
    tabs["gsegF"] = np.tile(f((c6[None, None, :] // 8) == n[None, :, None]).reshape(1, 8 * 64), (128, 1))
    offs, cols, o = {}, [], 0
    for k, v in tabs.items():
        offs[k] = (o, v.shape[1])
        cols.append(v)
        o += v.shape[1]
    return np.ascontiguousarray(np.concatenate(cols, 1)), offs


CST, CST_OFF = host_consts()
NCST = CST.shape[1]


def build_nc():
    nc = bass.Bass("TRN2", target_bir_lowering=False)
    di = lambda n, sh: nc.dram_tensor(n, sh, F32, kind="ExternalInput").ap()
    do = lambda n, sh: nc.dram_tensor(n, sh, F32, kind="ExternalOutput").ap()
    xP, xM = di("xP", [128, KB, PRE]), di("xM", [128, KB, NTM])
    cst = di("cst", [128, NCST])
    gvec = di("gvec", [128, 3, KB])
    w_rw, mu_rw, shiftT = di("w_rw", [NRW, 128, KB, 128]), di("mu_rw", [128, NRW]), di("shiftT", [128, NRW, NS])
    rwv = di("rwv", [128, 7, 8])
    lora2 = di("lora2", [128, 4, 1024])
    rwH0 = di("rwH0", [8, 128, NS, 64])
    w_gl, glv, gw2 = di("w_gl", [NGL, 128, KB, 128]), di("glv", [128, 6]), di("gw2", [128, 512])
    glS0 = di("glS0", [4, 128, NS, 256])
    w_o, w_u, w_d = di("w_o", [16, 128, KB, 128]), di("w_u", [64, 128, KB, 128]), di("w_d", [4, 16, 128, KB, 128])
    yT = do("yT", [128, KB, NTM])
    o_shift = do("o_shift", [128, NRW, 1 + NS])
    o_wkvp, o_wkvs = do("o_wkvp", [8, 128, 64]), do("o_wkvs", [8, 128, NS, 64])
    o_glap, o_glas = do("o_glap", [4, 128, 256]), do("o_glas", [4, 128, NS, 256])

    with ExitStack() as es:
        AR = es.enter_context(nc.sbuf_tensor("arena", [128, 53000], F32))
        PS = [es.enter_context(nc.psum_tensor(f"ps{i}", [128, 512], F32)) for i in range(8)]
        T = Tracker(nc, es)
        top = [0]

        def alloc(n):
            o = top[0]
            top[0] += n
            assert top[0] <= 53000, top[0]
            return o

        def V(o, n, rows=None):
            return AR[:, o:o + n] if rows is None else AR[rows[0]:rows[1], o:o + n]

        o_hT, o_oT = alloc(9216), alloc(9216)
        hT = V(o_hT, 9216).bitcast(BF16).rearrange("p (k t) -> p k t", k=KB)
        oT = V(o_oT, 9216).bitcast(BF16).rearrange("p (k t) -> p k t", k=KB)
        NW = 4
        o_w = alloc(1024 * NW)
        wst = [V(o_w + 1024 * i, 1024).bitcast(BF16).rearrange("p (k f) -> p k f", k=KB) for i in range(NW)]
        o_c = alloc(NCST)
        CT = {k: V(o_c + a, n) for k, (a, n) in CST_OFF.items()}
        gv = V(alloc(48), 48).rearrange("p (a k) -> p a k", a=3)
        mut = V(alloc(NRW), NRW)
        rv = V(alloc(56), 56).rearrange("p (a k) -> p a k", a=7)
        shin = V(alloc(NRW * NS), NRW * NS).rearrange("p (c n) -> p c n", c=NRW)
        shout = V(alloc(NRW * (1 + NS)), NRW * (1 + NS)).rearrange("p (c n) -> p c n", c=NRW)
        carry = V(alloc(NRW), NRW)
        glvt = V(alloc(6), 6)
        nw0, na0, omka, ngb = V(alloc(8), 8), V(alloc(8), 8), V(alloc(8), 8), V(alloc(4), 4)
        epsn, eps24, epsg, epsh, one1 = (V(alloc(1), 1) for _ in range(5))
        gw2b = V(alloc(256), 256).bitcast(BF16)
        o_l2b = alloc(2048)
        l2b = V(o_l2b, 2048).bitcast(BF16).rearrange("p (a f) -> p a f", a=4)
        wst += [V(o_l2b + 1024 * i, 1024).bitcast(BF16).rearrange("p (k f) -> p k f", k=KB) for i in range(2)]
        onesb = V(alloc(64), 64).bitcast(BF16)
        Hrw = V(alloc(1024), 1024).rearrange("p (h c) -> p h c", h=8)
        Sgl = V(alloc(1024), 1024).rearrange("p (g e) -> p g e", g=4)
        ov0 = top[0]

        st = {"eng": 0, "w": 0, "wr": 0}

        def cp(out, in_, r, w, scale=None):
            st["eng"] ^= 1
            if st["eng"]:
                if scale is None:
                    return T.add("scalar", lambda e: e.activation(out=out, in_=in_, func=AF.Copy), r, w)
                return T.add("scalar", lambda e: e.activation(out=out, in_=in_, func=AF.Copy, scale=scale), r, w)
            if scale is None:
                return T.add("vector", lambda e: e.tensor_copy(out, in_), r, w)
            return T.add("vector", lambda e: e.tensor_scalar(out, in_, scale, None, ALU.mult), r, w)

        def act(out, in_, func, r, w, bias=None, scale=1.0, accum=None):
            kw = dict(out=out, in_=in_, func=func, scale=scale)
            if bias is not None:
                kw["bias"] = bias
            if accum is not None:
                kw["accum_out"] = accum
            return T.add("scalar", lambda e: e.activation(**kw), r, w)

        def tt(out, a, b, op, r, w):
            return T.add("vector", lambda e: e.tensor_tensor(out, a, b, op), r, w)

        def ts(out, a, s1, s2, op0, op1, r, w):
            if s2 is None:
                return T.add("vector", lambda e: e.tensor_scalar(out, a, s1, None, op0), r, w)
            return T.add("vector", lambda e: e.tensor_scalar(out, a, s1, s2, op0, op1), r, w)

        def stt(out, a, sc, b, op0, op1, r, w):
            return T.add("vector", lambda e: e.scalar_tensor_tensor(out=out, in0=a, scalar=sc, in1=b, op0=op0, op1=op1), r, w)

        def mm(out, lhsT, rhs, start, stop, r, w):
            return T.add("tensor", lambda e: e.matmul(out, lhsT, rhs, start=start, stop=stop), r, w)

        def tr(out, in_, r, w):
            return T.add("tensor", lambda e: e.transpose(out, in_, identb), r + ["identb"], w)

        def ld(out, in_, w, r=()):
            return T.add("sync", lambda e: e.dma_start(out=out, in_=in_), r, w)

        def ldc(out, in_, w, r=()):
            return T.add("gpsimd", lambda e: e.dma_start(out=out, in_=in_), r, w)

        out_ids = []

        def store(out, in_, r):
            out_ids.append(T.add("sync", lambda e: e.dma_start(out=out, in_=in_), r, ()))

        ld(V(o_c, NCST), cst[:, :], ["cst"])
        ld(gv, gvec[:, :, :], ["gv"])
        ld(mut, mu_rw[:, :], ["mut"])
        ld(rv, rwv[:, :, :], ["rv"])
        ld(shin, shiftT[:, :, :], ["shin"])
        ld(glvt, glv[:, :], ["glv"])
        ldc(gw2b, gw2[:, :], ["gw2b"])
        ldc(l2b, lora2[:, :, :], ["l2b"])
        T.add("vector", lambda e: e.memset(onesb, 1.0), (), ["onesb"])
        for tl, val in ((epsn, NORM_EPS), (eps24, 1e-24), (epsg, GN_EPS), (epsh, HN_EPS), (one1, 1.0)):
            T.add("vector", lambda e, tl=tl, val=val: e.memset(tl, val), (), ["eps"])
        T.add("vector", lambda e: e.memset(carry, 0.0), (), ["carry"])
        T.add("vector", lambda e: e.memset(Hrw, 0.0), (), [("Hrw", h) for h in range(8)])
        T.add("vector", lambda e: e.memset(Sgl, 0.0), (), ["Sgl"])
        T.add("vector", lambda e: e.memset(shout, 0.0), (), ["shout"])
        ts(nw0, rv[:, 0, :], -1.0, None, ALU.mult, None, ["rv"], ["nw0"])
        ts(na0, rv[:, 1, :], -1.0, None, ALU.mult, None, ["rv"], ["na0"])
        ts(omka, rv[:, 3, :], -1.0, 1.0, ALU.mult, ALU.add, ["rv"], ["omka"])
        ts(ngb, glvt[:, 0:4], -1.0, None, ALU.mult, None, ["glv"], ["ngb"])

        lorT = V(alloc(2304), 2304).bitcast(BF16).rearrange("p (a t) -> p a t", a=4)
        xgT = V(alloc(576), 576).bitcast(BF16)
        identb = V(alloc(64), 64).bitcast(BF16)
        Hrwb = V(alloc(512), 512).bitcast(BF16).rearrange("p (h c) -> p h c", h=8)
        T.add("vector", lambda e: e.memset(Hrwb, 0.0), (), [("Hrwb", h) for h in range(8)])
        FTS = []
        for _fp in (0, 1):
            d = {"pb": V(alloc(260), 260)}
            for nm in ("db", "mr", "mk", "mv", "T0", "T1", "T3", "T4", "T5", "T6", "T7", "T8", "T9", "T10", "T11"):
                d[nm] = V(alloc(QT), QT)
            FTS.append(d)
        FT = FTS[0]
        MT, MTO = {}, {}
        mats0 = top[0]

        def mat(nm, n, dt=BF16):
            if dt == BF16:
                MTO[nm] = alloc(n // 2)
                MT[nm] = V(MTO[nm], n // 2).bitcast(BF16)
            else:
                MTO[nm] = alloc(n)
                MT[nm] = V(MTO[nm], n)
            return MT[nm]

        def psb(p):
            return p.bitcast(BF16)

        for nm, n in (("KR", 512), ("Bf", 256), ("Cf", 256), ("BBf", 256), ("KKf", 256), ("Vf", 256), ("YN", 256), ("YN_B", 256)):
            m = mat(nm, n)
            T.add("vector", lambda e, m=m: e.memset(m, 0.0), (), [nm])
        mat("sqt", 256)
        mat("sqt_B", 256)
        for nm, n in (("KZ", 512), ("BBt", 256), ("KKt", 256), ("Vt", 256), ("LA", 512), ("AK", 512),
                      ("Lt", 256), ("X", 256), ("PP", 512), ("QU", 512), ("Mst", 256), ("Rst", 256),
                      ("Hsb", 384), ("MnT", 1024)):
            mat(nm, n)
        for nm, n in (("Hs", 384), ("stat", 64), ("H0f", 1024), ("hf", 128), ("hf_B", 128), ("Hs_B", 384), ("stat_B", 64)):
            mat(nm, n, F32)
        for nm, n in (("KR", 512), ("Bf", 256), ("Cf", 256), ("BBf", 256), ("KKf", 256), ("Vf", 256)):
            m = mat(nm + "_B", n)
            T.add("vector", lambda e, m=m: e.memset(m, 0.0), (), [nm + "_B"])
        for nm, n in (("KZ", 512), ("BBt", 256), ("KKt", 256), ("Vt", 256), ("LA", 512), ("AK", 512),
                      ("Lt", 256), ("X", 256), ("PP", 512), ("QU", 512), ("Mst", 256), ("Rst", 256), ("Hsb", 384)):
            mat(nm + "_B", n)
        for nm, base in (("BBx", "KZ_B"), ("KKx", "LA_B"), ("Rsx", "Lt_B"), ("H0b", "QU_B")):
            MT[nm] = V(MTO[base], 512).bitcast(BF16)
        T.add("vector", lambda e: e.memset(MT["H0f"], 0.0), (), ["H0f"])
        T.add("vector", lambda e: e.tensor_copy(identb, CT["ident"]), ["cst"], ["identb"])
        mix_top = top[0]
        assert mix_top - mats0 >= 6656, (mix_top, mats0)

        def norm_tokens(src, ntok, gidx, dst, dkey, final_out=None, base=None):
            resident = isinstance(src, tuple)
            o = base
            if not resident:
                xsb = [AR[:, o - 4096 * i:o - 4096 * i + 4096].rearrange("p (k t) -> p k t", k=KB) for i in (0, 1)]
                o += 4096
            sq = AR[:, o:o + 2048].bitcast(BF16).rearrange("p (k t) -> p k t", k=KB)
            lnb = AR[:, o + 2048:o + 2304]
            rsd = AR[:, o + 2304:o + 2560]
            for t0 in range(0, ntok, 256):
                tw = min(256, ntok - t0)
                if resident:
                    xv = src[0][:, :, t0:t0 + tw]
                    xk = lambda k: [src[1][k]]
                else:
                    xi = (t0 // 256) % 2
                    xs = xsb[xi]
                    ld(xs[:, :, 0:tw], src[:, :, t0:t0 + tw], ["xs%d" % xi])
                    xv = xs[:, :, 0:tw]
                    xk = lambda k, xi=xi: ["xs%d" % xi]
                for k in range(KB):
                    act(sq[:, k, 0:tw], xv[:, k, :], AF.Square, xk(k), [("sq", k)])
                for k in range(KB):
                    mm(PS[0][:, 0:tw], onesb, sq[:, k, 0:tw], k == 0, k == KB - 1, ["onesb", ("sq", k)], ["ps0"])
                act(lnb[:, 0:tw], PS[0][:, 0:tw], AF.Ln, ["ps0", "eps"], ["lnb"], bias=epsn, scale=1.0 / D)
                act(rsd[:, 0:tw], lnb[:, 0:tw], AF.Exp, ["lnb"], ["rsd"], scale=-0.5)
                for k in range(KB):
                    if final_out is None:
                        stt(dst[:, k, t0:t0 + tw], xv[:, k, :], gv[:, gidx, k:k + 1], rsd[:, 0:tw],
                            ALU.mult, ALU.mult, xk(k) + ["gv", "rsd"], [dkey])
                    else:
                        stt(xv[:, k, :], xv[:, k, :], gv[:, gidx, k:k + 1], rsd[:, 0:tw],
                            ALU.mult, ALU.mult, xk(k) + ["gv", "rsd"], xk(k))
                if final_out is not None:
                    store(final_out[:, :, t0:t0 + tw], xv, [kk for k in range(KB) for kk in xk(k)])

        def wload(src):
            i = st["w"] % NW
            st["w"] += 1
            ldc(wst[i], src, [("w", i)])
            return i

        wst += [V(o_oT + 4608 + 1024 * i, 1024).bitcast(BF16).rearrange("p (k f) -> p k f", k=KB) for i in range(2)]
        RW_SLOTS = [0, 1, 2, 3, 6, 7]

        def wload_rw(src):
            i = RW_SLOTS[st["wr"] % len(RW_SLOTS)]
            st["wr"] += 1
            ldc(wst[i], src, [("w", i)])
            return i

        def proj(wi, tc0, ntok, ps):
            for k in range(KB):
                mm(ps[:, 0:ntok], wst[wi][:, k, :], hT[:, k, tc0:tc0 + ntok], k == 0, k == KB - 1,
                   [("w", wi), "hT"], [ps_key(ps)])

        def ps_key(ps):
            for i, p in enumerate(PS):
                if p is ps:
                    return f"ps{i}"
            raise KeyError

        pp = {"i": 0}

        def proj_ps():
            pp["i"] ^= 1
            return PS[pp["i"]]

        def shift(c, ps, grp, mdst, mkey, fp=0):
            tc0, ntok, kind, last_own = grp
            pb, db = FTS[fp]["pb"], FTS[fp]["db"]
            kpb, kdb = "pb#%d" % fp, "db#%d" % fp
            cp(pb[:, 0:1], carry[:, c:c + 1], ["carry"], [kpb])
            cp(pb[:, 1:1 + ntok], ps[:, 0:ntok], [ps_key(ps)], [kpb])
            if kind == "p":
                tt(db[:, 0:ntok], pb[:, 0:ntok], pb[:, 1:1 + ntok], ALU.subtract, [kpb], [kdb])
                cp(carry[:, c:c + 1], pb[:, ntok:ntok + 1], [kpb], ["carry"])
                if last_own:
                    cp(shout[:, c, 0:1], pb[:, ntok:ntok + 1], [kpb], ["shout"])
            else:
                cp(db[:, 0:ntok], pb[:, 0:ntok], [kpb], [kdb])
                cp(db[:, 0:ntok].rearrange("p (n t) -> p n t", t=DEC_T)[:, :, 0], shin[:, c, :], ["shin", kdb], [kdb])
                tt(db[:, 0:ntok], db[:, 0:ntok], pb[:, 1:1 + ntok], ALU.subtract, [kpb, kdb], [kdb])
                cp(shout[:, c, 1:1 + NS], pb[:, 1:1 + ntok].rearrange("p (n t) -> p n t", t=DEC_T)[:, :, DEC_T - 1],
                   [kpb], ["shout"])
            stt(mdst[:, 0:ntok], db[:, 0:ntok], mut[:, c:c + 1], pb[:, 1:1 + ntok], ALU.mult, ALU.add,
                [kdb, kpb, "mut"], [mkey])

        def groups(mode):
            if mode == "P":
                return [(t, QT, "p", False) for t in range(0, PRE, QT)]
            return [(t, QT, "p", t + QT == HALF) for t in range(0, HALF, QT)] + [(HALF, SMP, "s", False)]

        pmi = {"i": 0}
        LORK = [("lorT", c) for c in range(24, 28)]
        proj_done = set()
        sub_lock = {"busy": False}

        pinned = set()

        def pm(pin=False):
            while True:
                pmi["i"] = (pmi["i"] + 1) % 6
                if pmi["i"] not in pinned:
                    break
            if pin:
                pinned.add(pmi["i"])
            return PS[2 + pmi["i"]]

        def unpin(ps):
            for i in range(6):
                if PS[2 + i] is ps:
                    pinned.discard(i)

        def b3(ap, n, m):
            return ap.rearrange("p (n m) -> p n m", n=n)

        def bc_mid(ap2, n):
            return ap2.unsqueeze(1).to_broadcast([ap2.shape[0], n, ap2.shape[1]])

        def bc_last(ap2, m):
            return ap2.unsqueeze(2).to_broadcast([ap2.shape[0], ap2.shape[1], m])

        def lora_gen(lc, mode, fp):
            k0, k1 = "T0#%d" % fp, "T1#%d" % fp
            wi = wload(w_rw[lc])
            for grp in groups(mode):
                tc0, N, kind, _ = grp
                ps = proj_ps()
                proj(wi, tc0, N, ps)
                shift(lc, ps, grp, FTS[fp]["T0"], k0, fp)
                yield
                t0, t1 = FTS[fp]["T0"][:, 0:N], FTS[fp]["T1"][:, 0:N]
                dst = lorT[:, lc - 24, tc0:tc0 + N]
                if lc == 24:
                    act(t1, t0, AF.Exp, [k0], [k1], scale=2.0)
                    act(t1, t1, AF.Ln, [k1, "eps"], [k1], bias=one1)
                    act(t1, t1, AF.Exp, [k1], [k1], scale=-1.0)
                    ts(dst, t1, -2.0, 1.0, ALU.mult, ALU.add, [k1], [("lorT", lc)])
                elif lc == 25:
                    cp(dst, t0, [k0], [("lorT", lc)])
                else:
                    act(t1, t0, AF.Exp, [k0], [k1], scale=-1.0)
                    act(t1, t1, AF.Ln, [k1, "eps"], [k1], bias=one1)
                    act(dst, t1, AF.Exp, [k1], [("lorT", lc)], scale=-1.0)
                yield

        def lora_stage(mode):
            for pair in ((24, 25), (26, 27)):
                gens = [lora_gen(lc, mode, i) for i, lc in enumerate(pair)]
                while gens:
                    for gg in list(gens):
                        try:
                            next(gg)
                        except StopIteration:
                            gens.remove(gg)

        def rw_group(hp, grp, main, wk, wv, wr, fp, tag):
            fk = lambda n: n + "#%d" % fp
            FT = FTS[fp]
            tc0, N, kind, _ = grp
            L = 64 if kind == "p" else 8
            nseg = N // L
            F = {k: v[:, 0:N] for k, v in FT.items() if k != "pb"}
            hs = slice(hp, hp + 1)
            hcols = slice(hp * 128, (hp + 1) * 128)
            ps = proj_ps(); proj(wk, tc0, N, ps); shift(8 + hp, ps, grp, FT["mk"], fk("mk"), fp); yield
            ps = proj_ps(); proj(wv, tc0, N, ps); shift(16 + hp, ps, grp, FT["mv"], fk("mv"), fp); yield
            if main:
                ps = proj_ps(); proj(wr, tc0, N, ps); shift(hp, ps, grp, FT["mr"], fk("mr"), fp); yield
            proj_done.add(tag)
            scanm = CT["scan_p" if kind == "p" else "scan_s"][:, 0:N]
            p1 = pm(); k1 = ps_key(p1)
            mm(p1[:, 0:N], l2b[:, 0, hcols], lorT[:, 0, tc0:tc0 + N], True, True, ["l2b"] + LORK, [k1])
            act(F["T0"], p1[:, 0:N], AF.Exp, [k1, "nw0"], [fk("T0")], bias=nw0[:, hs], scale=-1.0)
            act(F["T0"], F["T0"], AF.Ln, [fk("T0"), "eps"], [fk("T0")], bias=one1)
            act(F["T0"], F["T0"], AF.Exp, [fk("T0")], [fk("T0")], scale=-1.0)
            T.add("vector", lambda e: e.tensor_tensor_scan(F["T1"], scanm, F["T0"], 0.0, ALU.mult, ALU.add),
                  ["cst", fk("T0")], [fk("T1")])
            tt(F["T0"], F["T1"], F["T0"], ALU.subtract, [fk("T0"), fk("T1")], [fk("T0")])
            act(F["T3"], F["T1"], AF.Exp, [fk("T1")], [fk("T3")], scale=-C0)
            act(F["T4"], F["T1"], AF.Exp, [fk("T1")], [fk("T4")], scale=C0)
            act(F["T0"], F["T0"], AF.Exp, [fk("T0")], [fk("T0")], scale=-C0)
            yield
            p2 = pm(); k2 = ps_key(p2)
            mm(p2[:, 0:N], l2b[:, 1, hcols], lorT[:, 1, tc0:tc0 + N], True, True, ["l2b"] + LORK, [k2])
            act(F["T1"], p2[:, 0:N], AF.Exp, [k2, "na0"], [fk("T1")], bias=na0[:, hs], scale=-1.0)
            act(F["T1"], F["T1"], AF.Ln, [fk("T1"), "eps"], [fk("T1")], bias=one1)
            act(F["T1"], F["T1"], AF.Exp, [fk("T1")], [fk("T1")], scale=-1.0)
            yield
            ts(F["T5"], F["mk"], rv[:, 2, hs], None, ALU.mult, None, [fk("mk"), "rv"], [fk("T5")])
            act(F["T6"], F["T5"], AF.Square, [fk("T5")], [fk("T6")])
            p3 = pm(); k3 = ps_key(p3)
            mm(p3[:, 0:N], CT["bones"], F["T6"], True, True, ["cst", fk("T6")], [k3])
            act(F["T6"], p3[:, 0:N], AF.Ln, [k3, "eps"], [fk("T6")], bias=eps24)
            act(F["T6"], F["T6"], AF.Exp, [fk("T6")], [fk("T6")], scale=-0.5)
            tt(F["T5"], F["T5"], F["T6"], ALU.mult, [fk("T5"), fk("T6")], [fk("T5")])
            ts(F["T6"], F["T1"], rv[:, 3, hs], omka[:, hs], ALU.mult, ALU.add, [fk("T1"), "rv", "omka"], [fk("T6")])
            tt(F["T6"], F["mk"], F["T6"], ALU.mult, [fk("mk"), fk("T6")], [fk("T6")])
            tt(F["T7"], F["T5"], F["T1"], ALU.mult, [fk("T5"), fk("T1")], [fk("T7")])
            yield
            if main:
                stt(F["T8"], F["mr"], rv[:, 4, hs], F["T6"], ALU.mult, ALU.mult, [fk("mr"), "rv", fk("T6")], [fk("T8")])
                p4 = pm(); k4 = ps_key(p4)
                mm(p4[:, 0:N], CT["bones"], F["T8"], True, True, ["cst", fk("T8")], [k4])
                tt(F["T8"], p4[:, 0:N], F["mv"], ALU.mult, [k4, fk("mv")], [fk("T8")])
                p5 = pm(); k5 = ps_key(p5)
                mm(p5[:, 0:N], l2b[:, 2, hcols], lorT[:, 2, tc0:tc0 + N], True, False, ["l2b"] + LORK, [k5])
                mm(p5[:, 0:N], l2b[:, 3, hcols], lorT[:, 3, tc0:tc0 + N], False, True, ["l2b"] + LORK, [k5])
                cp(F["T9"], p5[:, 0:N], [k5], [fk("T9")])
                tt(F["mr"], F["mr"], F["T3"], ALU.mult, [fk("mr"), fk("T3"), fk("T8")], [fk("mr")])
            yield
            tt(F["T7"], F["T7"], F["T4"], ALU.mult, [fk("T7"), fk("T4")], [fk("T7")])
            tt(F["T6"], F["T6"], F["T4"], ALU.mult, [fk("T6"), fk("T4"), fk("T8")], [fk("T6")])
            tt(F["T5"], F["T5"], F["T0"], ALU.mult, [fk("T5"), fk("T0")], [fk("T5")])
            wend = bc_last(b3(F["T3"], nseg, L)[:, :, L - 1], L)
            tt(b3(F["T10"], nseg, L), b3(F["T7"], nseg, L), wend, ALU.mult, [fk("T7"), fk("T3")], [fk("T10")])
            tt(b3(F["T11"], nseg, L), b3(F["T6"], nseg, L), wend, ALU.mult, [fk("T6"), fk("T3")], [fk("T11")])
            yield
            while sub_lock["busy"]:
                yield
            sub_lock["busy"] = True
            gens = [rw_sub(hp, grp, main, sub, F, ("", "_B")[sub], fk) for sub in range(N // 128)]
            in_tail, held = set(), True
            while gens:
                for gsub in list(gens):
                    try:
                        if next(gsub) == "tail":
                            in_tail.add(id(gsub))
                    except StopIteration:
                        gens.remove(gsub)
                if held and all(id(gsub) in in_tail for gsub in gens):
                    sub_lock["busy"] = False
                    held = False
                yield
            if main:
                stt(F["db"], F["db"], rv[:, 5, hs], F["T8"], ALU.mult, ALU.add, [fk("db"), "rv", fk("T8")], [fk("db")])
                stt(oT[:, hp, tc0:tc0 + N], F["db"], rv[:, 6, hs], F["T9"], ALU.add, ALU.mult,
                    [fk("db"), "rv", fk("T9")], [("oT", hp)])
                if grp[3]:
                    store(o_wkvp[hp, 0:64, :], Hrw[0:64, hp, 0:64], [("Hrw", hp)])
                    store(o_wkvp[hp, 64:128, :], Hrw[64:128, hp, 64:128], [("Hrw", hp)])

        def rw_sub(hp, grp, main, sub, F, sfx, fk):
            kn = lambda n: n + sfx
            tc0, N, kind, _ = grp
            c0 = sub * 128
            KR, KZ, LA, AK, QU = (b3(MT[kn(n)], 2, 256) for n in ("KR", "KZ", "LA", "AK", "QU"))
            Bf, Cf, BBf, KKf, Vf, BBt, KKt, Vt, Lt, X, Mst, Rst = (
                b3(MT[kn(n)], 2, 128) for n in ("Bf", "Cf", "BBf", "KKf", "Vf", "BBt", "KKt", "Vt", "Lt", "X", "Mst", "Rst"))
            PP = b3(MT[kn("PP")], 2, 256)
            stat = MT[kn("stat")]
            srcs = [("T5", KR, 0, kn("KR")), ("T7", Bf, 0, kn("Bf")), ("T6", Cf, 0, kn("Cf")), ("T10", BBf, 0, kn("BBf")),
                    ("T11", KKf, 0, kn("KKf")), ("mv", Vf, 0, kn("Vf"))]
            if main:
                srcs.append(("mr", KR, 128, kn("KR")))
            for nm, dst, co, dk in srcs:
                for h in (0, 1):
                    rows = slice(h * 64, h * 64 + 64)
                    cp(dst[rows, :, co + h * 64:co + h * 64 + 64],
                       F[nm][rows, c0:c0 + 128].rearrange("p (q s) -> p q s", q=2), [fk(nm)], [dk])
            yield
            for src, sk, dst, dk, co in ((KR, kn("KR"), KZ, kn("KZ"), 0), (BBf, kn("BBf"), BBt, kn("BBt"), 0),
                                         (KKf, kn("KKf"), KKt, kn("KKt"), 0), (Vf, kn("Vf"), Vt, kn("Vt"), 0)):
                p = pm(); k = ps_key(p)
                for q in (0, 1):
                    tr(psb(p)[:, q * 128:(q + 1) * 128], src[:, q, 0:128], [sk], [k])
                cp(dst[:, :, 0:128], b3(psb(p)[:, 0:256], 2, 128), [k], [dk])
            yield
            UU = CT["UU_p" if kind == "p" else "UU_s"]
            sL = CT["sL_p" if kind == "p" else "sL_s"]
            W = 256 if main else 128
            for lhs, lk, dst, dk in ((Bf, kn("Bf"), LA, kn("LA")), (Cf, kn("Cf"), AK, kn("AK"))):
                p = pm(); k = ps_key(p)
                for q in (0, 1):
                    mm(p[:, q * 256:q * 256 + W], lhs[:, q, :], KR[:, q, 0:W], True, True, [lk, kn("KR")], [k])
                tt(dst[:, :, 0:W], b3(p, 2, 256)[:, :, 0:W], bc_mid(UU[:, 0:W], 2), ALU.mult, [k, "cst"], [dk])
            p = pm(); k = ps_key(p)
            for q in (0, 1):
                mm(p[:, q * 128:(q + 1) * 128], KR[:, q, 0:128], Bf[:, q, :], True, True, [kn("KR"), kn("Bf")], [k])
            tt(Lt, b3(p[:, 0:256], 2, 128), bc_mid(sL, 2), ALU.mult, [k, "cst"], [kn("Lt")])
            yield
            p = pm(); k = ps_key(p)
            for q in (0, 1):
                mm(p[:, q * 256:q * 256 + 128], Lt[:, q, :], LA[:, q, 0:128], True, True, [kn("Lt"), kn("LA")], [k])
                mm(p[:, q * 256 + 128:q * 256 + 256], LA[:, q, 0:128], Lt[:, q, :], True, True, [kn("Lt"), kn("LA")], [k])
            act(PP, b3(p, 2, 256), AF.Copy, [k], [kn("PP")])
            tt(X, bc_mid(CT["ident"], 2), LA[:, :, 0:128], ALU.subtract, ["cst", kn("LA")], [kn("X")])
            for lvl in range(5):
                yield
                p = pm(); k = ps_key(p)
                for q in (0, 1):
                    mm(p[:, q * 128:(q + 1) * 128], PP[:, q, 128:256], X[:, q, :], True, True, [kn("PP"), kn("X")], [k])
                tt(X, X, b3(p[:, 0:256], 2, 128), ALU.add, [kn("X"), k], [kn("X")])
                if lvl < 4:
                    p = pm(); k = ps_key(p)
                    for q in (0, 1):
                        mm(p[:, q * 256:q * 256 + 128], PP[:, q, 128:256], PP[:, q, 0:128], True, True, [kn("PP")], [k])
                        mm(p[:, q * 256 + 128:q * 256 + 256], PP[:, q, 0:128], PP[:, q, 128:256], True, True, [kn("PP")], [k])
                    act(PP, b3(p, 2, 256), AF.Copy, [k], [kn("PP")])
            yield
            p = pm(); k = ps_key(p)
            for q in (0, 1):
                mm(p[:, q * 128:(q + 1) * 128], AK[:, q, 0:128], Vt[:, q, :], True, True, [kn("AK"), kn("Vt")], [k])
            cp(KZ[:, :, 128:256], b3(p[:, 0:256], 2, 128), [k], [kn("KZ")])
            p = pm(); k = ps_key(p)
            for q in (0, 1):
                mm(p[:, q * 256:(q + 1) * 256], X[:, q, :], KZ[:, q, :], True, True, [kn("X"), kn("KZ")], [k])
            cp(QU, b3(p, 2, 256), [k], [kn("QU")], scale=-1.0)
            yield
            if main:
                p = pm(); k = ps_key(p)
                for q in (0, 1):
                    mm(p[:, q * 128:(q + 1) * 128], QU[:, q, 0:128], LA[:, q, 128:256], True, True, [kn("QU"), kn("LA")], [k])
                tt(Rst, b3(p[:, 0:256], 2, 128), KR[:, :, 128:256], ALU.add, [k, kn("KR")], [kn("Rst")])
            pY = None
            yield
            if kind == "p":
                p = pm(); k = ps_key(p)
                for q in (0, 1):
                    mm(p[:, q * 128:(q + 1) * 128], QU[:, q, 0:128], BBt[:, q, :], True, True, [kn("QU"), kn("BBt")], [k])
                cp(Mst, b3(p[:, 0:256], 2, 128), [k], [kn("Mst")])
                Hs, Hsb = b3(MT[kn("Hs")], 3, 128), b3(MT[kn("Hsb")], 3, 128)
                st_f = [Hrw[:, hp, :], Hs[:, 1, :]]
                st_b = [Hrwb[:, hp, :], Hsb[:, 1, :]]
                kf = [("Hrw", hp), (kn("Hs"), 1)]
                kb = [("Hrwb", hp), (kn("Hsb"), 1)]
                if main:
                    pY = pm(pin=True); kY = ps_key(pY)
                for q in (0, 1):
                    if main:
                        yo = pY[:, q * 128:(q + 1) * 128]
                        mm(yo, LA[:, q, 128:256], QU[:, q, 128:256], True, False, [kn("LA"), kn("QU")], [kY])
                        mm(yo, AK[:, q, 128:256], Vt[:, q, :], False, False, [kn("AK"), kn("Vt")], [kY])
                        mm(yo, Rst[:, q, :], st_b[q], False, True, [kn("Rst"), kb[q]], [kY])
                    p = pm(); k = ps_key(p)
                    mm(p[:, 0:128], BBt[:, q, :], QU[:, q, 128:256], True, False, [kn("BBt"), kn("QU")], [k])
                    mm(p[:, 0:128], KKt[:, q, :], Vt[:, q, :], False, False, [kn("KKt"), kn("Vt")], [k])
                    mm(p[:, 0:128], Mst[:, q, :], st_b[q], False, True, [kn("Mst"), kb[q]], [k])
                    wc = F["T3"][:, c0 + q * 64 + 63:c0 + q * 64 + 64]
                    stt(st_b[1 - q], st_f[q], wc, p[:, 0:128], ALU.mult, ALU.add, [kf[q], fk("T3"), k], [kb[1 - q]])
                    stt(st_f[1 - q], st_f[q], wc, p[:, 0:128], ALU.mult, ALU.add, [kf[q], fk("T3"), k], [kf[1 - q]])
            else:
                BBx, KKx, Rsx, H0b, H0f = (b3(MT[n], 8, 128) for n in ("BBx", "KKx", "Rsx", "H0b", "H0f"))
                kBBx, kKKx, kRsx, kH0b = ["KZ_B", "BBt_B", "KKt_B"], ["LA_B", "AK_B"], ["Lt_B", "X_B", "PP_B"], ["QU_B", "Mst_B", "Rst_B"]
                segT, segF = CT["segT"], b3(CT["segF"], 8, 128)
                pY = pm(pin=True); kY = ps_key(pY)
                for q in (0, 1):
                    n0 = (sub * 2 + q) * 8
                    ld(H0f[0:64, :, 0:64], rwH0[hp, 0:64, n0:n0 + 8, :], ["H0f"])
                    ld(H0f[64:128, :, 64:128], rwH0[hp, 64:128, n0:n0 + 8, :], ["H0f"])
                    cp(H0b, H0f, ["H0f"], kH0b)
                    tt(BBx, bc_mid(BBt[:, q, :], 8), bc_last(segT, 128), ALU.mult, [kn("BBt"), "cst"], kBBx)
                    tt(KKx, bc_mid(KKt[:, q, :], 8), bc_last(segT, 128), ALU.mult, [kn("KKt"), "cst"], kKKx)
                    tt(Rsx, bc_mid(Rst[:, q, :], 8), segF, ALU.mult, [kn("Rst"), "cst"], kRsx)
                    yo = pY[:, q * 128:(q + 1) * 128]
                    mm(yo, LA[:, q, 128:256], QU[:, q, 128:256], True, False, [kn("LA"), kn("QU")], [kY])
                    mm(yo, AK[:, q, 128:256], Vt[:, q, :], False, False, [kn("AK"), kn("Vt")], [kY])
                    for n in range(8):
                        mm(yo, Rsx[:, n, :], H0b[:, n, :], False, n == 7, kRsx + kH0b, [kY])
                    MnT8 = b3(MT["MnT"], 8, 128)
                    pM = [pm(pin=True), pm(pin=True)]
                    for n in range(8):
                        mm(pM[n // 4][:, (n % 4) * 128:(n % 4 + 1) * 128], QU[:, q, 0:128], BBx[:, n, :], True, True,
                           [kn("QU")] + kBBx, [ps_key(pM[n // 4])])
                    for hf in (0, 1):
                        cp(MnT8[:, hf * 4:hf * 4 + 4, :], b3(pM[hf], 4, 128), [ps_key(pM[hf])], [("MnT", hf)])
                        unpin(pM[hf])
                    pS = [pm(pin=True), pm(pin=True)]
                    for n in range(8):
                        po = pS[n // 4]; ko = ps_key(po)
                        oo = po[:, (n % 4) * 128:(n % 4 + 1) * 128]
                        mm(oo, BBx[:, n, :], QU[:, q, 128:256], True, False, kBBx + [kn("QU")], [ko])
                        mm(oo, KKx[:, n, :], Vt[:, q, :], False, False, kKKx + [kn("Vt")], [ko])
                        mm(oo, MnT8[:, n, :], H0b[:, n, :], False, True, [("MnT", n // 4)] + kH0b, [ko])
                    wseg = b3(F["T3"][:, c0 + q * 64:c0 + q * 64 + 64], 8, 8)[:, :, 7]
                    for hf in (0, 1):
                        hv = H0f[:, hf * 4:hf * 4 + 4, :]
                        tt(hv, hv, bc_last(wseg[:, hf * 4:hf * 4 + 4], 128), ALU.mult, ["H0f", fk("T3")] + kH0b, ["H0f"])
                        tt(hv, hv, b3(pS[hf], 4, 128), ALU.add, ["H0f", ps_key(pS[hf])], ["H0f"])
                    unpin(pS[0]); unpin(pS[1])
                    store(o_wkvs[hp, 0:64, n0:n0 + 8, :], H0f[0:64, :, 0:64], ["H0f"])
                    store(o_wkvs[hp, 64:128, n0:n0 + 8, :], H0f[64:128, :, 64:128], ["H0f"])
            yield "tail"
            if main:
                YN = b3(MT[kn("YN")], 2, 128)
                yv = b3(pY[:, 0:256], 2, 128)
                T.add("vector", lambda e: e.tensor_reduce(out=stat[:, 0:2], in_=yv, axis=AX.X, op=ALU.add), [kY], [kn("stat")])
                sqt = b3(MT[kn("sqt")], 2, 128)
                act(MT[kn("sqt")], pY[:, 0:256], AF.Square, [kY], [kn("sqt")])
                T.add("vector", lambda e: e.tensor_reduce(out=stat[:, 2:4], in_=sqt, axis=AX.X, op=ALU.add), [kn("sqt")], [kn("stat")])
                ts(stat[:, 4:6], stat[:, 0:2], 1.0 / 64, None, ALU.mult, None, [kn("stat")], [kn("stat")])
                tt(stat[:, 6:8], stat[:, 4:6], stat[:, 4:6], ALU.mult, [kn("stat")], [kn("stat")])
                stt(stat[:, 8:10], stat[:, 2:4], 1.0 / 64, stat[:, 6:8], ALU.mult, ALU.subtract, [kn("stat")], [kn("stat")])
                act(stat[:, 10:12], stat[:, 8:10], AF.Ln, [kn("stat"), "eps"], [kn("stat")], bias=epsg)
                act(stat[:, 12:14], stat[:, 10:12], AF.Exp, [kn("stat")], [kn("stat")], scale=-0.5)
                stt(stat[:, 14:16], stat[:, 4:6], -1.0, stat[:, 12:14], ALU.mult, ALU.mult, [kn("stat")], [kn("stat")])
                for h in (0, 1):
                    rows = slice(h * 64, h * 64 + 64)
                    cs = slice(h * 64, h * 64 + 64)
                    tt(YN[rows, :, cs], yv[rows, :, cs], bc_last(stat[rows, 12:14], 64), ALU.mult, [kY, kn("stat")], [kn("YN")])
                    tt(YN[rows, :, cs], YN[rows, :, cs], bc_last(stat[rows, 14:16], 64), ALU.add, [kn("YN"), kn("stat")], [kn("YN")])
                p = pm(); k = ps_key(p)
                for q in (0, 1):
                    tr(psb(p)[:, q * 128:(q + 1) * 128], YN[:, q, :], [kn("YN")], [k])
                pv = b3(psb(p)[:, 0:256], 2, 128)
                half = MT[kn("hf")].rearrange("p (q s) -> p q s", q=2)
                cp(half, pv[:, :, 0:64], [k], [kn("hf")])
                tt(F["db"][:, c0:c0 + 128].rearrange("p (q s) -> p q s", q=2), half, pv[:, :, 64:128], ALU.add,
                   [kn("hf"), k], [fk("db")])
                unpin(pY)


        def trn(out, in_, np_, r, w):
            return T.add("tensor", lambda e: e.transpose(out, in_, CT["ident"][0:np_, 0:np_]), r + ["cst"], w)

        gbase = [mats0]

        def galloc(n):
            o = gbase[0]
            gbase[0] += n
            assert gbase[0] <= mix_top
            return V(o, n)

        GK, GV, GA, GON, GSs, GS0, GKx, GQx, GST = (galloc(n) for n in (128, 256, 64, 512, 768, 2048, 512, 256, 64))
        GSET = {"": (GK, GV, GA, GON, GSs, GST, galloc(384)), "_B": tuple(galloc(n) for n in (128, 256, 64, 512, 768, 64, 384))}
        GS0b = galloc(1024)

        def gla_xgate(mode):
            wi = wload(w_gl[16])
            for grp in groups(mode):
                tc0, N, kind, _ = grp
                ps = proj_ps()
                proj(wi, tc0, N, ps)
                cp(xgT[:, tc0:tc0 + N], ps[:, 0:N], [ps_key(ps)], ["xgT"])

        gl_lock = {"busy": False}

        def gl_group(g, grp, main, W, fp, tag):
            fk = lambda n: n + "#%d" % fp
            FT = FTS[fp]
            tc0, N, kind, _ = grp
            L = 64 if kind == "p" else 8
            nseg = N // L
            F = {k: v[:, 0:N] for k, v in FT.items() if k != "pb"}
            F.update({fk(k): v for k, v in list(F.items())})
            gs = slice(g, g + 1)
            scanm = CT["scan_p" if kind == "p" else "scan_s"][:, 0:N]
            p = pm(); k = ps_key(p)
            mm(p[:, 0:N], gw2b[:, g * 128:(g + 1) * 128], xgT[:, tc0:tc0 + N], True, True, ["gw2b", "xgT"], [k])
            act(F["T0"], p[:, 0:N], AF.Exp, [k, "ngb"], [fk("T0")], bias=ngb[:, gs], scale=-1.0)
            act(F["T0"], F["T0"], AF.Ln, [fk("T0"), "eps"], [fk("T0")], bias=one1)
            T.add("vector", lambda e: e.tensor_tensor_scan(F["T1"], scanm, F["T0"], 0.0, ALU.mult, ALU.add),
                  ["cst", fk("T0")], [fk("T1")])
            act(F["T3"], F["T1"], AF.Exp, [fk("T1")], [fk("T3")], scale=-1.0 / 16)
            act(F["T4"], F["T1"], AF.Exp, [fk("T1")], [fk("T4")], scale=1.0 / 16)
            yield
            wi = W["k"]; ps = proj_ps(); proj(wi, tc0, N, ps)
            tt(F["T5"], ps[:, 0:N], F["T4"], ALU.mult, [ps_key(ps), fk("T4")], [fk("T5")])
            wend = bc_last(b3(F["T3"], nseg, L)[:, :, L - 1], L)
            tt(b3(F["T6"], nseg, L), b3(F["T5"], nseg, L), wend, ALU.mult, [fk("T5"), fk("T3")], [fk("T6")])
            for hf, nm in ((0, fk("mk")), (1, fk("mv"))):
                yield
                wi = W["v%d" % hf]; ps = proj_ps(); proj(wi, tc0, N, ps)
                cp(F[nm], ps[:, 0:N], [ps_key(ps)], [nm])
            yield
            if main:
                wi = W["q"]; ps = proj_ps(); proj(wi, tc0, N, ps)
                stt(F["T7"], ps[:, 0:N], 128.0 ** -0.5, F["T3"], ALU.mult, ALU.mult, [ps_key(ps), fk("T3")], [fk("T7")])
                cp(F["T0"].bitcast(BF16)[:, 0:N], F["T5"], [fk("T5")], [fk("T0")])
                cp(F["T1"].bitcast(BF16)[:, 0:N], F["T7"], [fk("T7")], [fk("T1")])
                for hf, nm, tn in ((0, fk("T8"), fk("T10")), (1, fk("T9"), fk("T11"))):
                    yield
                    wi = W["go%d" % hf]; ps = proj_ps(); proj(wi, tc0, N, ps)
                    cp(F[nm], ps[:, 0:N], [ps_key(ps)], [nm])
                    act(F[tn], F[nm], AF.Exp, [nm], [tn], scale=-1.0)
                    act(F[tn], F[tn], AF.Ln, [tn, "eps"], [tn], bias=one1)
                    act(F[tn], F[tn], AF.Exp, [tn], [tn], scale=-1.0)
                    tt(F[nm], F[nm], F[tn], ALU.mult, [nm, tn], [nm])
            proj_done.add(tag)
            yield
            while gl_lock["busy"]:
                yield
            gl_lock["busy"] = True
            def gl_sub(sub, sfx):
                kn = lambda n: n + sfx
                GK, GV, GA, GON, GSs, GST, GSsb = GSET[sfx]
                GKb, GVb, GAb = GK.bitcast(BF16), GV.bitcast(BF16), GA.bitcast(BF16)
                T5b, T7b = F["T0"].bitcast(BF16), F["T1"].bitcast(BF16)
                c0 = sub * 128
                GKv, GVv, GAv, ONv = b3(GKb, 2, 128), b3(GVb, 2, 256), b3(GAb, 2, 64), b3(GON, 2, 256)
                Ss, Ssb = b3(GSs, 3, 256), b3(GSsb.bitcast(BF16), 3, 256)
                p = pm(); k = ps_key(p)
                for q in (0, 1):
                    trn(p[0:64, q * 128:(q + 1) * 128], F["T6"][:, c0 + q * 64:c0 + q * 64 + 64], 128, [fk("T6")], [k])
                cp(GKb[0:64, :], p[0:64, 0:256], [k], [kn("GK")])
                p = pm(); k = ps_key(p)
                for q in (0, 1):
                    for hf, nm in ((0, fk("mk")), (1, fk("mv"))):
                        trn(p[0:64, q * 256 + hf * 128:q * 256 + hf * 128 + 128],
                            F[nm][:, c0 + q * 64:c0 + q * 64 + 64], 128, [nm], [k])
                cp(GVb[0:64, :], p[0:64, 0:512], [k], [kn("GV")])
                yield
                if main:
                    p = pm(); k = ps_key(p)
                    for q in (0, 1):
                        cs = slice(c0 + q * 64, c0 + q * 64 + 64)
                        mm(p[0:64, q * 64:(q + 1) * 64], T5b[:, cs], T7b[:, cs], True, True, [fk("T0"), fk("T1")], [k])
                    gm = CT["G_p" if kind == "p" else "G_s"]
                    tt(GAv[0:64], b3(p[0:64, 0:128], 2, 64), bc_mid(gm[0:64, :], 2), ALU.mult, [k, "cst"], [kn("GA")])
                    pO = pm(pin=True); kO = ps_key(pO)
                yield
                if kind == "p":
                    cp(Ss[:, 0, :], Sgl[:, g, :], ["Sgl"], [(kn("Ss"), 0)])
                    cp(Ssb[:, 0, :], Sgl[:, g, :], ["Sgl"], [(kn("Ssb"), 0)])
                    for q in (0, 1):
                        cs = slice(c0 + q * 64, c0 + q * 64 + 64)
                        if main:
                            oo = pO[0:64, q * 256:(q + 1) * 256]
                            mm(oo, GAv[0:64, q, :], GVv[0:64, q, :], True, False, [kn("GA"), kn("GV")], [kO])
                            mm(oo, T7b[:, cs], Ssb[:, q, :], False, True, [fk("T1"), (kn("Ssb"), q)], [kO])
                        p = pm(); k = ps_key(p)
                        mm(p[:, 0:256], GKv[0:64, q, :], GVv[0:64, q, :], True, True, [kn("GK"), kn("GV")], [k])
                        wc = F["T3"][:, c0 + q * 64 + 63:c0 + q * 64 + 64]
                        stt(Ssb[:, q + 1, :], Ss[:, q, :], wc, p[:, 0:256], ALU.mult, ALU.add,
                            [(kn("Ss"), q), fk("T3"), k], [(kn("Ssb"), q + 1)])
                        stt(Ss[:, q + 1, :], Ss[:, q, :], wc, p[:, 0:256], ALU.mult, ALU.add,
                            [(kn("Ss"), q), fk("T3"), k], [(kn("Ss"), q + 1)])
                    cp(Sgl[:, g, :], Ss[:, 2, :], [(kn("Ss"), 2)], ["Sgl"])
                else:
                    S0v, Kxv, Qxv = b3(GS0, 8, 256), b3(GKx.bitcast(BF16), 8, 128), b3(GQx.bitcast(BF16), 8, 64)
                    S0bv = b3(GS0b.bitcast(BF16), 8, 256)
                    for q in (0, 1):
                        cs = slice(c0 + q * 64, c0 + q * 64 + 64)
                        n0 = (sub * 2 + q) * 8
                        ld(S0v, glS0[g, :, n0:n0 + 8, :], ["GS0"])
                        cp(S0bv, S0v, ["GS0"], ["GS0b"])
                        tt(Qxv, bc_mid(F["T7"][:, cs], 8), b3(CT["gsegF"], 8, 64), ALU.mult, [fk("T7"), "cst"], ["GQx"])
                        tt(Kxv[0:64], bc_mid(GKv[0:64, q, :], 8), bc_last(CT["segT"][0:64, :], 128), ALU.mult,
                           [kn("GK"), "cst"], ["GKx"])
                        oo = pO[0:64, q * 256:(q + 1) * 256]
                        mm(oo, GAv[0:64, q, :], GVv[0:64, q, :], True, False, [kn("GA"), kn("GV")], [kO])
                        for n in range(8):
                            mm(oo, Qxv[:, n, :], S0bv[:, n, :], False, n == 7, ["GQx", "GS0b"], [kO])
                        wseg = b3(F["T3"][:, cs], 8, 8)[:, :, 7]
                        pss = [pm(pin=True) for _ in range(4)]
                        for n in range(8):
                            po = pss[n // 2]
                            mm(po[:, (n % 2) * 256:(n % 2 + 1) * 256], Kxv[0:64, n, :], GVv[0:64, q, :], True, True,
                               ["GKx", kn("GV")], [ps_key(po)])
                        for j in range(4):
                            sv = S0v[:, 2 * j:2 * j + 2, :]
                            tt(sv, sv, bc_last(wseg[:, 2 * j:2 * j + 2], 256), ALU.mult, ["GS0", fk("T3"), kO], ["GS0"])
                            tt(sv, sv, b3(pss[j], 2, 256), ALU.add, ["GS0", ps_key(pss[j])], ["GS0"])
                        for po in pss:
                            unpin(po)
                        store(o_glas[g, :, n0:n0 + 8, :], S0v, ["GS0"])
                yield
                if main:
                    stat = GST
                    act(GON[0:64, :], pO[0:64, 0:512], AF.Square, [kO], [kn("GON")])
                    T.add("vector", lambda e: e.tensor_reduce(out=stat[0:64, 0:2], in_=ONv[0:64], axis=AX.X, op=ALU.add),
                          [kn("GON")], [kn("stat")])
                    act(stat[0:64, 2:4], stat[0:64, 0:2], AF.Ln, [kn("stat"), "eps"], [kn("stat")], bias=epsh[0:64], scale=1.0 / 256)
                    act(stat[0:64, 4:6], stat[0:64, 2:4], AF.Exp, [kn("stat")], [kn("stat")], scale=-0.5)
                    tt(ONv[0:64], b3(pO[0:64, 0:512], 2, 256), bc_last(stat[0:64, 4:6], 256), ALU.mult, [kO, kn("stat"), kn("GON")], [kn("GON")])
                    p = pm(); k = ps_key(p)
                    for q in (0, 1):
                        for hf in (0, 1):
                            trn(p[:, hf * 128 + q * 64:hf * 128 + q * 64 + 64], ONv[0:64, q, hf * 128:(hf + 1) * 128], 64,
                                [kn("GON")], [k])
                    for hf, nm in ((0, fk("T8")), (1, fk("T9"))):
                        stt(oT[:, 8 + 2 * g + hf, tc0 + c0:tc0 + c0 + 128], p[:, hf * 128:(hf + 1) * 128],
                            glvt[:, 4 + hf:5 + hf], F[nm][:, c0:c0 + 128], ALU.mult, ALU.mult,
                            [k, "glv", nm], [("oT", 8 + 2 * g + hf)])
                    unpin(pO)
                yield
            gens = [gl_sub(sub, ("", "_B")[sub]) for sub in range(N // 128)]
            while gens:
                for gsub in list(gens):
                    try:
                        next(gsub)
                    except StopIteration:
                        gens.remove(gsub)
                yield
            gl_lock["busy"] = False
            if main and grp[3]:
                store(o_glap[g, :, :], Sgl[:, g, :], ["Sgl"])

        def zero_blk():
            for nm in ("KR", "Bf", "Cf", "BBf", "KKf", "Vf", "H0f", "KR_B", "Bf_B", "Cf_B", "BBf_B", "KKf_B", "Vf_B", "YN", "YN_B"):
                T.add("vector", lambda e, m=MT[nm]: e.memset(m, 0.0), (), [nm])

        for mode in ("P", "M"):
            main = mode == "M"
            norm_tokens(xM if main else xP, NTM if main else PRE, 0, hT, "hT", base=top[0] - 6656)
            T.barrier()
            zero_blk()
            if main:
                ldc(l2b, lora2[:, :, :], ["l2b"])
            lora_stage(mode)
            seq = [(hp, gi, grp) for hp in range(8) for gi, grp in enumerate(groups(mode))]
            wts, active, nxt = {}, [], 0

            def start(j, main=main, seq=seq, wts=wts):
                hp, gi, grp = seq[j]
                if gi == 0:
                    wk, wv = wload_rw(w_rw[8 + hp]), wload_rw(w_rw[16 + hp])
                    if main:
                        wr = wload_rw(w_rw[hp])
                    else:
                        wr = None
                        wr0 = wload_rw(w_rw[hp])
                        ps = proj_ps()
                        proj(wr0, PRE - 64, 64, ps)
                        cp(carry[:, hp:hp + 1], ps[:, 63:64], [ps_key(ps)], ["carry"])
                    wts[hp] = (wk, wv, wr)
                return rw_group(hp, grp, main, *wts[hp], j % 2, ("rw", mode, j))

            while nxt < len(seq) or active:
                while len(active) < 2 and nxt < len(seq):
                    if seq[nxt][1] == 0 and any(t not in proj_done for _, t in active):
                        break
                    active.append((start(nxt), ("rw", mode, nxt)))
                    nxt += 1
                for gg in list(active):
                    try:
                        next(gg[0])
                    except StopIteration:
                        active.remove(gg)
            T.barrier()
            gla_xgate(mode)
            gseq = [(g, gi, grp) for g in range(4) for gi, grp in enumerate(groups(mode))]
            gw, gact, gnx = {}, [], 0

            def gstart(j, main=main, gseq=gseq, gw=gw):
                g, gi, grp = gseq[j]
                if gi == 0:
                    W = {}
                    srcs = [("k", 4 + g), ("v0", 8 + 2 * g), ("v1", 9 + 2 * g)]
                    if main:
                        srcs += [("q", g), ("go0", 17 + 2 * g), ("go1", 18 + 2 * g)]
                    base = 0 if main else (g % 2) * 3
                    for slot, (nm, ci) in enumerate(srcs):
                        ldc(wst[base + slot], w_gl[ci], [("w", base + slot)])
                        W[nm] = base + slot
                    gw[g] = W
                return gl_group(g, grp, main, gw[g], j % 2, ("gl", mode, j))

            while gnx < len(gseq) or gact:
                while len(gact) < 2 and gnx < len(gseq):
                    if main and gseq[gnx][1] == 0 and any(t not in proj_done for _, t in gact):
                        break
                    gact.append((gstart(gnx), ("gl", mode, gnx)))
                    gnx += 1
                for gg in list(gact):
                    try:
                        next(gg[0])
                    except StopIteration:
                        gact.remove(gg)
            T.barrier()
        store(o_shift[:, :, :], shout, ["shout"])

        X1 = V(ov0, 18432).rearrange("p (k t) -> p k t", k=KB)
        rtmp = V(ov0 + 18432 + 2560, 512)
        ttiles = [(0, 512), (512, 512), (1024, 128)]
        for fc in range(KB):
            wi = wload(w_o[fc])
            ld(X1[:, fc, :], xM[:, fc, :], [("X1", fc)])
            for t0, tw in ttiles:
                ps = proj_ps(); k = ps_key(ps)
                for kk in range(KB):
                    mm(ps[:, 0:tw], wst[wi][:, kk, :], oT[:, kk, t0:t0 + tw], kk == 0, kk == KB - 1,
                       [("w", wi), ("oT", kk)], [k])
                tt(X1[:, fc, t0:t0 + tw], X1[:, fc, t0:t0 + tw], ps[:, 0:tw], ALU.add, [("X1", fc), k], [("X1", fc)])
        x1k = [("X1", fc) for fc in range(KB)]
        norm_tokens((X1, x1k), NTM, 1, hT, "hT", base=ov0 + 18432)
        uT = oT
        for g4 in range(4):
            for fc in range(KB):
                wi = wload(w_u[g4 * KB + fc])
                for t0, tw in ttiles:
                    ps = proj_ps(); k = ps_key(ps)
                    for kk in range(KB):
                        mm(ps[:, 0:tw], wst[wi][:, kk, :], hT[:, kk, t0:t0 + tw], kk == 0, kk == KB - 1,
                           [("w", wi), "hT"], [k])
                    act(rtmp[:, 0:tw], ps[:, 0:tw], AF.Relu, [k], ["rtmp"])
                    tt(uT[:, fc, t0:t0 + tw], rtmp[:, 0:tw], rtmp[:, 0:tw], ALU.mult, ["rtmp"], [("oT", fc)])
            for fc in range(KB):
                wi = wload(w_d[g4, fc])
                for t0, tw in ttiles:
                    ps = proj_ps(); k = ps_key(ps)
                    for kk in range(KB):
                        mm(ps[:, 0:tw], wst[wi][:, kk, :], uT[:, kk, t0:t0 + tw], kk == 0, kk == KB - 1,
                           [("w", wi), ("oT", kk)], [k])
                    tt(X1[:, fc, t0:t0 + tw], X1[:, fc, t0:t0 + tw], ps[:, 0:tw], ALU.add, [("X1", fc), k], [("X1", fc)])
        fb = ov0 + 18432
        lnF, rsF = V(fb, NTM), V(fb + NTM, NTM)
        for k in range(KB):
            act(hT[:, k, :], X1[:, k, :], AF.Square, [("X1", k)], ["hT"])
        for ti, (t0, tw) in enumerate(ttiles):
            pb_ = PS[2 + ti]
            for k in range(KB):
                mm(pb_[:, 0:tw], onesb, hT[:, k, t0:t0 + tw], k == 0, k == KB - 1, ["onesb", "hT"], [ps_key(pb_)])
            act(lnF[:, t0:t0 + tw], pb_[:, 0:tw], AF.Ln, [ps_key(pb_), "eps"], ["lnF"], bias=epsn, scale=1.0 / D)
            act(rsF[:, t0:t0 + tw], lnF[:, t0:t0 + tw], AF.Exp, ["lnF"], ["rsF"], scale=-0.5)
        for k in range(KB):
            stt(X1[:, k, :], X1[:, k, :], gv[:, 2, k:k + 1], rsF, ALU.mult, ALU.mult, [("X1", k), "gv", "rsF"], [("X1", k)])
            store(yT[:, k, :], X1[:, k, :], [("X1", k)])
        T.emit(out_ids)
    return nc


def _fm(a2d):
    t = a2d.shape[0]
    return np.ascontiguousarray(a2d.T.reshape(KB, 128, t).transpose(1, 0, 2))


def _wt(w, nch):
    return np.ascontiguousarray(w.reshape(KB, 128, nch, 128).transpose(2, 1, 0, 3))


def _pad_cols(a, segs, axis=-1):
    out = []
    for s0, n in segs:
        piece = np.take(a, np.arange(s0, s0 + n), axis=axis)
        if n < 128:
            padw = [(0, 0)] * a.ndim
            padw[axis] = (0, 128 - n)
            piece = np.pad(piece, padw)
        out.append(piece)
    return np.concatenate(out, axis=axis)


RW_SEGS = [(i * 128, 128) for i in range(24)] + [(3072, 64), (3136, 64), (3200, 128), (3328, 32)]
RW_UNPAD = np.concatenate([np.arange(c * 128, c * 128 + n) for c, (s0, n) in enumerate(RW_SEGS)])
GL0 = RW_PROJ
GL_SEGS = [(GL0 + i * 128, 128) for i in range(16)] + [(GL0 + 2048, 16)] + [(GL0 + 2064 + i * 128, 128) for i in range(8)]


def kernel(x_prompt, x_sample, state_rwkv_shift, state_rwkv_wkv, state_gla, norm1_g, w_in, rw_mu,
           rw_w0, rw_w2, rw_a0, rw_a2, rw_g2, rw_k_k, rw_k_a, rw_r_k, rw_ln_w, rw_ln_b, gla_gw2,
           gla_gb, gla_norm_w, w_out, norm2_g, w_up, w_down, norm_f_g):
    f32 = np.float32
    A = lambda z: np.asarray(z, f32)
    x_prompt, x_sample = A(x_prompt), A(x_sample)
    w_in0 = A(w_in)[0]
    w_rw = _wt(_pad_cols(w_in0, RW_SEGS), NRW)
    w_gl = _wt(_pad_cols(w_in0, GL_SEGS), NGL)
    mu = np.ascontiguousarray(_pad_cols(A(rw_mu)[0], RW_SEGS).reshape(NRW, 128).T)
    vecT = lambda v: np.ascontiguousarray(A(v).reshape(-1, 128).T)
    gvec = np.ascontiguousarray(np.stack([vecT(A(norm1_g)[0]), vecT(A(norm2_g)[0]), vecT(A(norm_f_g))], 1))
    rwv = np.ascontiguousarray(np.stack([vecT(A(v)[0]) for v in (rw_w0, rw_a0, rw_k_k, rw_k_a, rw_r_k, rw_ln_w, rw_ln_b)], 1))
    lora2 = np.zeros((128, 4, 1024), f32)
    lora2[0:64, 0], lora2[0:64, 1] = A(rw_w2)[0], A(rw_a2)[0]
    lora2[0:128, 2], lora2[0:32, 3] = A(rw_g2)[0][0:128], A(rw_g2)[0][128:160]
    glv = np.ascontiguousarray(np.concatenate([vecT(A(gla_gb)[0]), vecT(A(gla_norm_w)[0])], 1))
    gw2 = np.zeros((128, 512), f32)
    gw2[0:16] = A(gla_gw2)[0]
    w_o = _wt(A(w_out)[0], 16)
    w_u = _wt(A(w_up)[0], 64)
    w_d = np.ascontiguousarray(A(w_down)[0].reshape(4, KB, 128, 16, 128).transpose(0, 3, 2, 1, 4))
    sh = A(state_rwkv_shift)[0]
    wkv = A(state_rwkv_wkv)[0]
    gls = A(state_gla)[0]
    shared = {"cst": CST, "gvec": gvec, "w_rw": w_rw, "mu_rw": mu, "rwv": rwv, "lora2": lora2, "w_gl": w_gl,
              "glv": glv, "gw2": gw2, "w_o": w_o, "w_u": w_u, "w_d": w_d}
    in_maps = []
    for c in range(N_CORES):
        b, par = divmod(c, 2)
        own = x_prompt[b, par * HALF:(par + 1) * HALF]
        pre = x_prompt[b, 0:PRE] if par == 1 else np.zeros((PRE, D), f32)
        smp = x_sample[NS * c:NS * (c + 1)].reshape(SMP, D)
        ns = slice(NS * c, NS * (c + 1))
        shT = np.ascontiguousarray(_pad_cols(sh[ns], RW_SEGS).reshape(NS, NRW, 128).transpose(2, 1, 0))
        h0 = np.ascontiguousarray(wkv[ns].transpose(1, 3, 0, 2).reshape(8, 128, NS, 64))
        s0 = np.ascontiguousarray(gls[ns].transpose(1, 2, 0, 3))
        m = dict(shared)
        m.update({"xP": _fm(pre), "xM": _fm(np.concatenate([own, smp], 0)), "shiftT": shT, "rwH0": h0, "glS0": s0})
        in_maps.append(m)
    nc = build_nc()
    res = run_bass_kernel_spmd(nc, in_maps, core_ids=list(range(N_CORES)))

    B = x_prompt.shape[0]
    y_p = np.zeros((B, SEQ, D), f32)
    y_s = np.zeros((DEC_B, DEC_T, D), f32)
    sh_p = np.zeros((1, B, RW_PROJ), f32)
    sh_s = np.zeros((1, DEC_B, RW_PROJ), f32)
    wkv_p = np.zeros((1, B, RW_H, RW_HD, RW_HD), f32)
    wkv_s = np.zeros((1, DEC_B, RW_H, RW_HD, RW_HD), f32)
    gla_p = np.zeros((1, B, 4, 128, 256), f32)
    gla_s = np.zeros((1, DEC_B, 4, 128, 256), f32)
    for c in range(N_CORES):
        b, par = divmod(c, 2)
        r = res.results[c]
        ns = slice(NS * c, NS * (c + 1))
        y = r["yT"].transpose(2, 1, 0).reshape(NTM, D)
        y_p[b, par * HALF:(par + 1) * HALF] = y[:HALF]
        y_s[ns] = y[HALF:].reshape(NS, DEC_T, D)
        rows = r["o_shift"].transpose(2, 1, 0).reshape(1 + NS, NRW * 128)[:, RW_UNPAD]
        sh_s[0, ns] = rows[1:]
        wkv_s[0, ns] = r["o_wkvs"].reshape(8, 2, 64, NS, 64).transpose(3, 0, 1, 4, 2).reshape(NS, 16, 64, 64)
        gla_s[0, ns] = r["o_glas"].transpose(2, 0, 1, 3)
        if par == 1:
            sh_p[0, b] = rows[0]
            wkv_p[0, b] = r["o_wkvp"].reshape(16, 64, 64).transpose(0, 2, 1)
            gla_p[0, b] = r["o_glap"]
    return (y_p, y_s, sh_p, wkv_p, gla_p, sh_s, wkv_s, gla_s)
```

```python
from contextlib import ExitStack

import numpy as np
import concourse.bass as bass
import concourse.mybir as mybir
from concourse.bass_utils import run_bass_kernel_spmd

F32 = mybir.dt.float32
BF16 = mybir.dt.bfloat16
ALU = mybir.AluOpType
AF = mybir.ActivationFunctionType

N_CORES = 8
D = 2048
KB = D // 128
SEQ = 2048
HALF = SEQ // 2
PRE = HALF
DEC_B, DEC_T = 128, 8
SMP = (DEC_B // N_CORES) * DEC_T
RW_W, RW_HD, RW_H = 1024, 64, 16
RW_PROJ = 3 * RW_W + 64 + 64 + 160
NORM_EPS = 1e-6

TOK = PRE + HALF + SMP
TT = 512


AX = mybir.AxisListType
NS = SMP // DEC_T
NTM = HALF + SMP
QT = 256
C0 = 0.6065306597126334
GN_EPS = 64e-5
HN_EPS = 1e-5
NRW = 28
NGL = 25


class Tracker:
    CE = ("tensor", "scalar", "vector")

    def __init__(self, nc, es):
        self.nc, self.ops, self.lw, self.rs = nc, [], {}, {}
        self.sem = {e: es.enter_context(nc.semaphore("c_" + e)) for e in self.CE}
        self.pool = {"sync": [es.enter_context(nc.semaphore(f"ds{i}")) for i in range(16)],
                     "gpsimd": [es.enter_context(nc.semaphore(f"dg{i}")) for i in range(12)]}
        self.bar = {}
        self.seen_after_bar = set()

    def barrier(self):
        last = {}
        for i, o in enumerate(self.ops):
            last[o["eng"]] = i
        self.bar = dict(last)
        self.bar_dma = []
        for q in ("sync", "gpsimd"):
            ids = [i for i, o in enumerate(self.ops) if o["eng"] == q and o["make"] is not None]
            self.bar_dma += ids[-len(self.pool[q]):]
        self.seen_after_bar = set()

    def add(self, eng, make, r=(), w=()):
        i = len(self.ops)
        raw, oth = set(), set()
        for k in r:
            p = self.lw.get(k)
            if p is not None:
                raw.add(p)
        for k in w:
            p = self.lw.get(k)
            if p is not None:
                oth.add(p)
            for q in self.rs.get(k, {}).values():
                oth.add(q)
        deps = set()
        for p in raw | oth:
            pe = self.ops[p]["eng"]
            if pe == eng:
                if eng == "tensor":
                    continue
                if eng in self.CE and p not in raw:
                    continue
            deps.add(p)
        if self.bar and eng not in self.seen_after_bar:
            self.seen_after_bar.add(eng)
            deps |= set(v for e, v in self.bar.items() if e != eng or eng not in self.CE)
            deps |= set(self.bar_dma)
        self.ops.append(dict(eng=eng, make=make, deps=deps, sig=False))
        for k in r:
            d = self.rs.setdefault(k, {})
            d[eng if eng in self.CE else ("dma", i)] = i
        for k in w:
            self.lw[k] = i
            self.rs[k] = {}
        return i

    def emit(self, out_ids):
        ops = self.ops
        fence = self.add("sync", None)
        ops[fence]["deps"] = set(out_ids)
        for o in ops:
            for p in o["deps"]:
                ops[p]["sig"] = True
        cnt = {e: 0 for e in self.CE}
        ndma = {q: 0 for q in self.pool}
        pcnt = {q: [0] * len(self.pool[q]) for q in self.pool}
        for o in ops:
            e = o["eng"]
            o["sv"] = None
            if o["make"] is None:
                continue
            if e in self.CE:
                if o["sig"]:
                    cnt[e] += 1
                    o["sv"] = (self.sem[e], cnt[e], 1)
            else:
                j = ndma[e] % len(self.pool[e])
                ndma[e] += 1
                pcnt[e][j] += 16
                o["sv"] = (self.pool[e][j], pcnt[e][j], 16)
        streams = {}
        for i, o in enumerate(ops):
            streams.setdefault(o["eng"], []).append(i)
        waited = {}
        self.nwaits = 0
        with self.nc.Block() as block:
            for e, ids in streams.items():
                def sec(eng, ids=ids, e=e):
                    for i in ids:
                        o = ops[i]
                        need = {}
                        for p in o["deps"]:
                            sem, val, _ = ops[p]["sv"]
                            k = id(sem)
                            if k not in need or need[k][1] < val:
                                need[k] = (sem, val)
                        for k, (sem, val) in need.items():
                            if waited.get((e, k), 0) < val:
                                eng.wait_ge(sem, val)
                                waited[(e, k)] = val
                                self.nwaits += 1
                        if o["make"] is not None:
                            ins = o["make"](eng)
                            if o["sv"] is not None:
                                ins.then_inc(o["sv"][0], o["sv"][2])
                getattr(block, e)(sec)


def host_consts():
    r = np.arange(128)
    h, s = r // 64, r % 64
    same_h = h[:, None] == h[None, :]
    sU = same_h & (s[:, None] < s[None, :])
    iU = same_h & (s[:, None] <= s[None, :])
    sL = same_h & (s[:, None] > s[None, :])
    seg = (s[:, None] // 8) == (s[None, :] // 8)
    f = lambda m: m.astype(np.float32)
    tabs = {}
    tabs["ident"] = np.eye(128, dtype=np.float32)
    tabs["bones"] = f(same_h)
    tabs["UU_p"] = np.concatenate([f(sU), f(iU)], 1)
    tabs["UU_s"] = np.concatenate([f(sU & seg), f(iU & seg)], 1)
    tabs["sL_p"] = f(sL)
    tabs["sL_s"] = f(sL & seg)
    t = np.arange(QT)
    tabs["scan_p"] = np.tile(f(t % 64 != 0)[None, :], (128, 1))
    tabs["scan_s"] = np.tile(f(t % 8 != 0)[None, :], (128, 1))
    n = np.arange(8)
    segF = f((s[None, None, :] // 8) == n[None, :, None])
    tabs["segF"] = np.tile(segF.reshape(1, 8 * 128), (128, 1))
    tabs["segT"] = f((s[:, None] // 8) == n[None, :])
    s6 = np.arange(128) % 64
    c6 = np.arange(64)
    gi = f(s6[:, None] <= c6[None, :])
    gseg = f((s6[:, None] // 8) == (c6[None, :] // 8))
    tabs["G_p"] = gi
    tabs["G_s"] = gi * gseg
    tabs["gsegF"] = np.tile(f((c6[None, None, :] // 8) == n[None, :, None]).reshape(1, 8 * 64), (128, 1))
    offs, cols, o = {}, [], 0
    for k, v in tabs.items():
        offs[k] = (o, v.shape[1])
        cols.append(v)
        o += v.shape[1]
    return np.ascontiguousarray(np.concatenate(cols, 1)), offs


CST, CST_OFF = host_consts()
NCST = CST.shape[1]


def build_nc():
    nc = bass.Bass("TRN2", target_bir_lowering=False)
    di = lambda n, sh: nc.dram_tensor(n, sh, F32, kind="ExternalInput").ap()
    do = lambda n, sh: nc.dram_tensor(n, sh, F32, kind="ExternalOutput").ap()
    xP, xM = di("xP", [128, KB, PRE]), di("xM", [128, KB, NTM])
    cst = di("cst", [128, NCST])
    gvec = di("gvec", [128, 3, KB])
    w_rw, mu_rw, shiftT = di("w_rw", [NRW, 128, KB, 128]), di("mu_rw", [128, NRW]), di("shiftT", [128, NRW, NS])
    rwv = di("rwv", [128, 7, 8])
    lora2 = di("lora2", [128, 4, 1024])
    rwH0 = di("rwH0", [8, 128, NS, 64])
    w_gl, glv, gw2 = di("w_gl", [NGL, 128, KB, 128]), di("glv", [128, 6]), di("gw2", [128, 512])
    glS0 = di("glS0", [4, 128, NS, 256])
    w_o, w_u, w_d = di("w_o", [16, 128, KB, 128]), di("w_u", [64, 128, KB, 128]), di("w_d", [4, 16, 128, KB, 128])
    yT = do("yT", [128, KB, NTM])
    o_shift = do("o_shift", [128, NRW, 1 + NS])
    o_wkvp, o_wkvs = do("o_wkvp", [8, 128, 64]), do("o_wkvs", [8, 128, NS, 64])
    o_glap, o_glas = do("o_glap", [4, 128, 256]), do("o_glas", [4, 128, NS, 256])

    with ExitStack() as es:
        AR = es.enter_context(nc.sbuf_tensor("arena", [128, 53000], F32))
        PS = [es.enter_context(nc.psum_tensor(f"ps{i}", [128, 512], F32)) for i in range(8)]
        T = Tracker(nc, es)
        top = [0]

        def alloc(n):
            o = top[0]
            top[0] += n
            assert top[0] <= 53000, top[0]
            return o

        def V(o, n, rows=None):
            return AR[:, o:o + n] if rows is None else AR[rows[0]:rows[1], o:o + n]

        o_hT, o_oT = alloc(9216), alloc(9216)
        hT = V(o_hT, 9216).bitcast(BF16).rearrange("p (k t) -> p k t", k=KB)
        oT = V(o_oT, 9216).bitcast(BF16).rearrange("p (k t) -> p k t", k=KB)
        NW = 4
        o_w = alloc(1024 * NW)
        wst = [V(o_w + 1024 * i, 1024).bitcast(BF16).rearrange("p (k f) -> p k f", k=KB) for i in range(NW)]
        o_c = alloc(NCST)
        CT = {k: V(o_c + a, n) for k, (a, n) in CST_OFF.items()}
        gv = V(alloc(48), 48).rearrange("p (a k) -> p a k", a=3)
        mut = V(alloc(NRW), NRW)
        rv = V(alloc(56), 56).rearrange("p (a k) -> p a k", a=7)
        shin = V(alloc(NRW * NS), NRW * NS).rearrange("p (c n) -> p c n", c=NRW)
        shout = V(alloc(NRW * (1 + NS)), NRW * (1 + NS)).rearrange("p (c n) -> p c n", c=NRW)
        carry = V(alloc(NRW), NRW)
        glvt = V(alloc(6), 6)
        nw0, na0, omka, ngb = V(alloc(8), 8), V(alloc(8), 8), V(alloc(8), 8), V(alloc(4), 4)
        epsn, eps24, epsg, epsh, one1 = (V(alloc(1), 1) for _ in range(5))
        gw2b = V(alloc(256), 256).bitcast(BF16)
        o_l2b = alloc(2048)
        l2b = V(o_l2b, 2048).bitcast(BF16).rearrange("p (a f) -> p a f", a=4)
        wst += [V(o_l2b + 1024 * i, 1024).bitcast(BF16).rearrange("p (k f) -> p k f", k=KB) for i in range(2)]
        onesb = V(alloc(64), 64).bitcast(BF16)
        Hrw = V(alloc(1024), 1024).rearrange("p (h c) -> p h c", h=8)
        Sgl = V(alloc(1024), 1024).rearrange("p (g e) -> p g e", g=4)
        ov0 = top[0]

        st = {"eng": 0, "w": 0, "wr": 0}

        def cp(out, in_, r, w, scale=None):
            st["eng"] ^= 1
            if st["eng"]:
                if scale is None:
                    return T.add("scalar", lambda e: e.activation(out=out, in_=in_, func=AF.Copy), r, w)
                return T.add("scalar", lambda e: e.activation(out=out, in_=in_, func=AF.Copy, scale=scale), r, w)
            if scale is None:
                return T.add("vector", lambda e: e.tensor_copy(out, in_), r, w)
            return T.add("vector", lambda e: e.tensor_scalar(out, in_, scale, None, ALU.mult), r, w)

        def act(out, in_, func, r, w, bias=None, scale=1.0, accum=None):
            kw = dict(out=out, in_=in_, func=func, scale=scale)
            if bias is not None:
                kw["bias"] = bias
            if accum is not None:
                kw["accum_out"] = accum
            return T.add("scalar", lambda e: e.activation(**kw), r, w)

        def tt(out, a, b, op, r, w):
            return T.add("vector", lambda e: e.tensor_tensor(out, a, b, op), r, w)

        def ts(out, a, s1, s2, op0, op1, r, w):
            if s2 is None:
                return T.add("vector", lambda e: e.tensor_scalar(out, a, s1, None, op0), r, w)
            return T.add("vector", lambda e: e.tensor_scalar(out, a, s1, s2, op0, op1), r, w)

        def stt(out, a, sc, b, op0, op1, r, w):
            return T.add("vector", lambda e: e.scalar_tensor_tensor(out=out, in0=a, scalar=sc, in1=b, op0=op0, op1=op1), r, w)

        def mm(out, lhsT, rhs, start, stop, r, w):
            return T.add("tensor", lambda e: e.matmul(out, lhsT, rhs, start=start, stop=stop), r, w)

        def tr(out, in_, r, w):
            return T.add("tensor", lambda e: e.transpose(out, in_, identb), r + ["identb"], w)

        def ld(out, in_, w, r=()):
            return T.add("sync", lambda e: e.dma_start(out=out, in_=in_), r, w)

        def ldc(out, in_, w, r=()):
            return T.add("gpsimd", lambda e: e.dma_start(out=out, in_=in_), r, w)

        out_ids = []

        def store(out, in_, r):
            out_ids.append(T.add("sync", lambda e: e.dma_start(out=out, in_=in_), r, ()))

        ld(V(o_c, NCST), cst[:, :], ["cst"])
        ld(gv, gvec[:, :, :], ["gv"])
        ld(mut, mu_rw[:, :], ["mut"])
        ld(rv, rwv[:, :, :], ["rv"])
        ld(shin, shiftT[:, :, :], ["shin"])
        ld(glvt, glv[:, :], ["glv"])
        ldc(gw2b, gw2[:, :], ["gw2b"])
        ldc(l2b, lora2[:, :, :], ["l2b"])
        T.add("vector", lambda e: e.memset(onesb, 1.0), (), ["onesb"])
        for tl, val in ((epsn, NORM_EPS), (eps24, 1e-24), (epsg, GN_EPS), (epsh, HN_EPS), (one1, 1.0)):
            T.add("vector", lambda e, tl=tl, val=val: e.memset(tl, val), (), ["eps"])
        T.add("vector", lambda e: e.memset(carry, 0.0), (), ["carry"])
        T.add("vector", lambda e: e.memset(Hrw, 0.0), (), [("Hrw", h) for h in range(8)])
        T.add("vector", lambda e: e.memset(Sgl, 0.0), (), ["Sgl"])
        T.add("vector", lambda e: e.memset(shout, 0.0), (), ["shout"])
        ts(nw0, rv[:, 0, :], -1.0, None, ALU.mult, None, ["rv"], ["nw0"])
        ts(na0, rv[:, 1, :], -1.0, None, ALU.mult, None, ["rv"], ["na0"])
        ts(omka, rv[:, 3, :], -1.0, 1.0, ALU.mult, ALU.add, ["rv"], ["omka"])
        ts(ngb, glvt[:, 0:4], -1.0, None, ALU.mult, None, ["glv"], ["ngb"])

        lorT = V(alloc(2304), 2304).bitcast(BF16).rearrange("p (a t) -> p a t", a=4)
        xgT = V(alloc(576), 576).bitcast(BF16)
        identb = V(alloc(64), 64).bitcast(BF16)
        Hrwb = V(alloc(512), 512).bitcast(BF16).rearrange("p (h c) -> p h c", h=8)
        T.add("vector", lambda e: e.memset(Hrwb, 0.0), (), [("Hrwb", h) for h in range(8)])
        FTS = []
        for _fp in (0, 1):
            d = {"pb": V(alloc(260), 260)}
            for nm in ("db", "mr", "mk", "mv", "T0", "T1", "T3", "T4", "T5", "T6", "T7", "T8", "T9", "T10", "T11"):
                d[nm] = V(alloc(QT), QT)
            FTS.append(d)
        FT = FTS[0]
        MT, MTO = {}, {}
        mats0 = top[0]

        def mat(nm, n, dt=BF16):
            if dt == BF16:
                MTO[nm] = alloc(n // 2)
                MT[nm] = V(MTO[nm], n // 2).bitcast(BF16)
            else:
                MTO[nm] = alloc(n)
                MT[nm] = V(MTO[nm], n)
            return MT[nm]

        def psb(p):
            return p.bitcast(BF16)

        for nm, n in (("KR", 512), ("Bf", 256), ("Cf", 256), ("BBf", 256), ("KKf", 256), ("Vf", 256), ("YN", 256), ("YN_B", 256)):
            m = mat(nm, n)
            T.add("vector", lambda e, m=m: e.memset(m, 0.0), (), [nm])
        mat("sqt", 256)
        mat("sqt_B", 256)
        for nm, n in (("KZ", 512), ("BBt", 256), ("KKt", 256), ("Vt", 256), ("LA", 512), ("AK", 512),
                      ("Lt", 256), ("X", 256), ("PP", 512), ("QU", 512), ("Mst", 256), ("Rst", 256),
                      ("Hsb", 384), ("MnT", 1024)):
            mat(nm, n)
        for nm, n in (("Hs", 384), ("stat", 64), ("H0f", 1024), ("hf", 128), ("hf_B", 128), ("Hs_B", 384), ("stat_B", 64)):
            mat(nm, n, F32)
        for nm, n in (("KR", 512), ("Bf", 256), ("Cf", 256), ("BBf", 256), ("KKf", 256), ("Vf", 256)):
            m = mat(nm + "_B", n)
            T.add("vector", lambda e, m=m: e.memset(m, 0.0), (), [nm + "_B"])
        for nm, n in (("KZ", 512), ("BBt", 256), ("KKt", 256), ("Vt", 256), ("LA", 512), ("AK", 512),
                      ("Lt", 256), ("X", 256), ("PP", 512), ("QU", 512), ("Mst", 256), ("Rst", 256), ("Hsb", 384)):
            mat(nm + "_B", n)
        for nm, base in (("BBx", "KZ_B"), ("KKx", "LA_B"), ("Rsx", "Lt_B"), ("H0b", "QU_B")):
            MT[nm] = V(MTO[base], 512).bitcast(BF16)
        T.add("vector", lambda e: e.memset(MT["H0f"], 0.0), (), ["H0f"])
        T.add("vector", lambda e: e.tensor_copy(identb, CT["ident"]), ["cst"], ["identb"])
        mix_top = top[0]
        assert mix_top - mats0 >= 6656, (mix_top, mats0)

        def norm_tokens(src, ntok, gidx, dst, dkey, final_out=None, base=None):
            resident = isinstance(src, tuple)
            o = base
            if not resident:
                xsb = [AR[:, o - 4096 * i:o - 4096 * i + 4096].rearrange("p (k t) -> p k t", k=KB) for i in (0, 1)]
                o += 4096
            sq = AR[:, o:o + 2048].bitcast(BF16).rearrange("p (k t) -> p k t", k=KB)
            lnb = AR[:, o + 2048:o + 2304]
            rsd = AR[:, o + 2304:o + 2560]
            for t0 in range(0, ntok, 256):
                tw = min(256, ntok - t0)
                if resident:
                    xv = src[0][:, :, t0:t0 + tw]
                    xk = lambda k: [src[1][k]]
                else:
                    xi = (t0 // 256) % 2
                    xs = xsb[xi]
                    ld(xs[:, :, 0:tw], src[:, :, t0:t0 + tw], ["xs%d" % xi])
                    xv = xs[:, :, 0:tw]
                    xk = lambda k, xi=xi: ["xs%d" % xi]
                for k in range(KB):
                    act(sq[:, k, 0:tw], xv[:, k, :], AF.Square, xk(k), [("sq", k)])
                for k in range(KB):
                    mm(PS[0][:, 0:tw], onesb, sq[:, k, 0:tw], k == 0, k == KB - 1, ["onesb", ("sq", k)], ["ps0"])
                act(lnb[:, 0:tw], PS[0][:, 0:tw], AF.Ln, ["ps0", "eps"], ["lnb"], bias=epsn, scale=1.0 / D)
                act(rsd[:, 0:tw], lnb[:, 0:tw], AF.Exp, ["lnb"], ["rsd"], scale=-0.5)
                for k in range(KB):
                    if final_out is None:
                        stt(dst[:, k, t0:t0 + tw], xv[:, k, :], gv[:, gidx, k:k + 1], rsd[:, 0:tw],
                            ALU.mult, ALU.mult, xk(k) + ["gv", "rsd"], [dkey])
                    else:
                        stt(xv[:, k, :], xv[:, k, :], gv[:, gidx, k:k + 1], rsd[:, 0:tw],
                            ALU.mult, ALU.mult, xk(k) + ["gv", "rsd"], xk(k))
                if final_out is not None:
                    store(final_out[:, :, t0:t0 + tw], xv, [kk for k in range(KB) for kk in xk(k)])

        def wload(src):
            i = st["w"] % NW
            st["w"] += 1
            ldc(wst[i], src, [("w", i)])
            return i

        wst += [V(o_oT + 4608 + 1024 * i, 1024).bitcast(BF16).rearrange("p (k f) -> p k f", k=KB) for i in range(2)]
        RW_SLOTS = [0, 1, 2, 3, 6, 7]

        def wload_rw(src):
            i = RW_SLOTS[st["wr"] % len(RW_SLOTS)]
            st["wr"] += 1
            ldc(wst[i], src, [("w", i)])
            return i

        def proj(wi, tc0, ntok, ps):
            for k in range(KB):
                mm(ps[:, 0:ntok], wst[wi][:, k, :], hT[:, k, tc0:tc0 + ntok], k == 0, k == KB - 1,
                   [("w", wi), "hT"], [ps_key(ps)])

        def ps_key(ps):
            for i, p in enumerate(PS):
                if p is ps:
                    return f"ps{i}"
            raise KeyError

        pp = {"i": 0}

        def proj_ps():
            pp["i"] ^= 1
            return PS[pp["i"]]

        def shift(c, ps, grp, mdst, mkey, fp=0):
            tc0, ntok, kind, last_own = grp
            pb, db = FTS[fp]["pb"], FTS[fp]["db"]
            kpb, kdb = "pb#%d" % fp, "db#%d" % fp
            cp(pb[:, 0:1], carry[:, c:c + 1], ["carry"], [kpb])
            cp(pb[:, 1:1 + ntok], ps[:, 0:ntok], [ps_key(ps)], [kpb])
            if kind == "p":
                tt(db[:, 0:ntok], pb[:, 0:ntok], pb[:, 1:1 + ntok], ALU.subtract, [kpb], [kdb])
                cp(carry[:, c:c + 1], pb[:, ntok:ntok + 1], [kpb], ["carry"])
                if last_own:
                    cp(shout[:, c, 0:1], pb[:, ntok:ntok + 1], [kpb], ["shout"])
            else:
                cp(db[:, 0:ntok], pb[:, 0:ntok], [kpb], [kdb])
                cp(db[:, 0:ntok].rearrange("p (n t) -> p n t", t=DEC_T)[:, :, 0], shin[:, c, :], ["shin", kdb], [kdb])
                tt(db[:, 0:ntok], db[:, 0:ntok], pb[:, 1:1 + ntok], ALU.subtract, [kpb, kdb], [kdb])
                cp(shout[:, c, 1:1 + NS], pb[:, 1:1 + ntok].rearrange("p (n t) -> p n t", t=DEC_T)[:, :, DEC_T - 1],
                   [kpb], ["shout"])
            stt(mdst[:, 0:ntok], db[:, 0:ntok], mut[:, c:c + 1], pb[:, 1:1 + ntok], ALU.mult, ALU.add,
                [kdb, kpb, "mut"], [mkey])

        def groups(mode):
            if mode == "P":
                return [(t, QT, "p", False) for t in range(0, PRE, QT)]
            return [(t, QT, "p", t + QT == HALF) for t in range(0, HALF, QT)] + [(HALF, SMP, "s", False)]

        pmi = {"i": 0}
        LORK = [("lorT", c) for c in range(24, 28)]
        proj_done = set()
        sub_lock = {"busy": False}

        pinned = set()

        def pm(pin=False):
            while True:
                pmi["i"] = (pmi["i"] + 1) % 6
                if pmi["i"] not in pinned:
                    break
            if pin:
                pinned.add(pmi["i"])
            return PS[2 + pmi["i"]]

        def unpin(ps):
            for i in range(6):
                if PS[2 + i] is ps:
                    pinned.discard(i)

        def b3(ap, n, m):
            return ap.rearrange("p (n m) -> p n m", n=n)

        def bc_mid(ap2, n):
            return ap2.unsqueeze(1).to_broadcast([ap2.shape[0], n, ap2.shape[1]])

        def bc_last(ap2, m):
            return ap2.unsqueeze(2).to_broadcast([ap2.shape[0], ap2.shape[1], m])

        def lora_gen(lc, mode, fp):
            k0, k1 = "T0#%d" % fp, "T1#%d" % fp
            wi = wload(w_rw[lc])
            for grp in groups(mode):
                tc0, N, kind, _ = grp
                ps = proj_ps()
                proj(wi, tc0, N, ps)
                shift(lc, ps, grp, FTS[fp]["T0"], k0, fp)
                yield
                t0, t1 = FTS[fp]["T0"][:, 0:N], FTS[fp]["T1"][:, 0:N]
                dst = lorT[:, lc - 24, tc0:tc0 + N]
                if lc == 24:
                    act(t1, t0, AF.Exp, [k0], [k1], scale=2.0)
                    act(t1, t1, AF.Ln, [k1, "eps"], [k1], bias=one1)
                    act(t1, t1, AF.Exp, [k1], [k1], scale=-1.0)
                    ts(dst, t1, -2.0, 1.0, ALU.mult, ALU.add, [k1], [("lorT", lc)])
                elif lc == 25:
                    cp(dst, t0, [k0], [("lorT", lc)])
                else:
                    act(t1, t0, AF.Exp, [k0], [k1], scale=-1.0)
                    act(t1, t1, AF.Ln, [k1, "eps"], [k1], bias=one1)
                    act(dst, t1, AF.Exp, [k1], [("lorT", lc)], scale=-1.0)
                yield

        def lora_stage(mode):
            for pair in ((24, 25), (26, 27)):
                gens = [lora_gen(lc, mode, i) for i, lc in enumerate(pair)]
                while gens:
                    for gg in list(gens):
                        try:
                            next(gg)
                        except StopIteration:
                            gens.remove(gg)

        def rw_group(hp, grp, main, wk, wv, wr, fp, tag):
            fk = lambda n: n + "#%d" % fp
            FT = FTS[fp]
            tc0, N, kind, _ = grp
            L = 64 if kind == "p" else 8
            nseg = N // L
            F = {k: v[:, 0:N] for k, v in FT.items() if k != "pb"}
            hs = slice(hp, hp + 1)
            hcols = slice(hp * 128, (hp + 1) * 128)
            ps = proj_ps(); proj(wk, tc0, N, ps); shift(8 + hp, ps, grp, FT["mk"], fk("mk"), fp); yield
            ps = proj_ps(); proj(wv, tc0, N, ps); shift(16 + hp, ps, grp, FT["mv"], fk("mv"), fp); yield
            if main:
                ps = proj_ps(); proj(wr, tc0, N, ps); shift(hp, ps, grp, FT["mr"], fk("mr"), fp); yield
            proj_done.add(tag)
            scanm = CT["scan_p" if kind == "p" else "scan_s"][:, 0:N]
            p1 = pm(); k1 = ps_key(p1)
            mm(p1[:, 0:N], l2b[:, 0, hcols], lorT[:, 0, tc0:tc0 + N], True, True, ["l2b"] + LORK, [k1])
            act(F["T0"], p1[:, 0:N], AF.Exp, [k1, "nw0"], [fk("T0")], bias=nw0[:, hs], scale=-1.0)
            act(F["T0"], F["T0"], AF.Ln, [fk("T0"), "eps"], [fk("T0")], bias=one1)
            act(F["T0"], F["T0"], AF.Exp, [fk("T0")], [fk("T0")], scale=-1.0)
            T.add("vector", lambda e: e.tensor_tensor_scan(F["T1"], scanm, F["T0"], 0.0, ALU.mult, ALU.add),
                  ["cst", fk("T0")], [fk("T1")])
            tt(F["T0"], F["T1"], F["T0"], ALU.subtract, [fk("T0"), fk("T1")], [fk("T0")])
            act(F["T3"], F["T1"], AF.Exp, [fk("T1")], [fk("T3")], scale=-C0)
            act(F["T4"], F["T1"], AF.Exp, [fk("T1")], [fk("T4")], scale=C0)
            act(F["T0"], F["T0"], AF.Exp, [fk("T0")], [fk("T0")], scale=-C0)
            yield
            p2 = pm(); k2 = ps_key(p2)
            mm(p2[:, 0:N], l2b[:, 1, hcols], lorT[:, 1, tc0:tc0 + N], True, True, ["l2b"] + LORK, [k2])
            act(F["T1"], p2[:, 0:N], AF.Exp, [k2, "na0"], [fk("T1")], bias=na0[:, hs], scale=-1.0)
            act(F["T1"], F["T1"], AF.Ln, [fk("T1"), "eps"], [fk("T1")], bias=one1)
            act(F["T1"], F["T1"], AF.Exp, [fk("T1")], [fk("T1")], scale=-1.0)
            yield
            ts(F["T5"], F["mk"], rv[:, 2, hs], None, ALU.mult, None, [fk("mk"), "rv"], [fk("T5")])
            act(F["T6"], F["T5"], AF.Square, [fk("T5")], [fk("T6")])
            p3 = pm(); k3 = ps_key(p3)
            mm(p3[:, 0:N], CT["bones"], F["T6"], True, True, ["cst", fk("T6")], [k3])
            act(F["T6"], p3[:, 0:N], AF.Ln, [k3, "eps"], [fk("T6")], bias=eps24)
            act(F["T6"], F["T6"], AF.Exp, [fk("T6")], [fk("T6")], scale=-0.5)
            tt(F["T5"], F["T5"], F["T6"], ALU.mult, [fk("T5"), fk("T6")], [fk("T5")])
            ts(F["T6"], F["T1"], rv[:, 3, hs], omka[:, hs], ALU.mult, ALU.add, [fk("T1"), "rv", "omka"], [fk("T6")])
            tt(F["T6"], F["mk"], F["T6"], ALU.mult, [fk("mk"), fk("T6")], [fk("T6")])
            tt(F["T7"], F["T5"], F["T1"], ALU.mult, [fk("T5"), fk("T1")], [fk("T7")])
            yield
            if main:
                stt(F["T8"], F["mr"], rv[:, 4, hs], F["T6"], ALU.mult, ALU.mult, [fk("mr"), "rv", fk("T6")], [fk("T8")])
                p4 = pm(); k4 = ps_key(p4)
                mm(p4[:, 0:N], CT["bones"], F["T8"], True, True, ["cst", fk("T8")], [k4])
                tt(F["T8"], p4[:, 0:N], F["mv"], ALU.mult, [k4, fk("mv")], [fk("T8")])
                p5 = pm(); k5 = ps_key(p5)
                mm(p5[:, 0:N], l2b[:, 2, hcols], lorT[:, 2, tc0:tc0 + N], True, False, ["l2b"] + LORK, [k5])
                mm(p5[:, 0:N], l2b[:, 3, hcols], lorT[:, 3, tc0:tc0 + N], False, True, ["l2b"] + LORK, [k5])
                cp(F["T9"], p5[:, 0:N], [k5], [fk("T9")])
                tt(F["mr"], F["mr"], F["T3"], ALU.mult, [fk("mr"), fk("T3"), fk("T8")], [fk("mr")])
            yield
            tt(F["T7"], F["T7"], F["T4"], ALU.mult, [fk("T7"), fk("T4")], [fk("T7")])
            tt(F["T6"], F["T6"], F["T4"], ALU.mult, [fk("T6"), fk("T4"), fk("T8")], [fk("T6")])
            tt(F["T5"], F["T5"], F["T0"], ALU.mult, [fk("T5"), fk("T0")], [fk("T5")])
            wend = bc_last(b3(F["T3"], nseg, L)[:, :, L - 1], L)
            tt(b3(F["T10"], nseg, L), b3(F["T7"], nseg, L), wend, ALU.mult, [fk("T7"), fk("T3")], [fk("T10")])
            tt(b3(F["T11"], nseg, L), b3(F["T6"], nseg, L), wend, ALU.mult, [fk("T6"), fk("T3")], [fk("T11")])
            yield
            while sub_lock["busy"]:
                yield
            sub_lock["busy"] = True
            gens = [rw_sub(hp, grp, main, sub, F, ("", "_B")[sub], fk) for sub in range(N // 128)]
            in_tail, held = set(), True
            while gens:
                for gsub in list(gens):
                    try:
                        if next(gsub) == "tail":
                            in_tail.add(id(gsub))
                    except StopIteration:
                        gens.remove(gsub)
                if held and all(id(gsub) in in_tail for gsub in gens):
                    sub_lock["busy"] = False
                    held = False
                yield
            if main:
                stt(F["db"], F["db"], rv[:, 5, hs], F["T8"], ALU.mult, ALU.add, [fk("db"), "rv", fk("T8")], [fk("db")])
                stt(oT[:, hp, tc0:tc0 + N], F["db"], rv[:, 6, hs], F["T9"], ALU.add, ALU.mult,
                    [fk("db"), "rv", fk("T9")], [("oT", hp)])
                if grp[3]:
                    store(o_wkvp[hp, 0:64, :], Hrw[0:64, hp, 0:64], [("Hrw", hp)])
                    store(o_wkvp[hp, 64:128, :], Hrw[64:128, hp, 64:128], [("Hrw", hp)])

        def rw_sub(hp, grp, main, sub, F, sfx, fk):
            kn = lambda n: n + sfx
            tc0, N, kind, _ = grp
            c0 = sub * 128
            KR, KZ, LA, AK, QU = (b3(MT[kn(n)], 2, 256) for n in ("KR", "KZ", "LA", "AK", "QU"))
            Bf, Cf, BBf, KKf, Vf, BBt, KKt, Vt, Lt, X, Mst, Rst = (
                b3(MT[kn(n)], 2, 128) for n in ("Bf", "Cf", "BBf", "KKf", "Vf", "BBt", "KKt", "Vt", "Lt", "X", "Mst", "Rst"))
            PP = b3(MT[kn("PP")], 2, 256)
            stat = MT[kn("stat")]
            srcs = [("T5", KR, 0, kn("KR")), ("T7", Bf, 0, kn("Bf")), ("T6", Cf, 0, kn("Cf")), ("T10", BBf, 0, kn("BBf")),
                    ("T11", KKf, 0, kn("KKf")), ("mv", Vf, 0, kn("Vf"))]
            if main:
                srcs.append(("mr", KR, 128, kn("KR")))
            for nm, dst, co, dk in srcs:
                for h in (0, 1):
                    rows = slice(h * 64, h * 64 + 64)
                    cp(dst[rows, :, co + h * 64:co + h * 64 + 64],
                       F[nm][rows, c0:c0 + 128].rearrange("p (q s) -> p q s", q=2), [fk(nm)], [dk])
            yield
            for src, sk, dst, dk, co in ((KR, kn("KR"), KZ, kn("KZ"), 0), (BBf, kn("BBf"), BBt, kn("BBt"), 0),
                                         (KKf, kn("KKf"), KKt, kn("KKt"), 0), (Vf, kn("Vf"), Vt, kn("Vt"), 0)):
                p = pm(); k = ps_key(p)
                for q in (0, 1):
                    tr(psb(p)[:, q * 128:(q + 1) * 128], src[:, q, 0:128], [sk], [k])
                cp(dst[:, :, 0:128], b3(psb(p)[:, 0:256], 2, 128), [k], [dk])
            yield
            UU = CT["UU_p" if kind == "p" else "UU_s"]
            sL = CT["sL_p" if kind == "p" else "sL_s"]
            W = 256 if main else 128
            for lhs, lk, dst, dk in ((Bf, kn("Bf"), LA, kn("LA")), (Cf, kn("Cf"), AK, kn("AK"))):
                p = pm(); k = ps_key(p)
                for q in (0, 1):
                    mm(p[:, q * 256:q * 256 + W], lhs[:, q, :], KR[:, q, 0:W], True, True, [lk, kn("KR")], [k])
                tt(dst[:, :, 0:W], b3(p, 2, 256)[:, :, 0:W], bc_mid(UU[:, 0:W], 2), ALU.mult, [k, "cst"], [dk])
            p = pm(); k = ps_key(p)
            for q in (0, 1):
                mm(p[:, q * 128:(q + 1) * 128], KR[:, q, 0:128], Bf[:, q, :], True, True, [kn("KR"), kn("Bf")], [k])
            tt(Lt, b3(p[:, 0:256], 2, 128), bc_mid(sL, 2), ALU.mult, [k, "cst"], [kn("Lt")])
            yield
            p = pm(); k = ps_key(p)
            for q in (0, 1):
                mm(p[:, q * 256:q * 256 + 128], Lt[:, q, :], LA[:, q, 0:128], True, True, [kn("Lt"), kn("LA")], [k])
                mm(p[:, q * 256 + 128:q * 256 + 256], LA[:, q, 0:128], Lt[:, q, :], True, True, [kn("Lt"), kn("LA")], [k])
            act(PP, b3(p, 2, 256), AF.Copy, [k], [kn("PP")])
            tt(X, bc_mid(CT["ident"], 2), LA[:, :, 0:128], ALU.subtract, ["cst", kn("LA")], [kn("X")])
            for lvl in range(5):
                yield
                p = pm(); k = ps_key(p)
                for q in (0, 1):
                    mm(p[:, q * 128:(q + 1) * 128], PP[:, q, 128:256], X[:, q, :], True, True, [kn("PP"), kn("X")], [k])
                tt(X, X, b3(p[:, 0:256], 2, 128), ALU.add, [kn("X"), k], [kn("X")])
                if lvl < 4:
                    p = pm(); k = ps_key(p)
                    for q in (0, 1):
                        mm(p[:, q * 256:q * 256 + 128], PP[:, q, 128:256], PP[:, q, 0:128], True, True, [kn("PP")], [k])
                        mm(p[:, q * 256 + 128:q * 256 + 256], PP[:, q, 0:128], PP[:, q, 128:256], True, True, [kn("PP")], [k])
                    act(PP, b3(p, 2, 256), AF.Copy, [k], [kn("PP")])
            yield
            p = pm(); k = ps_key(p)
            for q in (0, 1):
                mm(p[:, q * 128:(q + 1) * 128], AK[:, q, 0:128], Vt[:, q, :], True, True, [kn("AK"), kn("Vt")], [k])
            cp(KZ[:, :, 128:256], b3(p[:, 0:256], 2, 128), [k], [kn("KZ")])
            p = pm(); k = ps_key(p)
            for q in (0, 1):
                mm(p[:, q * 256:(q + 1) * 256], X[:, q, :], KZ[:, q, :], True, True, [kn("X"), kn("KZ")], [k])
            cp(QU, b3(p, 2, 256), [k], [kn("QU")], scale=-1.0)
            yield
            if main:
                p = pm(); k = ps_key(p)
                for q in (0, 1):
                    mm(p[:, q * 128:(q + 1) * 128], QU[:, q, 0:128], LA[:, q, 128:256], True, True, [kn("QU"), kn("LA")], [k])
                tt(Rst, b3(p[:, 0:256], 2, 128), KR[:, :, 128:256], ALU.add, [k, kn("KR")], [kn("Rst")])
            pY = None
            yield
            if kind == "p":
                p = pm(); k = ps_key(p)
                for q in (0, 1):
                    mm(p[:, q * 128:(q + 1) * 128], QU[:, q, 0:128], BBt[:, q, :], True, True, [kn("QU"), kn("BBt")], [k])
                cp(Mst, b3(p[:, 0:256], 2, 128), [k], [kn("Mst")])
                Hs, Hsb = b3(MT[kn("Hs")], 3, 128), b3(MT[kn("Hsb")], 3, 128)
                st_f = [Hrw[:, hp, :], Hs[:, 1, :]]
                st_b = [Hrwb[:, hp, :], Hsb[:, 1, :]]
                kf = [("Hrw", hp), (kn("Hs"), 1)]
                kb = [("Hrwb", hp), (kn("Hsb"), 1)]
                if main:
                    pY = pm(pin=True); kY = ps_key(pY)
                for q in (0, 1):
                    if main:
                        yo = pY[:, q * 128:(q + 1) * 128]
                        mm(yo, LA[:, q, 128:256], QU[:, q, 128:256], True, False, [kn("LA"), kn("QU")], [kY])
                        mm(yo, AK[:, q, 128:256], Vt[:, q, :], False, False, [kn("AK"), kn("Vt")], [kY])
                        mm(yo, Rst[:, q, :], st_b[q], False, True, [kn("Rst"), kb[q]], [kY])
                    p = pm(); k = ps_key(p)
                    mm(p[:, 0:128], BBt[:, q, :], QU[:, q, 128:256], True, False, [kn("BBt"), kn("QU")], [k])
                    mm(p[:, 0:128], KKt[:, q, :], Vt[:, q, :], False, False, [kn("KKt"), kn("Vt")], [k])
                    mm(p[:, 0:128], Mst[:, q, :], st_b[q], False, True, [kn("Mst"), kb[q]], [k])
                    wc = F["T3"][:, c0 + q * 64 + 63:c0 + q * 64 + 64]
                    stt(st_b[1 - q], st_f[q], wc, p[:, 0:128], ALU.mult, ALU.add, [kf[q], fk("T3"), k], [kb[1 - q]])
                    stt(st_f[1 - q], st_f[q], wc, p[:, 0:128], ALU.mult, ALU.add, [kf[q], fk("T3"), k], [kf[1 - q]])
            else:
                BBx, KKx, Rsx, H0b, H0f = (b3(MT[n], 8, 128) for n in ("BBx", "KKx", "Rsx", "H0b", "H0f"))
                kBBx, kKKx, kRsx, kH0b = ["KZ_B", "BBt_B", "KKt_B"], ["LA_B", "AK_B"], ["Lt_B", "X_B", "PP_B"], ["QU_B", "Mst_B", "Rst_B"]
                segT, segF = CT["segT"], b3(CT["segF"], 8, 128)
                pY = pm(pin=True); kY = ps_key(pY)
                for q in (0, 1):
                    n0 = (sub * 2 + q) * 8
                    ld(H0f[0:64, :, 0:64], rwH0[hp, 0:64, n0:n0 + 8, :], ["H0f"])
                    ld(H0f[64:128, :, 64:128], rwH0[hp, 64:128, n0:n0 + 8, :], ["H0f"])
                    cp(H0b, H0f, ["H0f"], kH0b)
                    tt(BBx, bc_mid(BBt[:, q, :], 8), bc_last(segT, 128), ALU.mult, [kn("BBt"), "cst"], kBBx)
                    tt(KKx, bc_mid(KKt[:, q, :], 8), bc_last(segT, 128), ALU.mult, [kn("KKt"), "cst"], kKKx)
                    tt(Rsx, bc_mid(Rst[:, q, :], 8), segF, ALU.mult, [kn("Rst"), "cst"], kRsx)
                    yo = pY[:, q * 128:(q + 1) * 128]
                    mm(yo, LA[:, q, 128:256], QU[:, q, 128:256], True, False, [kn("LA"), kn("QU")], [kY])
                    mm(yo, AK[:, q, 128:256], Vt[:, q, :], False, False, [kn("AK"), kn("Vt")], [kY])
                    for n in range(8):
                        mm(yo, Rsx[:, n, :], H0b[:, n, :], False, n == 7, kRsx + kH0b, [kY])
                    MnT8 = b3(MT["MnT"], 8, 128)
                    pM = [pm(pin=True), pm(pin=True)]
                    for n in range(8):
                        mm(pM[n // 4][:, (n % 4) * 128:(n % 4 + 1) * 128], QU[:, q, 0:128], BBx[:, n, :], True, True,
                           [kn("QU")] + kBBx, [ps_key(pM[n // 4])])
                    for hf in (0, 1):
                        cp(MnT8[:, hf * 4:hf * 4 + 4, :], b3(pM[hf], 4, 128), [ps_key(pM[hf])], [("MnT", hf)])
                        unpin(pM[hf])
                    pS = [pm(pin=True), pm(pin=True)]
                    for n in range(8):
                        po = pS[n // 4]; ko = ps_key(po)
                        oo = po[:, (n % 4) * 128:(n % 4 + 1) * 128]
                        mm(oo, BBx[:, n, :], QU[:, q, 128:256], True, False, kBBx + [kn("QU")], [ko])
                        mm(oo, KKx[:, n, :], Vt[:, q, :], False, False, kKKx + [kn("Vt")], [ko])
                        mm(oo, MnT8[:, n, :], H0b[:, n, :], False, True, [("MnT", n // 4)] + kH0b, [ko])
                    wseg = b3(F["T3"][:, c0 + q * 64:c0 + q * 64 + 64], 8, 8)[:, :, 7]
                    for hf in (0, 1):
                        hv = H0f[:, hf * 4:hf * 4 + 4, :]
                        tt(hv, hv, bc_last(wseg[:, hf * 4:hf * 4 + 4], 128), ALU.mult, ["H0f", fk("T3")] + kH0b, ["H0f"])
                        tt(hv, hv, b3(pS[hf], 4, 128), ALU.add, ["H0f", ps_key(pS[hf])], ["H0f"])
                    unpin(pS[0]); unpin(pS[1])
                    store(o_wkvs[hp, 0:64, n0:n0 + 8, :], H0f[0:64, :, 0:64], ["H0f"])
                    store(o_wkvs[hp, 64:128, n0:n0 + 8, :], H0f[64:128, :, 64:128], ["H0f"])
            yield "tail"
            if main:
                YN = b3(MT[kn("YN")], 2, 128)
                yv = b3(pY[:, 0:256], 2, 128)
                T.add("vector", lambda e: e.tensor_reduce(out=stat[:, 0:2], in_=yv, axis=AX.X, op=ALU.add), [kY], [kn("stat")])
                sqt = b3(MT[kn("sqt")], 2, 128)
                act(MT[kn("sqt")], pY[:, 0:256], AF.Square, [kY], [kn("sqt")])
                T.add("vector", lambda e: e.tensor_reduce(out=stat[:, 2:4], in_=sqt, axis=AX.X, op=ALU.add), [kn("sqt")], [kn("stat")])
                ts(stat[:, 4:6], stat[:, 0:2], 1.0 / 64, None, ALU.mult, None, [kn("stat")], [kn("stat")])
                tt(stat[:, 6:8], stat[:, 4:6], stat[:, 4:6], ALU.mult, [kn("stat")], [kn("stat")])
                stt(stat[:, 8:10], stat[:, 2:4], 1.0 / 64, stat[:, 6:8], ALU.mult, ALU.subtract, [kn("stat")], [kn("stat")])
                act(stat[:, 10:12], stat[:, 8:10], AF.Ln, [kn("stat"), "eps"], [kn("stat")], bias=epsg)
                act(stat[:, 12:14], stat[:, 10:12], AF.Exp, [kn("stat")], [kn("stat")], scale=-0.5)
                stt(stat[:, 14:16], stat[:, 4:6], -1.0, stat[:, 12:14], ALU.mult, ALU.mult, [kn("stat")], [kn("stat")])
                for h in (0, 1):
                    rows = slice(h * 64, h * 64 + 64)
                    cs = slice(h * 64, h * 64 + 64)
                    tt(YN[rows, :, cs], yv[rows, :, cs], bc_last(stat[rows, 12:14], 64), ALU.mult, [kY, kn("stat")], [kn("YN")])
                    tt(YN[rows, :, cs], YN[rows, :, cs], bc_last(stat[rows, 14:16], 64), ALU.add, [kn("YN"), kn("stat")], [kn("YN")])
                p = pm(); k = ps_key(p)
                for q in (0, 1):
                    tr(psb(p)[:, q * 128:(q + 1) * 128], YN[:, q, :], [kn("YN")], [k])
                pv = b3(psb(p)[:, 0:256], 2, 128)
                half = MT[kn("hf")].rearrange("p (q s) -> p q s", q=2)
                cp(half, pv[:, :, 0:64], [k], [kn("hf")])
                tt(F["db"][:, c0:c0 + 128].rearrange("p (q s) -> p q s", q=2), half, pv[:, :, 64:128], ALU.add,
                   [kn("hf"), k], [fk("db")])
                unpin(pY)


        def trn(out, in_, np_, r, w):
            return T.add("tensor", lambda e: e.transpose(out, in_, CT["ident"][0:np_, 0:np_]), r + ["cst"], w)

        gbase = [mats0]

        def galloc(n):
            o = gbase[0]
            gbase[0] += n
            assert gbase[0] <= mix_top
            return V(o, n)

        GK, GV, GA, GON, GSs, GS0, GKx, GQx, GST = (galloc(n) for n in (128, 256, 64, 512, 768, 2048, 512, 256, 64))
        GSET = {"": (GK, GV, GA, GON, GSs, GST, galloc(384)), "_B": tuple(galloc(n) for n in (128, 256, 64, 512, 768, 64, 384))}
        GS0b = galloc(1024)

        def gla_xgate(mode):
            wi = wload(w_gl[16])
            for grp in groups(mode):
                tc0, N, kind, _ = grp
                ps = proj_ps()
                proj(wi, tc0, N, ps)
                cp(xgT[:, tc0:tc0 + N], ps[:, 0:N], [ps_key(ps)], ["xgT"])

        gl_lock = {"busy": False}

        def gl_group(g, grp, main, W, fp, tag):
            fk = lambda n: n + "#%d" % fp
            FT = FTS[fp]
            tc0, N, kind, _ = grp
            L = 64 if kind == "p" else 8
            nseg = N // L
            F = {k: v[:, 0:N] for k, v in FT.items() if k != "pb"}
            F.update({fk(k): v for k, v in list(F.items())})
            gs = slice(g, g + 1)
            scanm = CT["scan_p" if kind == "p" else "scan_s"][:, 0:N]
            p = pm(); k = ps_key(p)
            mm(p[:, 0:N], gw2b[:, g * 128:(g + 1) * 128], xgT[:, tc0:tc0 + N], True, True, ["gw2b", "xgT"], [k])
            act(F["T0"], p[:, 0:N], AF.Exp, [k, "ngb"], [fk("T0")], bias=ngb[:, gs], scale=-1.0)
            act(F["T0"], F["T0"], AF.Ln, [fk("T0"), "eps"], [fk("T0")], bias=one1)
            T.add("vector", lambda e: e.tensor_tensor_scan(F["T1"], scanm, F["T0"], 0.0, ALU.mult, ALU.add),
                  ["cst", fk("T0")], [fk("T1")])
            act(F["T3"], F["T1"], AF.Exp, [fk("T1")], [fk("T3")], scale=-1.0 / 16)
            act(F["T4"], F["T1"], AF.Exp, [fk("T1")], [fk("T4")], scale=1.0 / 16)
            yield
            wi = W["k"]; ps = proj_ps(); proj(wi, tc0, N, ps)
            tt(F["T5"], ps[:, 0:N], F["T4"], ALU.mult, [ps_key(ps), fk("T4")], [fk("T5")])
            wend = bc_last(b3(F["T3"], nseg, L)[:, :, L - 1], L)
            tt(b3(F["T6"], nseg, L), b3(F["T5"], nseg, L), wend, ALU.mult, [fk("T5"), fk("T3")], [fk("T6")])
            for hf, nm in ((0, fk("mk")), (1, fk("mv"))):
                yield
                wi = W["v%d" % hf]; ps = proj_ps(); proj(wi, tc0, N, ps)
                cp(F[nm], ps[:, 0:N], [ps_key(ps)], [nm])
            yield
            if main:
                wi = W["q"]; ps = proj_ps(); proj(wi, tc0, N, ps)
                stt(F["T7"], ps[:, 0:N], 128.0 ** -0.5, F["T3"], ALU.mult, ALU.mult, [ps_key(ps), fk("T3")], [fk("T7")])
                cp(F["T0"].bitcast(BF16)[:, 0:N], F["T5"], [fk("T5")], [fk("T0")])
                cp(F["T1"].bitcast(BF16)[:, 0:N], F["T7"], [fk("T7")], [fk("T1")])
                for hf, nm, tn in ((0, fk("T8"), fk("T10")), (1, fk("T9"), fk("T11"))):
                    yield
                    wi = W["go%d" % hf]; ps = proj_ps(); proj(wi, tc0, N, ps)
                    cp(F[nm], ps[:, 0:N], [ps_key(ps)], [nm])
                    act(F[tn], F[nm], AF.Exp, [nm], [tn], scale=-1.0)
                    act(F[tn], F[tn], AF.Ln, [tn, "eps"], [tn], bias=one1)
                    act(F[tn], F[tn], AF.Exp, [tn], [tn], scale=-1.0)
                    tt(F[nm], F[nm], F[tn], ALU.mult, [nm, tn], [nm])
            proj_done.add(tag)
            yield
            while gl_lock["busy"]:
                yield
            gl_lock["busy"] = True
            def gl_sub(sub, sfx):
                kn = lambda n: n + sfx
                GK, GV, GA, GON, GSs, GST, GSsb = GSET[sfx]
                GKb, GVb, GAb = GK.bitcast(BF16), GV.bitcast(BF16), GA.bitcast(BF16)
                T5b, T7b = F["T0"].bitcast(BF16), F["T1"].bitcast(BF16)
                c0 = sub * 128
                GKv, GVv, GAv, ONv = b3(GKb, 2, 128), b3(GVb, 2, 256), b3(GAb, 2, 64), b3(GON, 2, 256)
                Ss, Ssb = b3(GSs, 3, 256), b3(GSsb.bitcast(BF16), 3, 256)
                p = pm(); k = ps_key(p)
                for q in (0, 1):
                    trn(p[0:64, q * 128:(q + 1) * 128], F["T6"][:, c0 + q * 64:c0 + q * 64 + 64], 128, [fk("T6")], [k])
                cp(GKb[0:64, :], p[0:64, 0:256], [k], [kn("GK")])
                p = pm(); k = ps_key(p)
                for q in (0, 1):
                    for hf, nm in ((0, fk("mk")), (1, fk("mv"))):
                        trn(p[0:64, q * 256 + hf * 128:q * 256 + hf * 128 + 128],
                            F[nm][:, c0 + q * 64:c0 + q * 64 + 64], 128, [nm], [k])
                cp(GVb[0:64, :], p[0:64, 0:512], [k], [kn("GV")])
                yield
                if main:
                    p = pm(); k = ps_key(p)
                    for q in (0, 1):
                        cs = slice(c0 + q * 64, c0 + q * 64 + 64)
                        mm(p[0:64, q * 64:(q + 1) * 64], T5b[:, cs], T7b[:, cs], True, True, [fk("T0"), fk("T1")], [k])
                    gm = CT["G_p" if kind == "p" else "G_s"]
                    tt(GAv[0:64], b3(p[0:64, 0:128], 2, 64), bc_mid(gm[0:64, :], 2), ALU.mult, [k, "cst"], [kn("GA")])
                    pO = pm(pin=True); kO = ps_key(pO)
                yield
                if kind == "p":
                    cp(Ss[:, 0, :], Sgl[:, g, :], ["Sgl"], [(kn("Ss"), 0)])
                    cp(Ssb[:, 0, :], Sgl[:, g, :], ["Sgl"], [(kn("Ssb"), 0)])
                    for q in (0, 1):
                        cs = slice(c0 + q * 64, c0 + q * 64 + 64)
                        if main:
                            oo = pO[0:64, q * 256:(q + 1) * 256]
                            mm(oo, GAv[0:64, q, :], GVv[0:64, q, :], True, False, [kn("GA"), kn("GV")], [kO])
                            mm(oo, T7b[:, cs], Ssb[:, q, :], False, True, [fk("T1"), (kn("Ssb"), q)], [kO])
                        p = pm(); k = ps_key(p)
                        mm(p[:, 0:256], GKv[0:64, q, :], GVv[0:64, q, :], True, True, [kn("GK"), kn("GV")], [k])
                        wc = F["T3"][:, c0 + q * 64 + 63:c0 + q * 64 + 64]
                        stt(Ssb[:, q + 1, :], Ss[:, q, :], wc, p[:, 0:256], ALU.mult, ALU.add,
                            [(kn("Ss"), q), fk("T3"), k], [(kn("Ssb"), q + 1)])
                        stt(Ss[:, q + 1, :], Ss[:, q, :], wc, p[:, 0:256], ALU.mult, ALU.add,
                            [(kn("Ss"), q), fk("T3"), k], [(kn("Ss"), q + 1)])
                    cp(Sgl[:, g, :], Ss[:, 2, :], [(kn("Ss"), 2)], ["Sgl"])
                else:
                    S0v, Kxv, Qxv = b3(GS0, 8, 256), b3(GKx.bitcast(BF16), 8, 128), b3(GQx.bitcast(BF16), 8, 64)
                    S0bv = b3(GS0b.bitcast(BF16), 8, 256)
                    for q in (0, 1):
                        cs = slice(c0 + q * 64, c0 + q * 64 + 64)
                        n0 = (sub * 2 + q) * 8
                        ld(S0v, glS0[g, :, n0:n0 + 8, :], ["GS0"])
                        cp(S0bv, S0v, ["GS0"], ["GS0b"])
                        tt(Qxv, bc_mid(F["T7"][:, cs], 8), b3(CT["gsegF"], 8, 64), ALU.mult, [fk("T7"), "cst"], ["GQx"])
                        tt(Kxv[0:64], bc_mid(GKv[0:64, q, :], 8), bc_last(CT["segT"][0:64, :], 128), ALU.mult,
                           [kn("GK"), "cst"], ["GKx"])
                        oo = pO[0:64, q * 256:(q + 1) * 256]
                        mm(oo, GAv[0:64, q, :], GVv[0:64, q, :], True, False, [kn("GA"), kn("GV")], [kO])
                        for n in range(8):
                            mm(oo, Qxv[:, n, :], S0bv[:, n, :], False, n == 7, ["GQx", "GS0b"], [kO])
                        wseg = b3(F["T3"][:, cs], 8, 8)[:, :, 7]
                        pss = [pm(pin=True) for _ in range(4)]
                        for n in range(8):
                            po = pss[n // 2]
                            mm(po[:, (n % 2) * 256:(n % 2 + 1) * 256], Kxv[0:64, n, :], GVv[0:64, q, :], True, True,
                               ["GKx", kn("GV")], [ps_key(po)])
                        for j in range(4):
                            sv = S0v[:, 2 * j:2 * j + 2, :]
                            tt(sv, sv, bc_last(wseg[:, 2 * j:2 * j + 2], 256), ALU.mult, ["GS0", fk("T3"), kO], ["GS0"])
                            tt(sv, sv, b3(pss[j], 2, 256), ALU.add, ["GS0", ps_key(pss[j])], ["GS0"])
                        for po in pss:
                            unpin(po)
                        store(o_glas[g, :, n0:n0 + 8, :], S0v, ["GS0"])
                yield
                if main:
                    stat = GST
                    act(GON[0:64, :], pO[0:64, 0:512], AF.Square, [kO], [kn("GON")])
                    T.add("vector", lambda e: e.tensor_reduce(out=stat[0:64, 0:2], in_=ONv[0:64], axis=AX.X, op=ALU.add),
                          [kn("GON")], [kn("stat")])
                    act(stat[0:64, 2:4], stat[0:64, 0:2], AF.Ln, [kn("stat"), "eps"], [kn("stat")], bias=epsh[0:64], scale=1.0 / 256)
                    act(stat[0:64, 4:6], stat[0:64, 2:4], AF.Exp, [kn("stat")], [kn("stat")], scale=-0.5)
                    tt(ONv[0:64], b3(pO[0:64, 0:512], 2, 256), bc_last(stat[0:64, 4:6], 256), ALU.mult, [kO, kn("stat"), kn("GON")], [kn("GON")])
                    p = pm(); k = ps_key(p)
                    for q in (0, 1):
                        for hf in (0, 1):
                            trn(p[:, hf * 128 + q * 64:hf * 128 + q * 64 + 64], ONv[0:64, q, hf * 128:(hf + 1) * 128], 64,
                                [kn("GON")], [k])
                    for hf, nm in ((0, fk("T8")), (1, fk("T9"))):
                        stt(oT[:, 8 + 2 * g + hf, tc0 + c0:tc0 + c0 + 128], p[:, hf * 128:(hf + 1) * 128],
                            glvt[:, 4 + hf:5 + hf], F[nm][:, c0:c0 + 128], ALU.mult, ALU.mult,
                            [k, "glv", nm], [("oT", 8 + 2 * g + hf)])
                    unpin(pO)
                yield
            gens = [gl_sub(sub, ("", "_B")[sub]) for sub in range(N // 128)]
            while gens:
                for gsub in list(gens):
                    try:
                        next(gsub)
                    except StopIteration:
                        gens.remove(gsub)
                yield
            gl_lock["busy"] = False
            if main and grp[3]:
                store(o_glap[g, :, :], Sgl[:, g, :], ["Sgl"])

        def zero_blk():
            for nm in ("KR", "Bf", "Cf", "BBf", "KKf", "Vf", "H0f", "KR_B", "Bf_B", "Cf_B", "BBf_B", "KKf_B", "Vf_B", "YN", "YN_B"):
                T.add("vector", lambda e, m=MT[nm]: e.memset(m, 0.0), (), [nm])

        for mode in ("P", "M"):
            main = mode == "M"
            norm_tokens(xM if main else xP, NTM if main else PRE, 0, hT, "hT", base=top[0] - 6656)
            T.barrier()
            zero_blk()
            if main:
                ldc(l2b, lora2[:, :, :], ["l2b"])
            lora_stage(mode)
            seq = [(hp, gi, grp) for hp in range(8) for gi, grp in enumerate(groups(mode))]
            wts, active, nxt = {}, [], 0

            def start(j, main=main, seq=seq, wts=wts):
                hp, gi, grp = seq[j]
                if gi == 0:
                    wk, wv = wload_rw(w_rw[8 + hp]), wload_rw(w_rw[16 + hp])
                    if main:
                        wr = wload_rw(w_rw[hp])
                    else:
                        wr = None
                        wr0 = wload_rw(w_rw[hp])
                        ps = proj_ps()
                        proj(wr0, PRE - 64, 64, ps)
                        cp(carry[:, hp:hp + 1], ps[:, 63:64], [ps_key(ps)], ["carry"])
                    wts[hp] = (wk, wv, wr)
                return rw_group(hp, grp, main, *wts[hp], j % 2, ("rw", mode, j))

            while nxt < len(seq) or active:
                while len(active) < 2 and nxt < len(seq):

                    active.append((start(nxt), ("rw", mode, nxt)))
                    nxt += 1
                for gg in list(active):
                    try:
                        next(gg[0])
                    except StopIteration:
                        active.remove(gg)
            T.barrier()
            gla_xgate(mode)
            gseq = [(g, gi, grp) for g in range(4) for gi, grp in enumerate(groups(mode))]
            gw, gact, gnx = {}, [], 0

            def gstart(j, main=main, gseq=gseq, gw=gw):
                g, gi, grp = gseq[j]
                if gi == 0:
                    W = {}
                    srcs = [("k", 4 + g), ("v0", 8 + 2 * g), ("v1", 9 + 2 * g)]
                    if main:
                        srcs += [("q", g), ("go0", 17 + 2 * g), ("go1", 18 + 2 * g)]
                    for slot, (nm, ci) in enumerate(srcs):
                        ldc(wst[slot], w_gl[ci], [("w", slot)])
                        W[nm] = slot
                    gw[g] = W
                return gl_group(g, grp, main, gw[g], j % 2, ("gl", mode, j))

            while gnx < len(gseq) or gact:
                while len(gact) < 2 and gnx < len(gseq):
                    if gseq[gnx][1] == 0 and any(t not in proj_done for _, t in gact):
                        break
                    gact.append((gstart(gnx), ("gl", mode, gnx)))
                    gnx += 1
                for gg in list(gact):
                    try:
                        next(gg[0])
                    except StopIteration:
                        gact.remove(gg)
            T.barrier()
        store(o_shift[:, :, :], shout, ["shout"])

        X1 = V(ov0, 18432).rearrange("p (k t) -> p k t", k=KB)
        rtmp = V(ov0 + 18432 + 2560, 512)
        ttiles = [(0, 512), (512, 512), (1024, 128)]
        for fc in range(KB):
            wi = wload(w_o[fc])
            ld(X1[:, fc, :], xM[:, fc, :], [("X1", fc)])
            for t0, tw in ttiles:
                ps = proj_ps(); k = ps_key(ps)
                for kk in range(KB):
                    mm(ps[:, 0:tw], wst[wi][:, kk, :], oT[:, kk, t0:t0 + tw], kk == 0, kk == KB - 1,
                       [("w", wi), ("oT", kk)], [k])
                tt(X1[:, fc, t0:t0 + tw], X1[:, fc, t0:t0 + tw], ps[:, 0:tw], ALU.add, [("X1", fc), k], [("X1", fc)])
        x1k = [("X1", fc) for fc in range(KB)]
        norm_tokens((X1, x1k), NTM, 1, hT, "hT", base=ov0 + 18432)
        uT = oT
        for g4 in range(4):
            for fc in range(KB):
                wi = wload(w_u[g4 * KB + fc])
                for t0, tw in ttiles:
                    ps = proj_ps(); k = ps_key(ps)
                    for kk in range(KB):
                        mm(ps[:, 0:tw], wst[wi][:, kk, :], hT[:, kk, t0:t0 + tw], kk == 0, kk == KB - 1,
                           [("w", wi), "hT"], [k])
                    act(rtmp[:, 0:tw], ps[:, 0:tw], AF.Relu, [k], ["rtmp"])
                    tt(uT[:, fc, t0:t0 + tw], rtmp[:, 0:tw], rtmp[:, 0:tw], ALU.mult, ["rtmp"], [("oT", fc)])
            for fc in range(KB):
                wi = wload(w_d[g4, fc])
                for t0, tw in ttiles:
                    ps = proj_ps(); k = ps_key(ps)
                    for kk in range(KB):
                        mm(ps[:, 0:tw], wst[wi][:, kk, :], uT[:, kk, t0:t0 + tw], kk == 0, kk == KB - 1,
                           [("w", wi), ("oT", kk)], [k])
                    tt(X1[:, fc, t0:t0 + tw], X1[:, fc, t0:t0 + tw], ps[:, 0:tw], ALU.add, [("X1", fc), k], [("X1", fc)])
        fb = ov0 + 18432
        lnF, rsF = V(fb, NTM), V(fb + NTM, NTM)
        for k in range(KB):
            act(hT[:, k, :], X1[:, k, :], AF.Square, [("X1", k)], ["hT"])
        for ti, (t0, tw) in enumerate(ttiles):
            pb_ = PS[2 + ti]
            for k in range(KB):
                mm(pb_[:, 0:tw], onesb, hT[:, k, t0:t0 + tw], k == 0, k == KB - 1, ["onesb", "hT"], [ps_key(pb_)])
            act(lnF[:, t0:t0 + tw], pb_[:, 0:tw], AF.Ln, [ps_key(pb_), "eps"], ["lnF"], bias=epsn, scale=1.0 / D)
            act(rsF[:, t0:t0 + tw], lnF[:, t0:t0 + tw], AF.Exp, ["lnF"], ["rsF"], scale=-0.5)
        for k in range(KB):
            stt(X1[:, k, :], X1[:, k, :], gv[:, 2, k:k + 1], rsF, ALU.mult, ALU.mult, [("X1", k), "gv", "rsF"], [("X1", k)])
            store(yT[:, k, :], X1[:, k, :], [("X1", k)])
        T.emit(out_ids)
    return nc


def _fm(a2d):
    t = a2d.shape[0]
    return np.ascontiguousarray(a2d.T.reshape(KB, 128, t).transpose(1, 0, 2))


def _wt(w, nch):
    return np.ascontiguousarray(w.reshape(KB, 128, nch, 128).transpose(2, 1, 0, 3))


def _pad_cols(a, segs, axis=-1):
    out = []
    for s0, n in segs:
        piece = np.take(a, np.arange(s0, s0 + n), axis=axis)
        if n < 128:
            padw = [(0, 0)] * a.ndim
            padw[axis] = (0, 128 - n)
            piece = np.pad(piece, padw)
        out.append(piece)
    return np.concatenate(out, axis=axis)


RW_SEGS = [(i * 128, 128) for i in range(24)] + [(3072, 64), (3136, 64), (3200, 128), (3328, 32)]
RW_UNPAD = np.concatenate([np.arange(c * 128, c * 128 + n) for c, (s0, n) in enumerate(RW_SEGS)])
GL0 = RW_PROJ
GL_SEGS = [(GL0 + i * 128, 128) for i in range(16)] + [(GL0 + 2048, 16)] + [(GL0 + 2064 + i * 128, 128) for i in range(8)]


def kernel(x_prompt, x_sample, state_rwkv_shift, state_rwkv_wkv, state_gla, norm1_g, w_in, rw_mu,
           rw_w0, rw_w2, rw_a0, rw_a2, rw_g2, rw_k_k, rw_k_a, rw_r_k, rw_ln_w, rw_ln_b, gla_gw2,
           gla_gb, gla_norm_w, w_out, norm2_g, w_up, w_down, norm_f_g):
    f32 = np.float32
    A = lambda z: np.asarray(z, f32)
    x_prompt, x_sample = A(x_prompt), A(x_sample)
    w_in0 = A(w_in)[0]
    w_rw = _wt(_pad_cols(w_in0, RW_SEGS), NRW)
    w_gl = _wt(_pad_cols(w_in0, GL_SEGS), NGL)
    mu = np.ascontiguousarray(_pad_cols(A(rw_mu)[0], RW_SEGS).reshape(NRW, 128).T)
    vecT = lambda v: np.ascontiguousarray(A(v).reshape(-1, 128).T)
    gvec = np.ascontiguousarray(np.stack([vecT(A(norm1_g)[0]), vecT(A(norm2_g)[0]), vecT(A(norm_f_g))], 1))
    rwv = np.ascontiguousarray(np.stack([vecT(A(v)[0]) for v in (rw_w0, rw_a0, rw_k_k, rw_k_a, rw_r_k, rw_ln_w, rw_ln_b)], 1))
    lora2 = np.zeros((128, 4, 1024), f32)
    lora2[0:64, 0], lora2[0:64, 1] = A(rw_w2)[0], A(rw_a2)[0]
    lora2[0:128, 2], lora2[0:32, 3] = A(rw_g2)[0][0:128], A(rw_g2)[0][128:160]
    glv = np.ascontiguousarray(np.concatenate([vecT(A(gla_gb)[0]), vecT(A(gla_norm_w)[0])], 1))
    gw2 = np.zeros((128, 512), f32)
    gw2[0:16] = A(gla_gw2)[0]
    w_o = _wt(A(w_out)[0], 16)
    w_u = _wt(A(w_up)[0], 64)
    w_d = np.ascontiguousarray(A(w_down)[0].reshape(4, KB, 128, 16, 128).transpose(0, 3, 2, 1, 4))
    sh = A(state_rwkv_shift)[0]
    wkv = A(state_rwkv_wkv)[0]
    gls = A(state_gla)[0]
    shared = {"cst": CST, "gvec": gvec, "w_rw": w_rw, "mu_rw": mu, "rwv": rwv, "lora2": lora2, "w_gl": w_gl,
              "glv": glv, "gw2": gw2, "w_o": w_o, "w_u": w_u, "w_d": w_d}
    in_maps = []
    for c in range(N_CORES):
        b, par = divmod(c, 2)
        own = x_prompt[b, par * HALF:(par + 1) * HALF]
        pre = x_prompt[b, 0:PRE] if par == 1 else np.zeros((PRE, D), f32)
        smp = x_sample[NS * c:NS * (c + 1)].reshape(SMP, D)
        ns = slice(NS * c, NS * (c + 1))
        shT = np.ascontiguousarray(_pad_cols(sh[ns], RW_SEGS).reshape(NS, NRW, 128).transpose(2, 1, 0))
        h0 = np.ascontiguousarray(wkv[ns].transpose(1, 3, 0, 2).reshape(8, 128, NS, 64))
        s0 = np.ascontiguousarray(gls[ns].transpose(1, 2, 0, 3))
        m = dict(shared)
        m.update({"xP": _fm(pre), "xM": _fm(np.concatenate([own, smp], 0)), "shiftT": shT, "rwH0": h0, "glS0": s0})
        in_maps.append(m)
    nc = build_nc()
    res = run_bass_kernel_spmd(nc, in_maps, core_ids=list(range(N_CORES)))

    B = x_prompt.shape[0]
    y_p = np.zeros((B, SEQ, D), f32)
    y_s = np.zeros((DEC_B, DEC_T, D), f32)
    sh_p = np.zeros((1, B, RW_PROJ), f32)
    sh_s = np.zeros((1, DEC_B, RW_PROJ), f32)
    wkv_p = np.zeros((1, B, RW_H, RW_HD, RW_HD), f32)
    wkv_s = np.zeros((1, DEC_B, RW_H, RW_HD, RW_HD), f32)
    gla_p = np.zeros((1, B, 4, 128, 256), f32)
    gla_s = np.zeros((1, DEC_B, 4, 128, 256), f32)
    for c in range(N_CORES):
        b, par = divmod(c, 2)
        r = res.results[c]
        ns = slice(NS * c, NS * (c + 1))
        y = r["yT"].transpose(2, 1, 0).reshape(NTM, D)
        y_p[b, par * HALF:(par + 1) * HALF] = y[:HALF]
        y_s[ns] = y[HALF:].reshape(NS, DEC_T, D)
        rows = r["o_shift"].transpose(2, 1, 0).reshape(1 + NS, NRW * 128)[:, RW_UNPAD]
        sh_s[0, ns] = rows[1:]
        wkv_s[0, ns] = r["o_wkvs"].reshape(8, 2, 64, NS, 64).transpose(3, 0, 1, 4, 2).reshape(NS, 16, 64, 64)
        gla_s[0, ns] = r["o_glas"].transpose(2, 0, 1, 3)
        if par == 1:
            sh_p[0, b] = rows[0]
            wkv_p[0, b] = r["o_wkvp"].reshape(16, 64, 64).transpose(0, 2, 1)
            gla_p[0, b] = r["o_glap"]
    return (y_p, y_s, sh_p, wkv_p, gla_p, sh_s, wkv_s, gla_s)
```

```python
from contextlib import ExitStack

import numpy as np
import concourse.bass as bass
import concourse.mybir as mybir
from concourse.bass_utils import run_bass_kernel_spmd

F32 = mybir.dt.float32
BF16 = mybir.dt.bfloat16
ALU = mybir.AluOpType
AF = mybir.ActivationFunctionType

N_CORES = 8
D = 2048
KB = D // 128
SEQ = 2048
HALF = SEQ // 2
PRE = HALF
DEC_B, DEC_T = 128, 8
SMP = (DEC_B // N_CORES) * DEC_T
RW_W, RW_HD, RW_H = 1024, 64, 16
RW_PROJ = 3 * RW_W + 64 + 64 + 160
NORM_EPS = 1e-6

TOK = PRE + HALF + SMP
TT = 512


AX = mybir.AxisListType
NS = SMP // DEC_T
NTM = HALF + SMP
QT = 256
C0 = 0.6065306597126334
GN_EPS = 64e-5
HN_EPS = 1e-5
NRW = 28
NGL = 25


class Tracker:
    CE = ("tensor", "scalar", "vector")

    def __init__(self, nc, es):
        self.nc, self.ops, self.lw, self.rs = nc, [], {}, {}
        self.sem = {e: es.enter_context(nc.semaphore("c_" + e)) for e in self.CE}
        self.pool = {"sync": [es.enter_context(nc.semaphore(f"ds{i}")) for i in range(16)],
                     "gpsimd": [es.enter_context(nc.semaphore(f"dg{i}")) for i in range(12)]}
        self.bar = {}
        self.seen_after_bar = set()

    def barrier(self):
        last = {}
        for i, o in enumerate(self.ops):
            last[o["eng"]] = i
        self.bar = dict(last)
        self.bar_dma = []
        for q in ("sync", "gpsimd"):
            ids = [i for i, o in enumerate(self.ops) if o["eng"] == q and o["make"] is not None]
            self.bar_dma += ids[-len(self.pool[q]):]
        self.seen_after_bar = set()

    def add(self, eng, make, r=(), w=()):
        i = len(self.ops)
        raw, oth = set(), set()
        for k in r:
            p = self.lw.get(k)
            if p is not None:
                raw.add(p)
        for k in w:
            p = self.lw.get(k)
            if p is not None:
                oth.add(p)
            for q in self.rs.get(k, {}).values():
                oth.add(q)
        deps = set()
        for p in raw | oth:
            pe = self.ops[p]["eng"]
            if pe == eng:
                if eng == "tensor":
                    continue
                if eng in self.CE and p not in raw:
                    continue
            deps.add(p)
        if self.bar and eng not in self.seen_after_bar:
            self.seen_after_bar.add(eng)
            deps |= set(v for e, v in self.bar.items() if e != eng or eng not in self.CE)
            deps |= set(self.bar_dma)
        self.ops.append(dict(eng=eng, make=make, deps=deps, sig=False))
        for k in r:
            d = self.rs.setdefault(k, {})
            d[eng if eng in self.CE else ("dma", i)] = i
        for k in w:
            self.lw[k] = i
            self.rs[k] = {}
        return i

    def emit(self, out_ids):
        ops = self.ops
        fence = self.add("sync", None)
        ops[fence]["deps"] = set(out_ids)
        for o in ops:
            for p in o["deps"]:
                ops[p]["sig"] = True
        cnt = {e: 0 for e in self.CE}
        ndma = {q: 0 for q in self.pool}
        pcnt = {q: [0] * len(self.pool[q]) for q in self.pool}
        for o in ops:
            e = o["eng"]
            o["sv"] = None
            if o["make"] is None:
                continue
            if e in self.CE:
                if o["sig"]:
                    cnt[e] += 1
                    o["sv"] = (self.sem[e], cnt[e], 1)
            else:
                j = ndma[e] % len(self.pool[e])
                ndma[e] += 1
                pcnt[e][j] += 16
                o["sv"] = (self.pool[e][j], pcnt[e][j], 16)
        streams = {}
        for i, o in enumerate(ops):
            streams.setdefault(o["eng"], []).append(i)
        waited = {}
        self.nwaits = 0
        with self.nc.Block() as block:
            for e, ids in streams.items():
                def sec(eng, ids=ids, e=e):
                    for i in ids:
                        o = ops[i]
                        need = {}
                        for p in o["deps"]:
                            sem, val, _ = ops[p]["sv"]
                            k = id(sem)
                            if k not in need or need[k][1] < val:
                                need[k] = (sem, val)
                        for k, (sem, val) in need.items():
                            if waited.get((e, k), 0) < val:
                                eng.wait_ge(sem, val)
                                waited[(e, k)] = val
                                self.nwaits += 1
                        if o["make"] is not None:
                            ins = o["make"](eng)
                            if o["sv"] is not None:
                                ins.then_inc(o["sv"][0], o["sv"][2])
                getattr(block, e)(sec)


def host_consts():
    r = np.arange(128)
    h, s = r // 64, r % 64
    same_h = h[:, None] == h[None, :]
    sU = same_h & (s[:, None] < s[None, :])
    iU = same_h & (s[:, None] <= s[None, :])
    sL = same_h & (s[:, None] > s[None, :])
    seg = (s[:, None] // 8) == (s[None, :] // 8)
    f = lambda m: m.astype(np.float32)
    tabs = {}
    tabs["ident"] = np.eye(128, dtype=np.float32)
    tabs["bones"] = f(same_h)
    tabs["UU_p"] = np.concatenate([f(sU), f(iU)], 1)
    tabs["UU_s"] = np.concatenate([f(sU & seg), f(iU & seg)], 1)
    tabs["sL_p"] = f(sL)
    tabs["sL_s"] = f(sL & seg)
    t = np.arange(QT)
    tabs["scan_p"] = np.tile(f(t % 64 != 0)[None, :], (128, 1))
    tabs["scan_s"] = np.tile(f(t % 8 != 0)[None, :], (128, 1))
    n = np.arange(8)
    segF = f((s[None, None, :] // 8) == n[None, :, None])
    tabs["segF"] = np.tile(segF.reshape(1, 8 * 128), (128, 1))
    tabs["segT"] = f((s[:, None] // 8) == n[None, :])
    s6 = np.arange(128) % 64
    c6 = np.arange(64)
    gi = f(s6[:, None] <= c6[None, :])
    gseg = f((s6[:, None] // 8) == (c6[None, :] // 8))
    tabs["G_p"] = gi
    tabs["G_s"] = gi * gseg
    tabs["gsegF"] = np.tile(f((c6[None, None, :] // 8) == n[None, :, None]).reshape(1, 8 * 64), (128, 1))
    offs, cols, o = {}, [], 0
    for k, v in tabs.items():
        offs[k] = (o, v.shape[1])
        cols.append(v)
        o += v.shape[1]
    return np.ascontiguousarray(np.concatenate(cols, 1)), offs


CST, CST_OFF = host_consts()
NCST = CST.shape[1]


def build_nc():
    nc = bass.Bass("TRN2", target_bir_lowering=False)
    di = lambda n, sh: nc.dram_tensor(n, sh, F32, kind="ExternalInput").ap()
    do = lambda n, sh: nc.dram_tensor(n, sh, F32, kind="ExternalOutput").ap()
    xP, xM = di("xP", [128, KB, PRE]), di("xM", [128, KB, NTM])
    cst = di("cst", [128, NCST])
    gvec = di("gvec", [128, 3, KB])
    w_rw, mu_rw, shiftT = di("w_rw", [NRW, 128, KB, 128]), di("mu_rw", [128, NRW]), di("shiftT", [128, NRW, NS])
    rwv = di("rwv", [128, 7, 8])
    lora2 = di("lora2", [128, 4, 1024])
    rwH0 = di("rwH0", [8, 128, NS, 64])
    w_gl, glv, gw2 = di("w_gl", [NGL, 128, KB, 128]), di("glv", [128, 6]), di("gw2", [128, 512])
    glS0 = di("glS0", [4, 128, NS, 256])
    w_o, w_u, w_d = di("w_o", [16, 128, KB, 128]), di("w_u", [64, 128, KB, 128]), di("w_d", [4, 16, 128, KB, 128])
    yT = do("yT", [128, KB, NTM])
    o_shift = do("o_shift", [128, NRW, 1 + NS])
    o_wkvp, o_wkvs = do("o_wkvp", [8, 128, 64]), do("o_wkvs", [8, 128, NS, 64])
    o_glap, o_glas = do("o_glap", [4, 128, 256]), do("o_glas", [4, 128, NS, 256])

    with ExitStack() as es:
        AR = es.enter_context(nc.sbuf_tensor("arena", [128, 53000], F32))
        PS = [es.enter_context(nc.psum_tensor(f"ps{i}", [128, 512], F32)) for i in range(8)]
        T = Tracker(nc, es)
        top = [0]

        def alloc(n):
            o = top[0]
            top[0] += n
            assert top[0] <= 53000, top[0]
            return o

        def V(o, n, rows=None):
            return AR[:, o:o + n] if rows is None else AR[rows[0]:rows[1], o:o + n]

        o_hT, o_oT = alloc(9216), alloc(9216)
        hT = V(o_hT, 9216).bitcast(BF16).rearrange("p (k t) -> p k t", k=KB)
        oT = V(o_oT, 9216).bitcast(BF16).rearrange("p (k t) -> p k t", k=KB)
        NW = 4
        o_w = alloc(1024 * NW)
        wst = [V(o_w + 1024 * i, 1024).bitcast(BF16).rearrange("p (k f) -> p k f", k=KB) for i in range(NW)]
        o_c = alloc(NCST)
        CT = {k: V(o_c + a, n) for k, (a, n) in CST_OFF.items()}
        gv = V(alloc(48), 48).rearrange("p (a k) -> p a k", a=3)
        mut = V(alloc(NRW), NRW)
        rv = V(alloc(56), 56).rearrange("p (a k) -> p a k", a=7)
        shin = V(alloc(NRW * NS), NRW * NS).rearrange("p (c n) -> p c n", c=NRW)
        shout = V(alloc(NRW * (1 + NS)), NRW * (1 + NS)).rearrange("p (c n) -> p c n", c=NRW)
        carry = V(alloc(NRW), NRW)
        glvt = V(alloc(6), 6)
        nw0, na0, omka, ngb = V(alloc(8), 8), V(alloc(8), 8), V(alloc(8), 8), V(alloc(4), 4)
        epsn, eps24, epsg, epsh, one1 = (V(alloc(1), 1) for _ in range(5))
        gw2b = V(alloc(256), 256).bitcast(BF16)
        o_l2b = alloc(2048)
        l2b = V(o_l2b, 2048).bitcast(BF16).rearrange("p (a f) -> p a f", a=4)
        wst += [V(o_l2b + 1024 * i, 1024).bitcast(BF16).rearrange("p (k f) -> p k f", k=KB) for i in range(2)]
        onesb = V(alloc(64), 64).bitcast(BF16)
        Hrw = V(alloc(1024), 1024).rearrange("p (h c) -> p h c", h=8)
        Sgl = V(alloc(1024), 1024).rearrange("p (g e) -> p g e", g=4)
        ov0 = top[0]

        st = {"eng": 0, "w": 0, "wr": 0}

        def cp(out, in_, r, w, scale=None):
            st["eng"] ^= 1
            if st["eng"]:
                if scale is None:
                    return T.add("scalar", lambda e: e.activation(out=out, in_=in_, func=AF.Copy), r, w)
                return T.add("scalar", lambda e: e.activation(out=out, in_=in_, func=AF.Copy, scale=scale), r, w)
            if scale is None:
                return T.add("vector", lambda e: e.tensor_copy(out, in_), r, w)
            return T.add("vector", lambda e: e.tensor_scalar(out, in_, scale, None, ALU.mult), r, w)

        def act(out, in_, func, r, w, bias=None, scale=1.0, accum=None):
            kw = dict(out=out, in_=in_, func=func, scale=scale)
            if bias is not None:
                kw["bias"] = bias
            if accum is not None:
                kw["accum_out"] = accum
            return T.add("scalar", lambda e: e.activation(**kw), r, w)

        def tt(out, a, b, op, r, w):
            return T.add("vector", lambda e: e.tensor_tensor(out, a, b, op), r, w)

        def ts(out, a, s1, s2, op0, op1, r, w):
            if s2 is None:
                return T.add("vector", lambda e: e.tensor_scalar(out, a, s1, None, op0), r, w)
            return T.add("vector", lambda e: e.tensor_scalar(out, a, s1, s2, op0, op1), r, w)

        def stt(out, a, sc, b, op0, op1, r, w):
            return T.add("vector", lambda e: e.scalar_tensor_tensor(out=out, in0=a, scalar=sc, in1=b, op0=op0, op1=op1), r, w)

        def mm(out, lhsT, rhs, start, stop, r, w):
            return T.add("tensor", lambda e: e.matmul(out, lhsT, rhs, start=start, stop=stop), r, w)

        def tr(out, in_, r, w):
            return T.add("tensor", lambda e: e.transpose(out, in_, identb), r + ["identb"], w)

        def ld(out, in_, w, r=()):
            return T.add("sync", lambda e: e.dma_start(out=out, in_=in_), r, w)

        def ldc(out, in_, w, r=()):
            return T.add("gpsimd", lambda e: e.dma_start(out=out, in_=in_), r, w)

        out_ids = []

        def store(out, in_, r):
            out_ids.append(T.add("sync", lambda e: e.dma_start(out=out, in_=in_), r, ()))

        ld(V(o_c, NCST), cst[:, :], ["cst"])
        ld(gv, gvec[:, :, :], ["gv"])
        ld(mut, mu_rw[:, :], ["mut"])
        ld(rv, rwv[:, :, :], ["rv"])
        ld(shin, shiftT[:, :, :], ["shin"])
        ld(glvt, glv[:, :], ["glv"])
        ldc(gw2b, gw2[:, :], ["gw2b"])
        ldc(l2b, lora2[:, :, :], ["l2b"])
        T.add("vector", lambda e: e.memset(onesb, 1.0), (), ["onesb"])
        for tl, val in ((epsn, NORM_EPS), (eps24, 1e-24), (epsg, GN_EPS), (epsh, HN_EPS), (one1, 1.0)):
            T.add("vector", lambda e, tl=tl, val=val: e.memset(tl, val), (), ["eps"])
        T.add("vector", lambda e: e.memset(carry, 0.0), (), ["carry"])
        T.add("vector", lambda e: e.memset(Hrw, 0.0), (), [("Hrw", h) for h in range(8)])
        T.add("vector", lambda e: e.memset(Sgl, 0.0), (), ["Sgl"])
        T.add("vector", lambda e: e.memset(shout, 0.0), (), ["shout"])
        ts(nw0, rv[:, 0, :], -1.0, None, ALU.mult, None, ["rv"], ["nw0"])
        ts(na0, rv[:, 1, :], -1.0, None, ALU.mult, None, ["rv"], ["na0"])
        ts(omka, rv[:, 3, :], -1.0, 1.0, ALU.mult, ALU.add, ["rv"], ["omka"])
        ts(ngb, glvt[:, 0:4], -1.0, None, ALU.mult, None, ["glv"], ["ngb"])

        lorT = V(alloc(2304), 2304).bitcast(BF16).rearrange("p (a t) -> p a t", a=4)
        xgT = V(alloc(576), 576).bitcast(BF16)
        identb = V(alloc(64), 64).bitcast(BF16)
        Hrwb = V(alloc(512), 512).bitcast(BF16).rearrange("p (h c) -> p h c", h=8)
        T.add("vector", lambda e: e.memset(Hrwb, 0.0), (), [("Hrwb", h) for h in range(8)])
        FTS = []
        for _fp in (0, 1):
            d = {"pb": V(alloc(260), 260)}
            for nm in ("db", "mr", "mk", "mv", "T0", "T1", "T3", "T4", "T5", "T6", "T7", "T8", "T9", "T10", "T11"):
                d[nm] = V(alloc(QT), QT)
            FTS.append(d)
        FT = FTS[0]
        MT, MTO = {}, {}
        mats0 = top[0]

        def mat(nm, n, dt=BF16):
            if dt == BF16:
                MTO[nm] = alloc(n // 2)
                MT[nm] = V(MTO[nm], n // 2).bitcast(BF16)
            else:
                MTO[nm] = alloc(n)
                MT[nm] = V(MTO[nm], n)
            return MT[nm]

        def psb(p):
            return p.bitcast(BF16)

        for nm, n in (("KR", 512), ("Bf", 256), ("Cf", 256), ("BBf", 256), ("KKf", 256), ("Vf", 256), ("YN", 256), ("YN_B", 256)):
            m = mat(nm, n)
            T.add("vector", lambda e, m=m: e.memset(m, 0.0), (), [nm])
        mat("sqt", 256)
        mat("sqt_B", 256)
        for nm, n in (("KZ", 512), ("BBt", 256), ("KKt", 256), ("Vt", 256), ("LA", 512), ("AK", 512),
                      ("Lt", 256), ("X", 256), ("PP", 512), ("QU", 512), ("Mst", 256), ("Rst", 256),
                      ("Hsb", 384), ("MnT", 1024)):
            mat(nm, n)
        for nm, n in (("Hs", 384), ("stat", 64), ("H0f", 1024), ("hf", 128), ("hf_B", 128), ("Hs_B", 384), ("stat_B", 64)):
            mat(nm, n, F32)
        for nm, n in (("KR", 512), ("Bf", 256), ("Cf", 256), ("BBf", 256), ("KKf", 256), ("Vf", 256)):
            m = mat(nm + "_B", n)
            T.add("vector", lambda e, m=m: e.memset(m, 0.0), (), [nm + "_B"])
        for nm, n in (("KZ", 512), ("BBt", 256), ("KKt", 256), ("Vt", 256), ("LA", 512), ("AK", 512),
                      ("Lt", 256), ("X", 256), ("PP", 512), ("QU", 512), ("Mst", 256), ("Rst", 256), ("Hsb", 384)):
            mat(nm + "_B", n)
        for nm, base in (("BBx", "KZ_B"), ("KKx", "LA_B"), ("Rsx", "Lt_B"), ("H0b", "QU_B")):
            MT[nm] = V(MTO[base], 512).bitcast(BF16)
        T.add("vector", lambda e: e.memset(MT["H0f"], 0.0), (), ["H0f"])
        T.add("vector", lambda e: e.tensor_copy(identb, CT["ident"]), ["cst"], ["identb"])
        mix_top = top[0]
        assert mix_top - mats0 >= 6656, (mix_top, mats0)

        def norm_tokens(src, ntok, gidx, dst, dkey, final_out=None, base=None):
            resident = isinstance(src, tuple)
            o = base
            if not resident:
                xsb = [AR[:, o - 4096 * i:o - 4096 * i + 4096].rearrange("p (k t) -> p k t", k=KB) for i in (0, 1)]
                o += 4096
            sq = AR[:, o:o + 2048].bitcast(BF16).rearrange("p (k t) -> p k t", k=KB)
            lnb = AR[:, o + 2048:o + 2304]
            rsd = AR[:, o + 2304:o + 2560]
            for t0 in range(0, ntok, 256):
                tw = min(256, ntok - t0)
                if resident:
                    xv = src[0][:, :, t0:t0 + tw]
                    xk = lambda k: [src[1][k]]
                else:
                    xi = (t0 // 256) % 2
                    xs = xsb[xi]
                    ld(xs[:, :, 0:tw], src[:, :, t0:t0 + tw], ["xs%d" % xi])
                    xv = xs[:, :, 0:tw]
                    xk = lambda k, xi=xi: ["xs%d" % xi]
                for k in range(KB):
                    act(sq[:, k, 0:tw], xv[:, k, :], AF.Square, xk(k), [("sq", k)])
                for k in range(KB):
                    mm(PS[0][:, 0:tw], onesb, sq[:, k, 0:tw], k == 0, k == KB - 1, ["onesb", ("sq", k)], ["ps0"])
                act(lnb[:, 0:tw], PS[0][:, 0:tw], AF.Ln, ["ps0", "eps"], ["lnb"], bias=epsn, scale=1.0 / D)
                act(rsd[:, 0:tw], lnb[:, 0:tw], AF.Exp, ["lnb"], ["rsd"], scale=-0.5)
                for k in range(KB):
                    if final_out is None:
                        stt(dst[:, k, t0:t0 + tw], xv[:, k, :], gv[:, gidx, k:k + 1], rsd[:, 0:tw],
                            ALU.mult, ALU.mult, xk(k) + ["gv", "rsd"], [dkey])
                    else:
                        stt(xv[:, k, :], xv[:, k, :], gv[:, gidx, k:k + 1], rsd[:, 0:tw],
                            ALU.mult, ALU.mult, xk(k) + ["gv", "rsd"], xk(k))
                if final_out is not None:
                    store(final_out[:, :, t0:t0 + tw], xv, [kk for k in range(KB) for kk in xk(k)])

        def wload(src):
            i = st["w"] % NW
            st["w"] += 1
            ldc(wst[i], src, [("w", i)])
            return i

        wst += [V(o_oT + 4608 + 1024 * i, 1024).bitcast(BF16).rearrange("p (k f) -> p k f", k=KB) for i in range(2)]
        RW_SLOTS = [0, 1, 2, 3, 6, 7]

        def wload_rw(src):
            i = RW_SLOTS[st["wr"] % len(RW_SLOTS)]
            st["wr"] += 1
            ldc(wst[i], src, [("w", i)])
            return i

        def proj(wi, tc0, ntok, ps):
            for k in range(KB):
                mm(ps[:, 0:ntok], wst[wi][:, k, :], hT[:, k, tc0:tc0 + ntok], k == 0, k == KB - 1,
                   [("w", wi), "hT"], [ps_key(ps)])

        def ps_key(ps):
            for i, p in enumerate(PS):
                if p is ps:
                    return f"ps{i}"
            raise KeyError

        pp = {"i": 0}

        def proj_ps():
            pp["i"] ^= 1
            return PS[pp["i"]]

        def shift(c, ps, grp, mdst, mkey, fp=0):
            tc0, ntok, kind, last_own = grp
            pb, db = FTS[fp]["pb"], FTS[fp]["db"]
            kpb, kdb = "pb#%d" % fp, "db#%d" % fp
            cp(pb[:, 0:1], carry[:, c:c + 1], ["carry"], [kpb])
            cp(pb[:, 1:1 + ntok], ps[:, 0:ntok], [ps_key(ps)], [kpb])
            if kind == "p":
                tt(db[:, 0:ntok], pb[:, 0:ntok], pb[:, 1:1 + ntok], ALU.subtract, [kpb], [kdb])
                cp(carry[:, c:c + 1], pb[:, ntok:ntok + 1], [kpb], ["carry"])
                if last_own:
                    cp(shout[:, c, 0:1], pb[:, ntok:ntok + 1], [kpb], ["shout"])
            else:
                cp(db[:, 0:ntok], pb[:, 0:ntok], [kpb], [kdb])
                cp(db[:, 0:ntok].rearrange("p (n t) -> p n t", t=DEC_T)[:, :, 0], shin[:, c, :], ["shin", kdb], [kdb])
                tt(db[:, 0:ntok], db[:, 0:ntok], pb[:, 1:1 + ntok], ALU.subtract, [kpb, kdb], [kdb])
                cp(shout[:, c, 1:1 + NS], pb[:, 1:1 + ntok].rearrange("p (n t) -> p n t", t=DEC_T)[:, :, DEC_T - 1],
                   [kpb], ["shout"])
            stt(mdst[:, 0:ntok], db[:, 0:ntok], mut[:, c:c + 1], pb[:, 1:1 + ntok], ALU.mult, ALU.add,
                [kdb, kpb, "mut"], [mkey])

        def groups(mode):
            if mode == "P":
                return [(t, QT, "p", False) for t in range(0, PRE, QT)]
            return [(t, QT, "p", t + QT == HALF) for t in range(0, HALF, QT)] + [(HALF, SMP, "s", False)]

        pmi = {"i": 0}
        LORK = [("lorT", c) for c in range(24, 28)]
        proj_done = set()
        sub_lock = {"busy": False}

        pinned = set()

        def pm(pin=False):
            while True:
                pmi["i"] = (pmi["i"] + 1) % 6
                if pmi["i"] not in pinned:
                    break
            if pin:
                pinned.add(pmi["i"])
            return PS[2 + pmi["i"]]

        def unpin(ps):
            for i in range(6):
                if PS[2 + i] is ps:
                    pinned.discard(i)

        def b3(ap, n, m):
            return ap.rearrange("p (n m) -> p n m", n=n)

        def bc_mid(ap2, n):
            return ap2.unsqueeze(1).to_broadcast([ap2.shape[0], n, ap2.shape[1]])

        def bc_last(ap2, m):
            return ap2.unsqueeze(2).to_broadcast([ap2.shape[0], ap2.shape[1], m])

        def lora_gen(lc, mode, fp):
            k0, k1 = "T0#%d" % fp, "T1#%d" % fp
            wi = wload(w_rw[lc])
            for grp in groups(mode):
                tc0, N, kind, _ = grp
                ps = proj_ps()
                proj(wi, tc0, N, ps)
                shift(lc, ps, grp, FTS[fp]["T0"], k0, fp)
                yield
                t0, t1 = FTS[fp]["T0"][:, 0:N], FTS[fp]["T1"][:, 0:N]
                dst = lorT[:, lc - 24, tc0:tc0 + N]
                if lc == 24:
                    act(t1, t0, AF.Exp, [k0], [k1], scale=2.0)
                    act(t1, t1, AF.Ln, [k1, "eps"], [k1], bias=one1)
                    act(t1, t1, AF.Exp, [k1], [k1], scale=-1.0)
                    ts(dst, t1, -2.0, 1.0, ALU.mult, ALU.add, [k1], [("lorT", lc)])
                elif lc == 25:
                    cp(dst, t0, [k0], [("lorT", lc)])
                else:
                    act(t1, t0, AF.Exp, [k0], [k1], scale=-1.0)
                    act(t1, t1, AF.Ln, [k1, "eps"], [k1], bias=one1)
                    act(dst, t1, AF.Exp, [k1], [("lorT", lc)], scale=-1.0)
                yield

        def lora_stage(mode):
            for pair in ((24, 25), (26, 27)):
                gens = [lora_gen(lc, mode, i) for i, lc in enumerate(pair)]
                while gens:
                    for gg in list(gens):
                        try:
                            next(gg)
                        except StopIteration:
                            gens.remove(gg)

        def rw_group(hp, grp, main, wk, wv, wr, fp, tag):
            fk = lambda n: n + "#%d" % fp
            FT = FTS[fp]
            tc0, N, kind, _ = grp
            L = 64 if kind == "p" else 8
            nseg = N // L
            F = {k: v[:, 0:N] for k, v in FT.items() if k != "pb"}
            hs = slice(hp, hp + 1)
            hcols = slice(hp * 128, (hp + 1) * 128)
            ps = proj_ps(); proj(wk, tc0, N, ps); shift(8 + hp, ps, grp, FT["mk"], fk("mk"), fp); yield
            ps = proj_ps(); proj(wv, tc0, N, ps); shift(16 + hp, ps, grp, FT["mv"], fk("mv"), fp); yield
            if main:
                ps = proj_ps(); proj(wr, tc0, N, ps); shift(hp, ps, grp, FT["mr"], fk("mr"), fp); yield
            proj_done.add(tag)
            scanm = CT["scan_p" if kind == "p" else "scan_s"][:, 0:N]
            p1 = pm(); k1 = ps_key(p1)
            mm(p1[:, 0:N], l2b[:, 0, hcols], lorT[:, 0, tc0:tc0 + N], True, True, ["l2b"] + LORK, [k1])
            act(F["T0"], p1[:, 0:N], AF.Exp, [k1, "nw0"], [fk("T0")], bias=nw0[:, hs], scale=-1.0)
            act(F["T0"], F["T0"], AF.Ln, [fk("T0"), "eps"], [fk("T0")], bias=one1)
            act(F["T0"], F["T0"], AF.Exp, [fk("T0")], [fk("T0")], scale=-1.0)
            T.add("vector", lambda e: e.tensor_tensor_scan(F["T1"], scanm, F["T0"], 0.0, ALU.mult, ALU.add),
                  ["cst", fk("T0")], [fk("T1")])
            tt(F["T0"], F["T1"], F["T0"], ALU.subtract, [fk("T0"), fk("T1")], [fk("T0")])
            act(F["T3"], F["T1"], AF.Exp, [fk("T1")], [fk("T3")], scale=-C0)
            act(F["T4"], F["T1"], AF.Exp, [fk("T1")], [fk("T4")], scale=C0)
            act(F["T0"], F["T0"], AF.Exp, [fk("T0")], [fk("T0")], scale=-C0)
            yield
            p2 = pm(); k2 = ps_key(p2)
            mm(p2[:, 0:N], l2b[:, 1, hcols], lorT[:, 1, tc0:tc0 + N], True, True, ["l2b"] + LORK, [k2])
            act(F["T1"], p2[:, 0:N], AF.Exp, [k2, "na0"], [fk("T1")], bias=na0[:, hs], scale=-1.0)
            act(F["T1"], F["T1"], AF.Ln, [fk("T1"), "eps"], [fk("T1")], bias=one1)
            act(F["T1"], F["T1"], AF.Exp, [fk("T1")], [fk("T1")], scale=-1.0)
            yield
            ts(F["T5"], F["mk"], rv[:, 2, hs], None, ALU.mult, None, [fk("mk"), "rv"], [fk("T5")])
            act(F["T6"], F["T5"], AF.Square, [fk("T5")], [fk("T6")])
            p3 = pm(); k3 = ps_key(p3)
            mm(p3[:, 0:N], CT["bones"], F["T6"], True, True, ["cst", fk("T6")], [k3])
            act(F["T6"], p3[:, 0:N], AF.Ln, [k3, "eps"], [fk("T6")], bias=eps24)
            act(F["T6"], F["T6"], AF.Exp, [fk("T6")], [fk("T6")], scale=-0.5)
            tt(F["T5"], F["T5"], F["T6"], ALU.mult, [fk("T5"), fk("T6")], [fk("T5")])
            ts(F["T6"], F["T1"], rv[:, 3, hs], omka[:, hs], ALU.mult, ALU.add, [fk("T1"), "rv", "omka"], [fk("T6")])
            tt(F["T6"], F["mk"], F["T6"], ALU.mult, [fk("mk"), fk("T6")], [fk("T6")])
            tt(F["T7"], F["T5"], F["T1"], ALU.mult, [fk("T5"), fk("T1")], [fk("T7")])
            yield
            if main:
                stt(F["T8"], F["mr"], rv[:, 4, hs], F["T6"], ALU.mult, ALU.mult, [fk("mr"), "rv", fk("T6")], [fk("T8")])
                p4 = pm(); k4 = ps_key(p4)
                mm(p4[:, 0:N], CT["bones"], F["T8"], True, True, ["cst", fk("T8")], [k4])
                tt(F["T8"], p4[:, 0:N], F["mv"], ALU.mult, [k4, fk("mv")], [fk("T8")])
                p5 = pm(); k5 = ps_key(p5)
                mm(p5[:, 0:N], l2b[:, 2, hcols], lorT[:, 2, tc0:tc0 + N], True, False, ["l2b"] + LORK, [k5])
                mm(p5[:, 0:N], l2b[:, 3, hcols], lorT[:, 3, tc0:tc0 + N], False, True, ["l2b"] + LORK, [k5])
                cp(F["T9"], p5[:, 0:N], [k5], [fk("T9")])
                tt(F["mr"], F["mr"], F["T3"], ALU.mult, [fk("mr"), fk("T3"), fk("T8")], [fk("mr")])
            yield
            tt(F["T7"], F["T7"], F["T4"], ALU.mult, [fk("T7"), fk("T4")], [fk("T7")])
            tt(F["T6"], F["T6"], F["T4"], ALU.mult, [fk("T6"), fk("T4"), fk("T8")], [fk("T6")])
            tt(F["T5"], F["T5"], F["T0"], ALU.mult, [fk("T5"), fk("T0")], [fk("T5")])
            wend = bc_last(b3(F["T3"], nseg, L)[:, :, L - 1], L)
            tt(b3(F["T10"], nseg, L), b3(F["T7"], nseg, L), wend, ALU.mult, [fk("T7"), fk("T3")], [fk("T10")])
            tt(b3(F["T11"], nseg, L), b3(F["T6"], nseg, L), wend, ALU.mult, [fk("T6"), fk("T3")], [fk("T11")])
            yield
            while sub_lock["busy"]:
                yield
            sub_lock["busy"] = True
            gens = [rw_sub(hp, grp, main, sub, F, ("", "_B")[sub], fk) for sub in range(N // 128)]
            in_tail, held = set(), True
            while gens:
                for gsub in list(gens):
                    try:
                        if next(gsub) == "tail":
                            in_tail.add(id(gsub))
                    except StopIteration:
                        gens.remove(gsub)
                if held and all(id(gsub) in in_tail for gsub in gens):
                    sub_lock["busy"] = False
                    held = False
                yield
            if main:
                stt(F["db"], F["db"], rv[:, 5, hs], F["T8"], ALU.mult, ALU.add, [fk("db"), "rv", fk("T8")], [fk("db")])
                stt(oT[:, hp, tc0:tc0 + N], F["db"], rv[:, 6, hs], F["T9"], ALU.add, ALU.mult,
                    [fk("db"), "rv", fk("T9")], [("oT", hp)])
                if grp[3]:
                    store(o_wkvp[hp, 0:64, :], Hrw[0:64, hp, 0:64], [("Hrw", hp)])
                    store(o_wkvp[hp, 64:128, :], Hrw[64:128, hp, 64:128], [("Hrw", hp)])

        def rw_sub(hp, grp, main, sub, F, sfx, fk):
            kn = lambda n: n + sfx
            tc0, N, kind, _ = grp
            c0 = sub * 128
            KR, KZ, LA, AK, QU = (b3(MT[kn(n)], 2, 256) for n in ("KR", "KZ", "LA", "AK", "QU"))
            Bf, Cf, BBf, KKf, Vf, BBt, KKt, Vt, Lt, X, Mst, Rst = (
                b3(MT[kn(n)], 2, 128) for n in ("Bf", "Cf", "BBf", "KKf", "Vf", "BBt", "KKt", "Vt", "Lt", "X", "Mst", "Rst"))
            PP = b3(MT[kn("PP")], 2, 256)
            stat = MT[kn("stat")]
            srcs = [("T5", KR, 0, kn("KR")), ("T7", Bf, 0, kn("Bf")), ("T6", Cf, 0, kn("Cf")), ("T10", BBf, 0, kn("BBf")),
                    ("T11", KKf, 0, kn("KKf")), ("mv", Vf, 0, kn("Vf"))]
            if main:
                srcs.append(("mr", KR, 128, kn("KR")))
            for nm, dst, co, dk in srcs:
                for h in (0, 1):
                    rows = slice(h * 64, h * 64 + 64)
                    cp(dst[rows, :, co + h * 64:co + h * 64 + 64],
                       F[nm][rows, c0:c0 + 128].rearrange("p (q s) -> p q s", q=2), [fk(nm)], [dk])
            yield
            for src, sk, dst, dk, co in ((KR, kn("KR"), KZ, kn("KZ"), 0), (BBf, kn("BBf"), BBt, kn("BBt"), 0),
                                         (KKf, kn("KKf"), KKt, kn("KKt"), 0), (Vf, kn("Vf"), Vt, kn("Vt"), 0)):
                p = pm(); k = ps_key(p)
                for q in (0, 1):
                    tr(psb(p)[:, q * 128:(q + 1) * 128], src[:, q, 0:128], [sk], [k])
                cp(dst[:, :, 0:128], b3(psb(p)[:, 0:256], 2, 128), [k], [dk])
            yield
            UU = CT["UU_p" if kind == "p" else "UU_s"]
            sL = CT["sL_p" if kind == "p" else "sL_s"]
            W = 256 if main else 128
            for lhs, lk, dst, dk in ((Bf, kn("Bf"), LA, kn("LA")), (Cf, kn("Cf"), AK, kn("AK"))):
                p = pm(); k = ps_key(p)
                for q in (0, 1):
                    mm(p[:, q * 256:q * 256 + W], lhs[:, q, :], KR[:, q, 0:W], True, True, [lk, kn("KR")], [k])
                tt(dst[:, :, 0:W], b3(p, 2, 256)[:, :, 0:W], bc_mid(UU[:, 0:W], 2), ALU.mult, [k, "cst"], [dk])
            p = pm(); k = ps_key(p)
            for q in (0, 1):
                mm(p[:, q * 128:(q + 1) * 128], KR[:, q, 0:128], Bf[:, q, :], True, True, [kn("KR"), kn("Bf")], [k])
            tt(Lt, b3(p[:, 0:256], 2, 128), bc_mid(sL, 2), ALU.mult, [k, "cst"], [kn("Lt")])
            yield
            p = pm(); k = ps_key(p)
            for q in (0, 1):
                mm(p[:, q * 256:q * 256 + 128], Lt[:, q, :], LA[:, q, 0:128], True, True, [kn("Lt"), kn("LA")], [k])
                mm(p[:, q * 256 + 128:q * 256 + 256], LA[:, q, 0:128], Lt[:, q, :], True, True, [kn("Lt"), kn("LA")], [k])
            act(PP, b3(p, 2, 256), AF.Copy, [k], [kn("PP")])
            tt(X, bc_mid(CT["ident"], 2), LA[:, :, 0:128], ALU.subtract, ["cst", kn("LA")], [kn("X")])
            for lvl in range(5):
                yield
                p = pm(); k = ps_key(p)
                for q in (0, 1):
                    mm(p[:, q * 128:(q + 1) * 128], PP[:, q, 128:256], X[:, q, :], True, True, [kn("PP"), kn("X")], [k])
                tt(X, X, b3(p[:, 0:256], 2, 128), ALU.add, [kn("X"), k], [kn("X")])
                if lvl < 4:
                    p = pm(); k = ps_key(p)
                    for q in (0, 1):
                        mm(p[:, q * 256:q * 256 + 128], PP[:, q, 128:256], PP[:, q, 0:128], True, True, [kn("PP")], [k])
                        mm(p[:, q * 256 + 128:q * 256 + 256], PP[:, q, 0:128], PP[:, q, 128:256], True, True, [kn("PP")], [k])
                    act(PP, b3(p, 2, 256), AF.Copy, [k], [kn("PP")])
            yield
            p = pm(); k = ps_key(p)
            for q in (0, 1):
                mm(p[:, q * 128:(q + 1) * 128], AK[:, q, 0:128], Vt[:, q, :], True, True, [kn("AK"), kn("Vt")], [k])
            cp(KZ[:, :, 128:256], b3(p[:, 0:256], 2, 128), [k], [kn("KZ")])
            p = pm(); k = ps_key(p)
            for q in (0, 1):
                mm(p[:, q * 256:(q + 1) * 256], X[:, q, :], KZ[:, q, :], True, True, [kn("X"), kn("KZ")], [k])
            cp(QU, b3(p, 2, 256), [k], [kn("QU")], scale=-1.0)
            yield
            if main:
                p = pm(); k = ps_key(p)
                for q in (0, 1):
                    mm(p[:, q * 128:(q + 1) * 128], QU[:, q, 0:128], LA[:, q, 128:256], True, True, [kn("QU"), kn("LA")], [k])
                tt(Rst, b3(p[:, 0:256], 2, 128), KR[:, :, 128:256], ALU.add, [k, kn("KR")], [kn("Rst")])
            pY = None
            yield
            if kind == "p":
                p = pm(); k = ps_key(p)
                for q in (0, 1):
                    mm(p[:, q * 128:(q + 1) * 128], QU[:, q, 0:128], BBt[:, q, :], True, True, [kn("QU"), kn("BBt")], [k])
                cp(Mst, b3(p[:, 0:256], 2, 128), [k], [kn("Mst")])
                Hs, Hsb = b3(MT[kn("Hs")], 3, 128), b3(MT[kn("Hsb")], 3, 128)
                st_f = [Hrw[:, hp, :], Hs[:, 1, :]]
                st_b = [Hrwb[:, hp, :], Hsb[:, 1, :]]
                kf = [("Hrw", hp), (kn("Hs"), 1)]
                kb = [("Hrwb", hp), (kn("Hsb"), 1)]
                if main:
                    pY = pm(pin=True); kY = ps_key(pY)
                for q in (0, 1):
                    if main:
                        yo = pY[:, q * 128:(q + 1) * 128]
                        mm(yo, LA[:, q, 128:256], QU[:, q, 128:256], True, False, [kn("LA"), kn("QU")], [kY])
                        mm(yo, AK[:, q, 128:256], Vt[:, q, :], False, False, [kn("AK"), kn("Vt")], [kY])
                        mm(yo, Rst[:, q, :], st_b[q], False, True, [kn("Rst"), kb[q]], [kY])
                    p = pm(); k = ps_key(p)
                    mm(p[:, 0:128], BBt[:, q, :], QU[:, q, 128:256], True, False, [kn("BBt"), kn("QU")], [k])
                    mm(p[:, 0:128], KKt[:, q, :], Vt[:, q, :], False, False, [kn("KKt"), kn("Vt")], [k])
                    mm(p[:, 0:128], Mst[:, q, :], st_b[q], False, True, [kn("Mst"), kb[q]], [k])
                    wc = F["T3"][:, c0 + q * 64 + 63:c0 + q * 64 + 64]
                    stt(st_b[1 - q], st_f[q], wc, p[:, 0:128], ALU.mult, ALU.add, [kf[q], fk("T3"), k], [kb[1 - q]])
                    stt(st_f[1 - q], st_f[q], wc, p[:, 0:128], ALU.mult, ALU.add, [kf[q], fk("T3"), k], [kf[1 - q]])
            else:
                BBx, KKx, Rsx, H0b, H0f = (b3(MT[n], 8, 128) for n in ("BBx", "KKx", "Rsx", "H0b", "H0f"))
                kBBx, kKKx, kRsx, kH0b = ["KZ_B", "BBt_B", "KKt_B"], ["LA_B", "AK_B"], ["Lt_B", "X_B", "PP_B"], ["QU_B", "Mst_B", "Rst_B"]
                segT, segF = CT["segT"], b3(CT["segF"], 8, 128)
                pY = pm(pin=True); kY = ps_key(pY)
                for q in (0, 1):
                    n0 = (sub * 2 + q) * 8
                    ld(H0f[0:64, :, 0:64], rwH0[hp, 0:64, n0:n0 + 8, :], ["H0f"])
                    ld(H0f[64:128, :, 64:128], rwH0[hp, 64:128, n0:n0 + 8, :], ["H0f"])
                    cp(H0b, H0f, ["H0f"], kH0b)
                    tt(BBx, bc_mid(BBt[:, q, :], 8), bc_last(segT, 128), ALU.mult, [kn("BBt"), "cst"], kBBx)
                    tt(KKx, bc_mid(KKt[:, q, :], 8), bc_last(segT, 128), ALU.mult, [kn("KKt"), "cst"], kKKx)
                    tt(Rsx, bc_mid(Rst[:, q, :], 8), segF, ALU.mult, [kn("Rst"), "cst"], kRsx)
                    yo = pY[:, q * 128:(q + 1) * 128]
                    mm(yo, LA[:, q, 128:256], QU[:, q, 128:256], True, False, [kn("LA"), kn("QU")], [kY])
                    mm(yo, AK[:, q, 128:256], Vt[:, q, :], False, False, [kn("AK"), kn("Vt")], [kY])
                    for n in range(8):
                        mm(yo, Rsx[:, n, :], H0b[:, n, :], False, n == 7, kRsx + kH0b, [kY])
                    MnT8 = b3(MT["MnT"], 8, 128)
                    pM = [pm(pin=True), pm(pin=True)]
                    for n in range(8):
                        mm(pM[n // 4][:, (n % 4) * 128:(n % 4 + 1) * 128], QU[:, q, 0:128], BBx[:, n, :], True, True,
                           [kn("QU")] + kBBx, [ps_key(pM[n // 4])])
                    for hf in (0, 1):
                        cp(MnT8[:, hf * 4:hf * 4 + 4, :], b3(pM[hf], 4, 128), [ps_key(pM[hf])], [("MnT", hf)])
                        unpin(pM[hf])
                    pS = [pm(pin=True), pm(pin=True)]
                    for n in range(8):
                        po = pS[n // 4]; ko = ps_key(po)
                        oo = po[:, (n % 4) * 128:(n % 4 + 1) * 128]
                        mm(oo, BBx[:, n, :], QU[:, q, 128:256], True, False, kBBx + [kn("QU")], [ko])
                        mm(oo, KKx[:, n, :], Vt[:, q, :], False, False, kKKx + [kn("Vt")], [ko])
                        mm(oo, MnT8[:, n, :], H0b[:, n, :], False, True, [("MnT", n // 4)] + kH0b, [ko])
                    wseg = b3(F["T3"][:, c0 + q * 64:c0 + q * 64 + 64], 8, 8)[:, :, 7]
                    for hf in (0, 1):
                        hv = H0f[:, hf * 4:hf * 4 + 4, :]
                        tt(hv, hv, bc_last(wseg[:, hf * 4:hf * 4 + 4], 128), ALU.mult, ["H0f", fk("T3")] + kH0b, ["H0f"])
                        tt(hv, hv, b3(pS[hf], 4, 128), ALU.add, ["H0f", ps_key(pS[hf])], ["H0f"])
                    unpin(pS[0]); unpin(pS[1])
                    store(o_wkvs[hp, 0:64, n0:n0 + 8, :], H0f[0:64, :, 0:64], ["H0f"])
                    store(o_wkvs[hp, 64:128, n0:n0 + 8, :], H0f[64:128, :, 64:128], ["H0f"])
            yield "tail"
            if main:
                YN = b3(MT[kn("YN")], 2, 128)
                yv = b3(pY[:, 0:256], 2, 128)
                T.add("vector", lambda e: e.tensor_reduce(out=stat[:, 0:2], in_=yv, axis=AX.X, op=ALU.add), [kY], [kn("stat")])
                sqt = b3(MT[kn("sqt")], 2, 128)
                act(MT[kn("sqt")], pY[:, 0:256], AF.Square, [kY], [kn("sqt")])
                T.add("vector", lambda e: e.tensor_reduce(out=stat[:, 2:4], in_=sqt, axis=AX.X, op=ALU.add), [kn("sqt")], [kn("stat")])
                ts(stat[:, 4:6], stat[:, 0:2], 1.0 / 64, None, ALU.mult, None, [kn("stat")], [kn("stat")])
                tt(stat[:, 6:8], stat[:, 4:6], stat[:, 4:6], ALU.mult, [kn("stat")], [kn("stat")])
                stt(stat[:, 8:10], stat[:, 2:4], 1.0 / 64, stat[:, 6:8], ALU.mult, ALU.subtract, [kn("stat")], [kn("stat")])
                act(stat[:, 10:12], stat[:, 8:10], AF.Ln, [kn("stat"), "eps"], [kn("stat")], bias=epsg)
                act(stat[:, 12:14], stat[:, 10:12], AF.Exp, [kn("stat")], [kn("stat")], scale=-0.5)
                stt(stat[:, 14:16], stat[:, 4:6], -1.0, stat[:, 12:14], ALU.mult, ALU.mult, [kn("stat")], [kn("stat")])
                for h in (0, 1):
                    rows = slice(h * 64, h * 64 + 64)
                    cs = slice(h * 64, h * 64 + 64)
                    tt(YN[rows, :, cs], yv[rows, :, cs], bc_last(stat[rows, 12:14], 64), ALU.mult, [kY, kn("stat")], [kn("YN")])
                    tt(YN[rows, :, cs], YN[rows, :, cs], bc_last(stat[rows, 14:16], 64), ALU.add, [kn("YN"), kn("stat")], [kn("YN")])
                p = pm(); k = ps_key(p)
                for q in (0, 1):
                    tr(psb(p)[:, q * 128:(q + 1) * 128], YN[:, q, :], [kn("YN")], [k])
                pv = b3(psb(p)[:, 0:256], 2, 128)
                half = MT[kn("hf")].rearrange("p (q s) -> p q s", q=2)
                cp(half, pv[:, :, 0:64], [k], [kn("hf")])
                tt(F["db"][:, c0:c0 + 128].rearrange("p (q s) -> p q s", q=2), half, pv[:, :, 64:128], ALU.add,
                   [kn("hf"), k], [fk("db")])
                unpin(pY)


        def trn(out, in_, np_, r, w):
            return T.add("tensor", lambda e: e.transpose(out, in_, CT["ident"][0:np_, 0:np_]), r + ["cst"], w)

        gbase = [mats0]

        def galloc(n):
            o = gbase[0]
            gbase[0] += n
            assert gbase[0] <= mix_top
            return V(o, n)

        GK, GV, GA, GON, GSs, GS0, GKx, GQx, GST = (galloc(n) for n in (128, 256, 64, 512, 768, 2048, 512, 256, 64))
        GSET = {"": (GK, GV, GA, GON, GSs, GST, galloc(384)), "_B": tuple(galloc(n) for n in (128, 256, 64, 512, 768, 64, 384))}
        GS0b = galloc(1024)

        def gla_xgate(mode):
            wi = wload(w_gl[16])
            for grp in groups(mode):
                tc0, N, kind, _ = grp
                ps = proj_ps()
                proj(wi, tc0, N, ps)
                cp(xgT[:, tc0:tc0 + N], ps[:, 0:N], [ps_key(ps)], ["xgT"])

        gl_lock = {"busy": False}

        def gl_group(g, grp, main, W, fp, tag):
            fk = lambda n: n + "#%d" % fp
            FT = FTS[fp]
            tc0, N, kind, _ = grp
            L = 64 if kind == "p" else 8
            nseg = N // L
            F = {k: v[:, 0:N] for k, v in FT.items() if k != "pb"}
            F.update({fk(k): v for k, v in list(F.items())})
            gs = slice(g, g + 1)
            scanm = CT["scan_p" if kind == "p" else "scan_s"][:, 0:N]
            p = pm(); k = ps_key(p)
            mm(p[:, 0:N], gw2b[:, g * 128:(g + 1) * 128], xgT[:, tc0:tc0 + N], True, True, ["gw2b", "xgT"], [k])
            act(F["T0"], p[:, 0:N], AF.Exp, [k, "ngb"], [fk("T0")], bias=ngb[:, gs], scale=-1.0)
            act(F["T0"], F["T0"], AF.Ln, [fk("T0"), "eps"], [fk("T0")], bias=one1)
            T.add("vector", lambda e: e.tensor_tensor_scan(F["T1"], scanm, F["T0"], 0.0, ALU.mult, ALU.add),
                  ["cst", fk("T0")], [fk("T1")])
            act(F["T3"], F["T1"], AF.Exp, [fk("T1")], [fk("T3")], scale=-1.0 / 16)
            act(F["T4"], F["T1"], AF.Exp, [fk("T1")], [fk("T4")], scale=1.0 / 16)
            yield
            wi = W["k"]; ps = proj_ps(); proj(wi, tc0, N, ps)
            tt(F["T5"], ps[:, 0:N], F["T4"], ALU.mult, [ps_key(ps), fk("T4")], [fk("T5")])
            wend = bc_last(b3(F["T3"], nseg, L)[:, :, L - 1], L)
            tt(b3(F["T6"], nseg, L), b3(F["T5"], nseg, L), wend, ALU.mult, [fk("T5"), fk("T3")], [fk("T6")])
            for hf, nm in ((0, fk("mk")), (1, fk("mv"))):
                yield
                wi = W["v%d" % hf]; ps = proj_ps(); proj(wi, tc0, N, ps)
                cp(F[nm], ps[:, 0:N], [ps_key(ps)], [nm])
            yield
            if main:
                wi = W["q"]; ps = proj_ps(); proj(wi, tc0, N, ps)
                stt(F["T7"], ps[:, 0:N], 128.0 ** -0.5, F["T3"], ALU.mult, ALU.mult, [ps_key(ps), fk("T3")], [fk("T7")])
                cp(F["T0"].bitcast(BF16)[:, 0:N], F["T5"], [fk("T5")], [fk("T0")])
                cp(F["T1"].bitcast(BF16)[:, 0:N], F["T7"], [fk("T7")], [fk("T1")])
                for hf, nm, tn in ((0, fk("T8"), fk("T10")), (1, fk("T9"), fk("T11"))):
                    yield
                    wi = W["go%d" % hf]; ps = proj_ps(); proj(wi, tc0, N, ps)
                    cp(F[nm], ps[:, 0:N], [ps_key(ps)], [nm])
                    act(F[tn], F[nm], AF.Exp, [nm], [tn], scale=-1.0)
                    act(F[tn], F[tn], AF.Ln, [tn, "eps"], [tn], bias=one1)
                    act(F[tn], F[tn], AF.Exp, [tn], [tn], scale=-1.0)
                    tt(F[nm], F[nm], F[tn], ALU.mult, [nm, tn], [nm])
            proj_done.add(tag)
            yield
            while gl_lock["busy"]:
                yield
            gl_lock["busy"] = True
            def gl_sub(sub, sfx):
                kn = lambda n: n + sfx
                GK, GV, GA, GON, GSs, GST, GSsb = GSET[sfx]
                GKb, GVb, GAb = GK.bitcast(BF16), GV.bitcast(BF16), GA.bitcast(BF16)
                T5b, T7b = F["T0"].bitcast(BF16), F["T1"].bitcast(BF16)
                c0 = sub * 128
                GKv, GVv, GAv, ONv = b3(GKb, 2, 128), b3(GVb, 2, 256), b3(GAb, 2, 64), b3(GON, 2, 256)
                Ss, Ssb = b3(GSs, 3, 256), b3(GSsb.bitcast(BF16), 3, 256)
                p = pm(); k = ps_key(p)
                for q in (0, 1):
                    trn(p[0:64, q * 128:(q + 1) * 128], F["T6"][:, c0 + q * 64:c0 + q * 64 + 64], 128, [fk("T6")], [k])
                cp(GKb[0:64, :], p[0:64, 0:256], [k], [kn("GK")])
                p = pm(); k = ps_key(p)
                for q in (0, 1):
                    for hf, nm in ((0, fk("mk")), (1, fk("mv"))):
                        trn(p[0:64, q * 256 + hf * 128:q * 256 + hf * 128 + 128],
                            F[nm][:, c0 + q * 64:c0 + q * 64 + 64], 128, [nm], [k])
                cp(GVb[0:64, :], p[0:64, 0:512], [k], [kn("GV")])
                yield
                if main:
                    p = pm(); k = ps_key(p)
                    for q in (0, 1):
                        cs = slice(c0 + q * 64, c0 + q * 64 + 64)
                        mm(p[0:64, q * 64:(q + 1) * 64], T5b[:, cs], T7b[:, cs], True, True, [fk("T0"), fk("T1")], [k])
                    gm = CT["G_p" if kind == "p" else "G_s"]
                    tt(GAv[0:64], b3(p[0:64, 0:128], 2, 64), bc_mid(gm[0:64, :], 2), ALU.mult, [k, "cst"], [kn("GA")])
                    pO = pm(pin=True); kO = ps_key(pO)
                yield
                if kind == "p":
                    cp(Ss[:, 0, :], Sgl[:, g, :], ["Sgl"], [(kn("Ss"), 0)])
                    cp(Ssb[:, 0, :], Sgl[:, g, :], ["Sgl"], [(kn("Ssb"), 0)])
                    for q in (0, 1):
                        cs = slice(c0 + q * 64, c0 + q * 64 + 64)
                        if main:
                            oo = pO[0:64, q * 256:(q + 1) * 256]
                            mm(oo, GAv[0:64, q, :], GVv[0:64, q, :], True, False, [kn("GA"), kn("GV")], [kO])
                            mm(oo, T7b[:, cs], Ssb[:, q, :], False, True, [fk("T1"), (kn("Ssb"), q)], [kO])
                        p = pm(); k = ps_key(p)
                        mm(p[:, 0:256], GKv[0:64, q, :], GVv[0:64, q, :], True, True, [kn("GK"), kn("GV")], [k])
                        wc = F["T3"][:, c0 + q * 64 + 63:c0 + q * 64 + 64]
                        stt(Ssb[:, q + 1, :], Ss[:, q, :], wc, p[:, 0:256], ALU.mult, ALU.add,
                            [(kn("Ss"), q), fk("T3"), k], [(kn("Ssb"), q + 1)])
                        stt(Ss[:, q + 1, :], Ss[:, q, :], wc, p[:, 0:256], ALU.mult, ALU.add,
                            [(kn("Ss"), q), fk("T3"), k], [(kn("Ss"), q + 1)])
                    cp(Sgl[:, g, :], Ss[:, 2, :], [(kn("Ss"), 2)], ["Sgl"])
                else:
                    S0v, Kxv, Qxv = b3(GS0, 8, 256), b3(GKx.bitcast(BF16), 8, 128), b3(GQx.bitcast(BF16), 8, 64)
                    S0bv = b3(GS0b.bitcast(BF16), 8, 256)
                    for q in (0, 1):
                        cs = slice(c0 + q * 64, c0 + q * 64 + 64)
                        n0 = (sub * 2 + q) * 8
                        ld(S0v, glS0[g, :, n0:n0 + 8, :], ["GS0"])
                        cp(S0bv, S0v, ["GS0"], ["GS0b"])
                        tt(Qxv, bc_mid(F["T7"][:, cs], 8), b3(CT["gsegF"], 8, 64), ALU.mult, [fk("T7"), "cst"], ["GQx"])
                        tt(Kxv[0:64], bc_mid(GKv[0:64, q, :], 8), bc_last(CT["segT"][0:64, :], 128), ALU.mult,
                           [kn("GK"), "cst"], ["GKx"])
                        oo = pO[0:64, q * 256:(q + 1) * 256]
                        mm(oo, GAv[0:64, q, :], GVv[0:64, q, :], True, False, [kn("GA"), kn("GV")], [kO])
                        for n in range(8):
                            mm(oo, Qxv[:, n, :], S0bv[:, n, :], False, n == 7, ["GQx", "GS0b"], [kO])
                        wseg = b3(F["T3"][:, cs], 8, 8)[:, :, 7]
                        pss = [pm(pin=True) for _ in range(4)]
                        for n in range(8):
                            po = pss[n // 2]
                            mm(po[:, (n % 2) * 256:(n % 2 + 1) * 256], Kxv[0:64, n, :], GVv[0:64, q, :], True, True,
                               ["GKx", kn("GV")], [ps_key(po)])
                        for j in range(4):
                            sv = S0v[:, 2 * j:2 * j + 2, :]
                            tt(sv, sv, bc_last(wseg[:, 2 * j:2 * j + 2], 256), ALU.mult, ["GS0", fk("T3"), kO], ["GS0"])
                            tt(sv, sv, b3(pss[j], 2, 256), ALU.add, ["GS0", ps_key(pss[j])], ["GS0"])
                        for po in pss:
                            unpin(po)
                        store(o_glas[g, :, n0:n0 + 8, :], S0v, ["GS0"])
                yield
                if main:
                    stat = GST
                    act(GON[0:64, :], pO[0:64, 0:512], AF.Square, [kO], [kn("GON")])
                    T.add("vector", lambda e: e.tensor_reduce(out=stat[0:64, 0:2], in_=ONv[0:64], axis=AX.X, op=ALU.add),
                          [kn("GON")], [kn("stat")])
                    act(stat[0:64, 2:4], stat[0:64, 0:2], AF.Ln, [kn("stat"), "eps"], [kn("stat")], bias=epsh[0:64], scale=1.0 / 256)
                    act(stat[0:64, 4:6], stat[0:64, 2:4], AF.Exp, [kn("stat")], [kn("stat")], scale=-0.5)
                    tt(ONv[0:64], b3(pO[0:64, 0:512], 2, 256), bc_last(stat[0:64, 4:6], 256), ALU.mult, [kO, kn("stat"), kn("GON")], [kn("GON")])
                    p = pm(); k = ps_key(p)
                    for q in (0, 1):
                        for hf in (0, 1):
                            trn(p[:, hf * 128 + q * 64:hf * 128 + q * 64 + 64], ONv[0:64, q, hf * 128:(hf + 1) * 128], 64,
                                [kn("GON")], [k])
                    for hf, nm in ((0, fk("T8")), (1, fk("T9"))):
                        stt(oT[:, 8 + 2 * g + hf, tc0 + c0:tc0 + c0 + 128], p[:, hf * 128:(hf + 1) * 128],
                            glvt[:, 4 + hf:5 + hf], F[nm][:, c0:c0 + 128], ALU.mult, ALU.mult,
                            [k, "glv", nm], [("oT", 8 + 2 * g + hf)])
                    unpin(pO)
                yield
            gens = [gl_sub(sub, ("", "_B")[sub]) for sub in range(N // 128)]
            while gens:
                for gsub in list(gens):
                    try:
                        next(gsub)
                    except StopIteration:
                        gens.remove(gsub)
                yield
            gl_lock["busy"] = False
            if main and grp[3]:
                store(o_glap[g, :, :], Sgl[:, g, :], ["Sgl"])

        def zero_blk():
            for nm in ("KR", "Bf", "Cf", "BBf", "KKf", "Vf", "H0f", "KR_B", "Bf_B", "Cf_B", "BBf_B", "KKf_B", "Vf_B", "YN", "YN_B"):
                T.add("vector", lambda e, m=MT[nm]: e.memset(m, 0.0), (), [nm])

        for mode in ("P", "M"):
            main = mode == "M"
            norm_tokens(xM if main else xP, NTM if main else PRE, 0, hT, "hT", base=top[0] - 6656)
            T.barrier()
            zero_blk()
            if main:
                ldc(l2b, lora2[:, :, :], ["l2b"])
            lora_stage(mode)
            seq = [(hp, gi, grp) for hp in range(8) for gi, grp in enumerate(groups(mode))]
            wts, active, nxt = {}, [], 0

            def start(j, main=main, seq=seq, wts=wts):
                hp, gi, grp = seq[j]
                if gi == 0:
                    wk, wv = wload_rw(w_rw[8 + hp]), wload_rw(w_rw[16 + hp])
                    if main:
                        wr = wload_rw(w_rw[hp])
                    else:
                        wr = None
                        wr0 = wload_rw(w_rw[hp])
                        ps = proj_ps()
                        proj(wr0, PRE - 64, 64, ps)
                        cp(carry[:, hp:hp + 1], ps[:, 63:64], [ps_key(ps)], ["carry"])
                    wts[hp] = (wk, wv, wr)
                return rw_group(hp, grp, main, *wts[hp], j % 2, ("rw", mode, j))

            while nxt < len(seq) or active:
                while len(active) < 2 and nxt < len(seq):
                    if seq[nxt][1] == 0 and any(t not in proj_done for _, t in active):
                        break

                    active.append((start(nxt), ("rw", mode, nxt)))
                    nxt += 1
                for gg in list(active):
                    try:
                        next(gg[0])
                    except StopIteration:
                        active.remove(gg)
            T.barrier()
            gla_xgate(mode)
            gseq = [(g, gi, grp) for g in range(4) for gi, grp in enumerate(groups(mode))]
            gw, gact, gnx = {}, [], 0

            def gstart(j, main=main, gseq=gseq, gw=gw):
                g, gi, grp = gseq[j]
                if gi == 0:
                    W = {}
                    srcs = [("k", 4 + g), ("v0", 8 + 2 * g), ("v1", 9 + 2 * g)]
                    if main:
                        srcs += [("q", g), ("go0", 17 + 2 * g), ("go1", 18 + 2 * g)]
                    for slot, (nm, ci) in enumerate(srcs):
                        ldc(wst[slot], w_gl[ci], [("w", slot)])
                        W[nm] = slot
                    gw[g] = W
                return gl_group(g, grp, main, gw[g], j % 2, ("gl", mode, j))

            while gnx < len(gseq) or gact:
                while len(gact) < 2 and gnx < len(gseq):
                    if gseq[gnx][1] == 0 and any(t not in proj_done for _, t in gact):
                        break
                    gact.append((gstart(gnx), ("gl", mode, gnx)))
                    gnx += 1
                for gg in list(gact):
                    try:
                        next(gg[0])
                    except StopIteration:
                        gact.remove(gg)
            T.barrier()
        store(o_shift[:, :, :], shout, ["shout"])

        X1 = V(ov0, 18432).rearrange("p (k t) -> p k t", k=KB)
        rtmp = V(ov0 + 18432 + 2560, 512)
        ttiles = [(0, 384), (384, 384), (768, 384)]
        for fc in range(KB):
            wi = wload(w_o[fc])
            ld(X1[:, fc, :], xM[:, fc, :], [("X1", fc)])
            for t0, tw in ttiles:
                ps = proj_ps(); k = ps_key(ps)
                for kk in range(KB):
                    mm(ps[:, 0:tw], wst[wi][:, kk, :], oT[:, kk, t0:t0 + tw], kk == 0, kk == KB - 1,
                       [("w", wi), ("oT", kk)], [k])
                tt(X1[:, fc, t0:t0 + tw], X1[:, fc, t0:t0 + tw], ps[:, 0:tw], ALU.add, [("X1", fc), k], [("X1", fc)])
        x1k = [("X1", fc) for fc in range(KB)]
        norm_tokens((X1, x1k), NTM, 1, hT, "hT", base=ov0 + 18432)
        uT = oT
        for g4 in range(4):
            for fc in range(KB):
                wi = wload(w_u[g4 * KB + fc])
                for t0, tw in ttiles:
                    ps = proj_ps(); k = ps_key(ps)
                    for kk in range(KB):
                        mm(ps[:, 0:tw], wst[wi][:, kk, :], hT[:, kk, t0:t0 + tw], kk == 0, kk == KB - 1,
                           [("w", wi), "hT"], [k])
                    act(rtmp[:, 0:tw], ps[:, 0:tw], AF.Relu, [k], ["rtmp"])
                    tt(uT[:, fc, t0:t0 + tw], rtmp[:, 0:tw], rtmp[:, 0:tw], ALU.mult, ["rtmp"], [("oT", fc)])
            for fc in range(KB):
                wi = wload(w_d[g4, fc])
                for t0, tw in ttiles:
                    ps = proj_ps(); k = ps_key(ps)
                    for kk in range(KB):
                        mm(ps[:, 0:tw], wst[wi][:, kk, :], uT[:, kk, t0:t0 + tw], kk == 0, kk == KB - 1,
                           [("w", wi), ("oT", kk)], [k])
                    tt(X1[:, fc, t0:t0 + tw], X1[:, fc, t0:t0 + tw], ps[:, 0:tw], ALU.add, [("X1", fc), k], [("X1", fc)])
        fb = ov0 + 18432
        lnF, rsF = V(fb, NTM), V(fb + NTM, NTM)
        for k in range(KB):
            act(hT[:, k, :], X1[:, k, :], AF.Square, [("X1", k)], ["hT"])
        for ti, (t0, tw) in enumerate(ttiles):
            pb_ = PS[2 + ti]
            for k in range(KB):
                mm(pb_[:, 0:tw], onesb, hT[:, k, t0:t0 + tw], k == 0, k == KB - 1, ["onesb", "hT"], [ps_key(pb_)])
            act(lnF[:, t0:t0 + tw], pb_[:, 0:tw], AF.Ln, [ps_key(pb_), "eps"], ["lnF"], bias=epsn, scale=1.0 / D)
            act(rsF[:, t0:t0 + tw], lnF[:, t0:t0 + tw], AF.Exp, ["lnF"], ["rsF"], scale=-0.5)
        for k in range(KB):
            stt(X1[:, k, :], X1[:, k, :], gv[:, 2, k:k + 1], rsF, ALU.mult, ALU.mult, [("X1", k), "gv", "rsF"], [("X1", k)])
            store(yT[:, k, :], X1[:, k, :], [("X1", k)])
        T.emit(out_ids)
    return nc


def _fm(a2d):
    t = a2d.shape[0]
    return np.ascontiguousarray(a2d.T.reshape(KB, 128, t).transpose(1, 0, 2))


def _wt(w, nch):
    return np.ascontiguousarray(w.reshape(KB, 128, nch, 128).transpose(2, 1, 0, 3))


def _pad_cols(a, segs, axis=-1):
    out = []
    for s0, n in segs:
        piece = np.take(a, np.arange(s0, s0 + n), axis=axis)
        if n < 128:
            padw = [(0, 0)] * a.ndim
            padw[axis] = (0, 128 - n)
            piece = np.pad(piece, padw)
        out.append(piece)
    return np.concatenate(out, axis=axis)


RW_SEGS = [(i * 128, 128) for i in range(24)] + [(3072, 64), (3136, 64), (3200, 128), (3328, 32)]
RW_UNPAD = np.concatenate([np.arange(c * 128, c * 128 + n) for c, (s0, n) in enumerate(RW_SEGS)])
GL0 = RW_PROJ
GL_SEGS = [(GL0 + i * 128, 128) for i in range(16)] + [(GL0 + 2048, 16)] + [(GL0 + 2064 + i * 128, 128) for i in range(8)]


def kernel(x_prompt, x_sample, state_rwkv_shift, state_rwkv_wkv, state_gla, norm1_g, w_in, rw_mu,
           rw_w0, rw_w2, rw_a0, rw_a2, rw_g2, rw_k_k, rw_k_a, rw_r_k, rw_ln_w, rw_ln_b, gla_gw2,
           gla_gb, gla_norm_w, w_out, norm2_g, w_up, w_down, norm_f_g):
    f32 = np.float32
    A = lambda z: np.asarray(z, f32)
    x_prompt, x_sample = A(x_prompt), A(x_sample)
    w_in0 = A(w_in)[0]
    w_rw = _wt(_pad_cols(w_in0, RW_SEGS), NRW)
    w_gl = _wt(_pad_cols(w_in0, GL_SEGS), NGL)
    mu = np.ascontiguousarray(_pad_cols(A(rw_mu)[0], RW_SEGS).reshape(NRW, 128).T)
    vecT = lambda v: np.ascontiguousarray(A(v).reshape(-1, 128).T)
    gvec = np.ascontiguousarray(np.stack([vecT(A(norm1_g)[0]), vecT(A(norm2_g)[0]), vecT(A(norm_f_g))], 1))
    rwv = np.ascontiguousarray(np.stack([vecT(A(v)[0]) for v in (rw_w0, rw_a0, rw_k_k, rw_k_a, rw_r_k, rw_ln_w, rw_ln_b)], 1))
    lora2 = np.zeros((128, 4, 1024), f32)
    lora2[0:64, 0], lora2[0:64, 1] = A(rw_w2)[0], A(rw_a2)[0]
    lora2[0:128, 2], lora2[0:32, 3] = A(rw_g2)[0][0:128], A(rw_g2)[0][128:160]
    glv = np.ascontiguousarray(np.concatenate([vecT(A(gla_gb)[0]), vecT(A(gla_norm_w)[0])], 1))
    gw2 = np.zeros((128, 512), f32)
    gw2[0:16] = A(gla_gw2)[0]
    w_o = _wt(A(w_out)[0], 16)
    w_u = _wt(A(w_up)[0], 64)
    w_d = np.ascontiguousarray(A(w_down)[0].reshape(4, KB, 128, 16, 128).transpose(0, 3, 2, 1, 4))
    sh = A(state_rwkv_shift)[0]
    wkv = A(state_rwkv_wkv)[0]
    gls = A(state_gla)[0]
    shared = {"cst": CST, "gvec": gvec, "w_rw": w_rw, "mu_rw": mu, "rwv": rwv, "lora2": lora2, "w_gl": w_gl,
              "glv": glv, "gw2": gw2, "w_o": w_o, "w_u": w_u, "w_d": w_d}
    in_maps = []
    for c in range(N_CORES):
        b, par = divmod(c, 2)
        own = x_prompt[b, par * HALF:(par + 1) * HALF]
        pre = x_prompt[b, 0:PRE] if par == 1 else np.zeros((PRE, D), f32)
        smp = x_sample[NS * c:NS * (c + 1)].reshape(SMP, D)
        ns = slice(NS * c, NS * (c + 1))
        shT = np.ascontiguousarray(_pad_cols(sh[ns], RW_SEGS).reshape(NS, NRW, 128).transpose(2, 1, 0))
        h0 = np.ascontiguousarray(wkv[ns].transpose(1, 3, 0, 2).reshape(8, 128, NS, 64))
        s0 = np.ascontiguousarray(gls[ns].transpose(1, 2, 0, 3))
        m = dict(shared)
        m.update({"xP": _fm(pre), "xM": _fm(np.concatenate([own, smp], 0)), "shiftT": shT, "rwH0": h0, "glS0": s0})
        in_maps.append(m)
    nc = build_nc()
    res = run_bass_kernel_spmd(nc, in_maps, core_ids=list(range(N_CORES)))

    B = x_prompt.shape[0]
    y_p = np.zeros((B, SEQ, D), f32)
    y_s = np.zeros((DEC_B, DEC_T, D), f32)
    sh_p = np.zeros((1, B, RW_PROJ), f32)
    sh_s = np.zeros((1, DEC_B, RW_PROJ), f32)
    wkv_p = np.zeros((1, B, RW_H, RW_HD, RW_HD), f32)
    wkv_s = np.zeros((1, DEC_B, RW_H, RW_HD, RW_HD), f32)
    gla_p = np.zeros((1, B, 4, 128, 256), f32)
    gla_s = np.zeros((1, DEC_B, 4, 128, 256), f32)
    for c in range(N_CORES):
        b, par = divmod(c, 2)
        r = res.results[c]
        ns = slice(NS * c, NS * (c + 1))
        y = r["yT"].transpose(2, 1, 0).reshape(NTM, D)
        y_p[b, par * HALF:(par + 1) * HALF] = y[:HALF]
        y_s[ns] = y[HALF:].reshape(NS, DEC_T, D)
        rows = r["o_shift"].transpose(2, 1, 0).reshape(1 + NS, NRW * 128)[:, RW_UNPAD]
        sh_s[0, ns] = rows[1:]
        wkv_s[0, ns] = r["o_wkvs"].reshape(8, 2, 64, NS, 64).transpose(3, 0, 1, 4, 2).reshape(NS, 16, 64, 64)
        gla_s[0, ns] = r["o_glas"].transpose(2, 0, 1, 3)
        if par == 1:
            sh_p[0, b] = rows[0]
            wkv_p[0, b] = r["o_wkvp"].reshape(16, 64, 64).transpose(0, 2, 1)
            gla_p[0, b] = r["o_glap"]
    return (y_p, y_s, sh_p, wkv_p, gla_p, sh_s, wkv_s, gla_s)
```
